# Optimizing a Trainium2 kernel written in Bass

```python
import jax, jax.numpy as jnp
from jax import lax
import numpy as np

D_MODEL = 1024
BATCH = 2
SEQ = 16384
DEPTH = 2

N_MIXERS = 2
N_GLA = (DEPTH + 1) // N_MIXERS
N_RWKV = DEPTH // N_MIXERS

GLA_HEADS = 4
GLA_KEY_WIDTH = D_MODEL // 2
GLA_VAL_WIDTH = D_MODEL
GLA_DK = GLA_KEY_WIDTH // GLA_HEADS
GLA_DV = GLA_VAL_WIDTH // GLA_HEADS
GLA_GATE_RANK = 16
GLA_GATE_NORMALIZER = 16.0
GLA_CHUNK = 64
GLA_IN = 2 * GLA_KEY_WIDTH + 2 * GLA_VAL_WIDTH + GLA_GATE_RANK

RWKV_HEAD = 64
RWKV_WIDTH = D_MODEL
RWKV_HEADS = RWKV_WIDTH // RWKV_HEAD
RWKV_DECAY_RANK = 64
RWKV_ICLR_RANK = 64
RWKV_IN = 4 * RWKV_WIDTH + RWKV_DECAY_RANK + RWKV_ICLR_RANK

RMS_EPS = 1e-6
GN_EPS = 64e-5

kernel_name = "hybrid_gla_rwkv7_interleaved"


def rmsnorm(x, g):
    xf = x.astype(jnp.float32)
    y = xf * lax.rsqrt(jnp.mean(xf * xf, axis=-1, keepdims=True) + RMS_EPS)
    return (y * g.astype(jnp.float32)).astype(x.dtype)


def gla_mixer(h, w_in, w_alpha_up, b_alpha, g_head, w_out):
    B, T, _ = h.shape
    nc = T // GLA_CHUNK
    p = h @ w_in
    q, k, v, gate, ad = jnp.split(p, [GLA_KEY_WIDTH, 2 * GLA_KEY_WIDTH,
                                       2 * GLA_KEY_WIDTH + GLA_VAL_WIDTH,
                                       2 * GLA_KEY_WIDTH + 2 * GLA_VAL_WIDTH], axis=-1)
    f32 = jnp.float32
    log_alpha = jax.nn.log_sigmoid((ad.astype(f32) @ w_alpha_up.astype(f32)
                                    + b_alpha.astype(f32))) / GLA_GATE_NORMALIZER

    def heads(z, d):
        return z.astype(f32).reshape(B, nc, GLA_CHUNK, GLA_HEADS, d).transpose(0, 3, 1, 2, 4)

    q = heads(q, GLA_DK) * (GLA_DK ** -0.5)
    k = heads(k, GLA_DK)
    v = heads(v, GLA_DV)
    bcum = jnp.cumsum(heads(log_alpha, GLA_DK), axis=3)
    b_last = bcum[:, :, :, -1, :]
    q_dec = q * jnp.exp(bcum)
    k_dec = k * jnp.exp(-bcum)
    k_to_end = k * jnp.exp(b_last[:, :, :, None, :] - bcum)

    mask = jnp.tril(jnp.ones((GLA_CHUNK, GLA_CHUNK), dtype=bool))
    att = jnp.where(mask, jnp.einsum('bhncd,bhnsd->bhncs', q_dec, k_dec), 0.0)
    o_intra = jnp.einsum('bhncs,bhnsv->bhncv', att, v)

    def step(S, inp):
        qd, kt, vv, dl = inp
        o = jnp.einsum('bhcd,bhdv->bhcv', qd, S)
        S = S * jnp.exp(dl)[..., None] + jnp.einsum('bhcd,bhcv->bhdv', kt, vv)
        return S, o

    S0 = jnp.zeros((B, GLA_HEADS, GLA_DK, GLA_DV), f32)
    xs = (jnp.moveaxis(q_dec, 2, 0), jnp.moveaxis(k_to_end, 2, 0),
          jnp.moveaxis(v, 2, 0), jnp.moveaxis(b_last, 2, 0))
    _, o_inter = lax.scan(step, S0, xs)
    o = o_intra + jnp.moveaxis(o_inter, 0, 2)

    o = o * lax.rsqrt(jnp.mean(o * o, axis=-1, keepdims=True) + RMS_EPS) * g_head.astype(f32)
    o = o.transpose(0, 2, 3, 1, 4).reshape(B, T, GLA_VAL_WIDTH)
    o = o * jax.nn.silu(gate.astype(f32))
    return o.astype(h.dtype) @ w_out


def token_shift(p):
    return jnp.pad(p, ((0, 0), (1, 0), (0, 0)))[:, :-1]


def rwkv7_mixer(h, w_in, mu, w0, w_decay_up, a0, w_iclr_up, k_k, k_a, r_k, ln_w, ln_b, w_out):
    B, T, _ = h.shape
    f32 = jnp.float32
    p = h @ w_in
    p = p + mu * (token_shift(p) - p)
    r, k, v, gate, wd, ad = jnp.split(p, [RWKV_WIDTH, 2 * RWKV_WIDTH, 3 * RWKV_WIDTH,
                                           4 * RWKV_WIDTH, 4 * RWKV_WIDTH + RWKV_DECAY_RANK], axis=-1)
    r, k, v, gate, wd, ad = (z.astype(f32) for z in (r, k, v, gate, wd, ad))

    w_log = -jax.nn.softplus(-(w0.astype(f32) + jnp.tanh(wd) @ w_decay_up.astype(f32))) - 0.5
    decay = jnp.exp(-jnp.exp(w_log))
    a = jax.nn.sigmoid(a0.astype(f32) + ad @ w_iclr_up.astype(f32))

    def heads(z):
        return z.reshape(B, T, RWKV_HEADS, RWKV_HEAD)

    kk = heads(k * k_k.astype(f32))
    kk = kk / jnp.maximum(jnp.sqrt(jnp.sum(kk * kk, axis=-1, keepdims=True)), 1e-12)
    k = k * (1.0 + (a - 1.0) * k_a.astype(f32))
    r, decay, k, v, a = heads(r), heads(decay), heads(k), heads(v), heads(a)

    def step(S, inp):
        r_t, w_t, k_t, v_t, kk_t, a_t = inp
        sa = jnp.einsum('bhvk,bhk->bhv', S, -kk_t)
        S = (S * w_t[:, :, None, :] + sa[..., None] * (kk_t * a_t)[:, :, None, :]
             + v_t[..., None] * k_t[:, :, None, :])
        y = jnp.einsum('bhvk,bhk->bhv', S, r_t)
        return S, y

    S0 = jnp.zeros((B, RWKV_HEADS, RWKV_HEAD, RWKV_HEAD), f32)
    xs = tuple(jnp.moveaxis(z, 1, 0) for z in (r, decay, k, v, kk, a))
    _, y = lax.scan(step, S0, xs)
    y = jnp.moveaxis(y, 0, 1)

    mean = jnp.mean(y, axis=-1, keepdims=True)
    var = jnp.mean(jnp.square(y - mean), axis=-1, keepdims=True)
    y = ((y - mean) * lax.rsqrt(var + GN_EPS)).reshape(B, T, RWKV_WIDTH)
    y = y * ln_w.astype(f32) + ln_b.astype(f32)
    bonus = jnp.sum(r * k * r_k.astype(f32), axis=-1, keepdims=True) * v
    y = y + bonus.reshape(B, T, RWKV_WIDTH)
    y = y * jax.nn.silu(gate)
    return y.astype(h.dtype) @ w_out


def setup_inputs(seed: int = 0) -> dict:
    key = jax.random.key(seed)
    ks = jax.random.split(key, 24)
    nrm = jax.random.normal
    f32 = jnp.float32
    D = D_MODEL
    x = nrm(ks[0], (BATCH, SEQ, D), f32)
    gla_pre_norm = 1.0 + 0.02 * nrm(ks[1], (N_GLA, D), f32)
    gla_w_in = nrm(ks[2], (N_GLA, D, GLA_IN), f32) * D ** -0.5
    gla_w_alpha_up = nrm(ks[3], (N_GLA, GLA_GATE_RANK, GLA_KEY_WIDTH), f32) * GLA_GATE_RANK ** -0.5
    gla_b_alpha = 0.1 * nrm(ks[4], (N_GLA, GLA_KEY_WIDTH), f32)
    gla_head_norm = 1.0 + 0.02 * nrm(ks[5], (N_GLA, GLA_DV), f32)
    gla_w_out = nrm(ks[6], (N_GLA, GLA_VAL_WIDTH, D), f32) * GLA_VAL_WIDTH ** -0.5
    gla_post_norm = 1.0 + 0.02 * nrm(ks[7], (N_GLA, D), f32)
    rwkv_pre_norm = 1.0 + 0.02 * nrm(ks[8], (N_RWKV, D), f32)
    rwkv_w_in = nrm(ks[9], (N_RWKV, D, RWKV_IN), f32) * D ** -0.5
    rwkv_mu = jax.random.uniform(ks[10], (N_RWKV, RWKV_IN), f32)
    rwkv_w0 = -1.0 + 0.5 * nrm(ks[11], (N_RWKV, RWKV_WIDTH), f32)
    rwkv_w_decay_up = 0.1 * nrm(ks[12], (N_RWKV, RWKV_DECAY_RANK, RWKV_WIDTH), f32)
    rwkv_a0 = 0.1 * nrm(ks[13], (N_RWKV, RWKV_WIDTH), f32)
    rwkv_w_iclr_up = 0.1 * nrm(ks[14], (N_RWKV, RWKV_ICLR_RANK, RWKV_WIDTH), f32)
    rwkv_k_k = 0.85 + 0.05 * nrm(ks[15], (N_RWKV, RWKV_WIDTH), f32)
    rwkv_k_a = 1.0 + 0.05 * nrm(ks[16], (N_RWKV, RWKV_WIDTH), f32)
    rwkv_r_k = 0.1 * nrm(ks[17], (N_RWKV, RWKV_HEADS, RWKV_HEAD), f32)
    rwkv_ln_w = 1.0 + 0.02 * nrm(ks[18], (N_RWKV, RWKV_WIDTH), f32)
    rwkv_ln_b = 0.02 * nrm(ks[19], (N_RWKV, RWKV_WIDTH), f32)
    rwkv_w_out = nrm(ks[20], (N_RWKV, RWKV_WIDTH, D), f32) * RWKV_WIDTH ** -0.5
    rwkv_post_norm = 1.0 + 0.02 * nrm(ks[21], (N_RWKV, D), f32)
    return {"x": x,
            "gla_pre_norm": gla_pre_norm, "gla_w_in": gla_w_in, "gla_w_alpha_up": gla_w_alpha_up,
            "gla_b_alpha": gla_b_alpha, "gla_head_norm": gla_head_norm, "gla_w_out": gla_w_out,
            "gla_post_norm": gla_post_norm,
            "rwkv_pre_norm": rwkv_pre_norm, "rwkv_w_in": rwkv_w_in, "rwkv_mu": rwkv_mu,
            "rwkv_w0": rwkv_w0, "rwkv_w_decay_up": rwkv_w_decay_up, "rwkv_a0": rwkv_a0,
            "rwkv_w_iclr_up": rwkv_w_iclr_up, "rwkv_k_k": rwkv_k_k, "rwkv_k_a": rwkv_k_a,
            "rwkv_r_k": rwkv_r_k, "rwkv_ln_w": rwkv_ln_w, "rwkv_ln_b": rwkv_ln_b,
            "rwkv_w_out": rwkv_w_out, "rwkv_post_norm": rwkv_post_norm}


def reference(x, gla_pre_norm, gla_w_in, gla_w_alpha_up, gla_b_alpha, gla_head_norm, gla_w_out,
              gla_post_norm, rwkv_pre_norm, rwkv_w_in, rwkv_mu, rwkv_w0, rwkv_w_decay_up, rwkv_a0,
              rwkv_w_iclr_up, rwkv_k_k, rwkv_k_a, rwkv_r_k, rwkv_ln_w, rwkv_ln_b, rwkv_w_out,
              rwkv_post_norm):
    for i in range(DEPTH):
        j = i // N_MIXERS
        if i % N_MIXERS == 0:
            h = rmsnorm(x, gla_pre_norm[j])
            y = gla_mixer(h, gla_w_in[j], gla_w_alpha_up[j], gla_b_alpha[j], gla_head_norm[j],
                          gla_w_out[j])
            x = x + rmsnorm(y, gla_post_norm[j])
        else:
            h = rmsnorm(x, rwkv_pre_norm[j])
            y = rwkv7_mixer(h, rwkv_w_in[j], rwkv_mu[j], rwkv_w0[j], rwkv_w_decay_up[j], rwkv_a0[j],
                            rwkv_w_iclr_up[j], rwkv_k_k[j], rwkv_k_a[j], rwkv_r_k[j], rwkv_ln_w[j],
                            rwkv_ln_b[j], rwkv_w_out[j])
            x = x + rmsnorm(y, rwkv_post_norm[j])
    return x
```

```python
import math
import numpy as np
import concourse.bass as bass
import concourse.mybir as mybir
from concourse.bass_utils import run_bass_kernel_spmd

F32 = mybir.dt.float32
AF = mybir.ActivationFunctionType
ALU = mybir.AluOpType

D = 1024
C = 128
ST = 256
RMS_EPS = 1e-6
GN_EPS = 64e-5
S_DEC = -math.exp(-0.5)


class Buf:
    def __init__(self, name):
        self.name = name
        self.w = None
        self.r = []
        self.dsem = None
        self.dcnt = 0


class Eng:
    def __init__(self, k, e, name):
        self.k, self.e, self.name = k, e, name
        self.sem = k.new_sem(name)
        self.cnt = 0
        self.seen = {}

    def wait(self, ev):
        if ev is None:
            return
        sem, val = ev
        if self.seen.get(id(sem), 0) >= val:
            return
        if sem is self.sem and self.name == "pe":
            return
        self.e.wait_ge(sem, val)
        self.seen[id(sem)] = val


class KB:
    def __init__(self):
        self.nc = bass.Bass("TRN2", target_bir_lowering=False)
        self.stack = []
        nc = self.nc
        self.pe = Eng(self, nc.tensor, "pe")
        self.act = Eng(self, nc.scalar, "act")
        self.dve = Eng(self, nc.vector, "dve")
        self.pool = Eng(self, nc.gpsimd, "pool")
        self.sp = Eng(self, nc.sync, "sp")
        self.nbuf = 0
        self.ninst = 0
        self.guards = []
        self.marks = []
        self.dma_bufs = []
        self.psb = None
        self.psi = 0

    def new_sem(self, name):
        cm = self.nc.semaphore(name)
        s = cm.__enter__()
        self.stack.append(cm)
        return s

    def close(self):
        for cm in reversed(self.stack):
            cm.__exit__(None, None, None)
        self.stack = []

    def sb(self, name, shape, dtype=F32):
        self.nbuf += 1
        g = self.nc.sbuf_tensor("%s_%d" % (name, self.nbuf), list(shape), dtype)
        t = g.__enter__()
        self.guards.append(g)
        return t, Buf(name)

    def begin_stage(self):
        self.marks.append(len(self.guards))

    def end_stage(self):
        self.barrier()
        m = self.marks.pop()
        while len(self.guards) > m:
            self.guards.pop().__exit__(None, None, None)

    def barrier(self):
        engs = [self.pe, self.act, self.dve, self.sp]
        for e in engs:
            for f in engs:
                if f is not e and f.cnt > 0:
                    e.wait((f.sem, f.cnt))
            for b in self.dma_bufs:
                e.wait((b.dsem, b.dcnt))

    def psum_banks(self):
        if self.psb is None:
            self.psb = [self.ps("ps%d" % i, [128, 512]) for i in range(8)]
        return self.psb

    def newps(self):
        r = self.psum_banks()[self.psi % 8]
        self.psi += 1
        return r

    def ps(self, name, shape):
        t = self.nc.alloc_psum_tensor(name, list(shape), F32)
        return t, Buf(name)

    def _deps(self, eng, R, W):
        for b in R:
            eng.wait(b.w)
        for b in W:
            eng.wait(b.w)
            for ev in b.r:
                eng.wait(ev)

    def op(self, eng, fn, R=(), W=()):
        self._deps(eng, R, W)
        ins = fn()
        eng.cnt += 1
        ins.then_inc(eng.sem, 1)
        ev = (eng.sem, eng.cnt)
        eng.seen[id(eng.sem)] = max(eng.seen.get(id(eng.sem), 0), 0)
        for b in R:
            b.r.append(ev)
        for b in W:
            b.w = ev
            b.r = []
        self.ninst += 1
        return ins

    def dma(self, eng, out, in_, R=(), W=(), key=None):
        self._deps(eng, R, W)
        if key.dsem is None:
            self.nbuf += 1
            key.dsem = self.new_sem("d_%s_%d" % (key.name, self.nbuf))
            self.dma_bufs.append(key)
        ins = eng.e.dma_start(out=out, in_=in_)
        key.dcnt += 16
        ins.then_inc(key.dsem, 16)
        ev = (key.dsem, key.dcnt)
        for b in R:
            b.r.append(ev)
        for b in W:
            b.w = ev
            b.r = []
        self.ninst += 1
        return ev

    def mm(self, out, lhsT, rhs, start, stop, R, W):
        return self.op(self.pe, lambda: self.nc.tensor.matmul(out, lhsT=lhsT, rhs=rhs, start=start, stop=stop), R, W)

    def tr(self, out, in_, ident, R, W):
        return self.op(self.pe, lambda: self.nc.tensor.transpose(out, in_, ident), R, W)

    def actf(self, out, in_, func, R, W, bias=0.0, scale=1.0):
        return self.op(self.act, lambda: self.nc.scalar.activation(out=out, in_=in_, func=func, bias=bias, scale=scale), R, W)

    def tt(self, out, a, b, op, R, W, eng=None):
        eng = eng or self.dve
        return self.op(eng, lambda: eng.e.tensor_tensor(out, a, b, op), R, W)

    def ts(self, out, a, s1, s2, op0, op1, R, W, eng=None):
        eng = eng or self.dve
        if s2 is None:
            return self.op(eng, lambda: eng.e.tensor_single_scalar(out, a, s1, op0), R, W)
        return self.op(eng, lambda: eng.e.tensor_scalar(out, a, s1, s2, op0, op1), R, W)

    def stt(self, out, a, s, b, op0, op1, R, W, eng=None):
        eng = eng or self.dve
        return self.op(eng, lambda: eng.e.scalar_tensor_tensor(out, a, s, b, op0, op1), R, W)

    def cp(self, out, in_, R, W, eng=None):
        eng = eng or self.dve
        if eng is self.act:
            return self.op(eng, lambda: self.nc.scalar.copy(out, in_), R, W)
        return self.op(eng, lambda: eng.e.tensor_copy(out, in_), R, W)

    def recip(self, out, in_, R, W):
        return self.op(self.dve, lambda: self.nc.vector.reciprocal(out, in_), R, W)

    def finish(self, out_bufs=()):
        self.barrier()
        for b in self.dma_bufs:
            self.sp.e.wait_ge(b.dsem, b.dcnt)
        while self.guards:
            self.guards.pop().__exit__(None, None, None)
        self.close()


NH = 4
HS = 64
NB = 18
V_MU, V_W0, V_A0, V_KK, V_KA, V_RK, V_LW, V_LB = 0, 18, 22, 26, 30, 34, 38, 42
NVEC = 46


def build_rwkv_main(T):
    k = KB()
    nc = k.nc
    xT = nc.dram_tensor("xT", [128, 8, T], F32, kind="ExternalInput").ap()
    Wd = nc.dram_tensor("W", [128, 8, NB * 64], F32, kind="ExternalInput").ap()
    vecd = nc.dram_tensor("vec", [64, NVEC], F32, kind="ExternalInput").ap()
    gpd = nc.dram_tensor("gpre", [128, 8], F32, kind="ExternalInput").ap()
    wdud = nc.dram_tensor("wdu", [64, NH * 64], F32, kind="ExternalInput").ap()
    wiud = nc.dram_tensor("wiu", [64, NH * 64], F32, kind="ExternalInput").ap()
    cstd = nc.dram_tensor("cst", [128, 4, 128], F32, kind="ExternalInput").ap()
    yTd = nc.dram_tensor("yT", [64, NH, T], F32, kind="ExternalOutput").ap()
    emit_rwkv_main(k, T, xT, Wd, vecd, gpd, wdud, wiud, cstd, yTd)
    k.finish()
    return k


def emit_rwkv_main(k, T, xT, Wd, vecd, gpd, wdud, wiud, cstd, yTd):
    nc = k.nc
    nst = T // ST
    nck = ST // C
    k.begin_stage()
    XT, bXT = k.sb("XT", [128, 8, ST])
    SQ, bSQ = k.sb("SQ", [128, 8, ST])
    HT, bHT = k.sb("HT", [128, 8, ST])
    RS, bRS = k.sb("RS", [128, ST])
    W, bW = k.sb("Wsb", [128, 8, NB * 64])
    VEC, bVEC = k.sb("VEC", [64, NVEC])
    OMM, bOMM = k.sb("OMM", [64, NB])
    OMKA, bOMKA = k.sb("OMKA", [64, NH])
    GP, bGP = k.sb("GP", [128, 8])
    WDU, bWDU = k.sb("WDU", [64, NH * 64])
    WIU, bWIU = k.sb("WIU", [64, NH * 64])
    CST, bCST = k.sb("CST", [128, 4, 128])
    MSKI, bMSKI = k.sb("MSKI", [128, NH, 128])
    MSKS, bMSKS = k.sb("MSKS", [128, NH, 128])
    MSKL, bMSKL = k.sb("MSKL", [128, NH, 128])
    ONES, bONES = k.sb("ONESr", [64, C])
    P, bP = k.sb("P", [64, NB, ST])
    TMP, bTMP = k.sb("TMP", [64, ST + 1])
    CAR, bCAR = k.sb("CAR", [64, NB])
    names = ["SG", "A", "CS", "CSX", "E1", "EX", "E2", "E3", "KAP", "T1", "T2", "BM", "K2", "BV",
             "RT", "KPT", "KT", "BT", "KE", "BE", "YT", "YC"]
    Z = {}
    for n in names:
        Z[n] = k.sb(n, [64, NH, ST])
    SCE, bSCE = k.sb("SCE", [64, NH, nck])
    WC, bWC = k.sb("WC", [64, NH, nck])
    H, bH = k.sb("H", [64, NH, HS])
    VT, bVT = k.sb("VT", [128, NH, HS])
    KET, bKET = k.sb("KET", [128, NH, HS])
    BET, bBET = k.sb("BET", [128, NH, HS])
    RSB, bRSB = k.sb("RSB", [128, NH, HS])
    USB, bUSB = k.sb("USB", [128, NH, HS])
    mats = {}
    for n in ["QN0", "QT0", "QN1", "QT1", "TT0", "TT1", "AKK", "ARK", "ARB"]:
        mats[n] = k.sb(n, [128, NH, 128])
    newps = k.newps

    ident = CST[:, 0, :]
    onesm = CST[:, 1, :]

    for kc in range(8):
        k.dma(k.sp, W[:, kc, :], Wd[:, kc, :], W=[bW], key=bW)
    k.dma(k.sp, VEC[:, :], vecd[:, :], W=[bVEC], key=bVEC)
    k.dma(k.sp, GP[:, :], gpd[:, :], W=[bGP], key=bGP)
    k.dma(k.sp, WDU[:, :], wdud[:, :], W=[bWDU], key=bWDU)
    k.dma(k.sp, WIU[:, :], wiud[:, :], W=[bWIU], key=bWIU)
    k.dma(k.sp, CST[:, :, :], cstd[:, :, :], W=[bCST], key=bCST)
    k.ts(OMM[:, :], VEC[:, V_MU:V_MU + NB], -1.0, 1.0, ALU.mult, ALU.add, [bVEC], [bOMM])
    k.ts(OMKA[:, :], VEC[:, V_KA:V_KA + NH], -1.0, 1.0, ALU.mult, ALU.add, [bVEC], [bOMKA])
    for h in range(NH):
        k.cp(MSKI[:, h, :], CST[:, 2, :], [bCST], [bMSKI])
        k.cp(MSKS[:, h, :], CST[:, 3, :], [bCST], [bMSKS])
    pt, bpt = newps()
    k.tr(pt[:, 0:128], CST[:, 3, :], ident, [bCST], [bpt])
    for h in range(NH):
        k.cp(MSKL[:, h, :], pt[:, 0:128], [bpt], [bMSKL])
    k.op(k.dve, lambda: nc.vector.memset(ONES[:, :], 1.0), [], [bONES])
    k.op(k.dve, lambda: nc.vector.memset(CAR[:, :], 0.0), [], [bCAR])
    k.op(k.dve, lambda: nc.vector.memset(H[:, :, :], 0.0), [], [bH])

    def z(n):
        return Z[n][0]

    def zb(n):
        return Z[n][1]

    def flat(t):
        return t[:, :, :].rearrange("p a b -> p (a b)")

    for s in range(nst):
        t0 = s * ST
        k.dma(k.sp, XT[:, :, :], xT[:, :, t0:t0 + ST], W=[bXT], key=bXT)
        k.tt(flat(SQ), flat(XT), flat(XT), ALU.mult, [bXT], [bSQ])
        ps, bps = newps()
        for kc in range(8):
            k.mm(ps[:, 0:ST], onesm, SQ[:, kc, :], kc == 0, kc == 7, [bSQ, bCST], [bps])
        k.actf(RS[:, :], ps[:, 0:ST], AF.Sqrt, [bps], [bRS], bias=RMS_EPS, scale=1.0 / D)
        k.recip(RS[:, :], RS[:, :], [bRS], [bRS])
        for kc in range(8):
            k.stt(HT[:, kc, :], XT[:, kc, :], GP[:, kc:kc + 1], RS[:, :], ALU.mult, ALU.mult, [bXT, bGP, bRS], [bHT])
        for j in range(NB):
            ps, bps = newps()
            for kc in range(8):
                k.mm(ps[0:64, 0:ST], W[:, kc, j * 64:(j + 1) * 64], HT[:, kc, :], kc == 0, kc == 7, [bW, bHT], [bps])
            k.actf(TMP[:, 1:ST + 1], ps[0:64, 0:ST], AF.Copy, [bps, bVEC], [bTMP], scale=VEC[:, V_MU + j:V_MU + j + 1])
            k.cp(TMP[:, 0:1], CAR[:, j:j + 1], [bCAR], [bTMP], eng=k.act)
            k.stt(P[:, j, :], ps[0:64, 0:ST], OMM[:, j:j + 1], TMP[:, 0:ST], ALU.mult, ALU.add, [bps, bOMM, bTMP], [bP])
            k.cp(CAR[:, j:j + 1], TMP[:, ST:ST + 1], [bTMP], [bCAR], eng=k.act)
        k.actf(P[:, 16, :], P[:, 16, :], AF.Tanh, [bP], [bP])
        for (lw, vb, dst, col) in ((WDU, bWDU, "SG", 16), (WIU, bWIU, "A", 17)):
            for h2 in range(NH // 2):
                ps, bps = newps()
                for hh in range(2):
                    h = h2 * 2 + hh
                    k.mm(ps[0:64, hh * ST:(hh + 1) * ST], lw[:, h * 64:(h + 1) * 64], P[:, col, :], True, True, [vb, bP], [bps])
                for hh in range(2):
                    h = h2 * 2 + hh
                    vcol = (V_W0 if dst == "SG" else V_A0) + h
                    k.actf(z(dst)[:, h, :], ps[0:64, hh * ST:(hh + 1) * ST], AF.Sigmoid, [bps, bVEC], [zb(dst)],
                           bias=VEC[:, vcol:vcol + 1])
        for h in range(NH):
            for ck in range(nck):
                sl = slice(ck * C, (ck + 1) * C)
                k.op(k.dve, lambda h=h, sl=sl: nc.vector.tensor_tensor_scan(z("CS")[:, h, sl], ONES[:, :], z("SG")[:, h, sl], 0.0, ALU.mult, ALU.add),
                     [bONES, zb("SG")], [zb("CS")])
        k.tt(flat(z("CSX")), flat(z("CS")), flat(z("SG")), ALU.subtract, [zb("CS"), zb("SG")], [zb("CSX")])
        k.actf(flat(z("E1")), flat(z("CS")), AF.Exp, [zb("CS")], [zb("E1")], scale=S_DEC)
        k.actf(flat(z("EX")), flat(z("CSX")), AF.Exp, [zb("CSX")], [zb("EX")], scale=S_DEC)
        k.actf(flat(z("E2")), flat(z("CS")), AF.Exp, [zb("CS")], [zb("E2")], scale=-S_DEC)
        for h in range(NH):
            for ck in range(nck):
                e = (ck + 1) * C - 1
                k.ts(SCE[:, h, ck:ck + 1], z("CS")[:, h, e:e + 1], S_DEC, None, ALU.mult, None, [zb("CS")], [bSCE])
        k.actf(SCE[:, :, :].rearrange("p a b -> p (a b)") if False else WC[:, :, :].rearrange("p a b -> p (a b)"),
               SCE[:, :, :].rearrange("p a b -> p (a b)"), AF.Exp, [bSCE], [bWC])
        for h in range(NH):
            for ck in range(nck):
                sl = slice(ck * C, (ck + 1) * C)
                k.actf(z("E3")[:, h, sl], z("CS")[:, h, sl], AF.Exp, [zb("CS"), bSCE], [zb("E3")],
                       bias=SCE[:, h, ck:ck + 1], scale=-S_DEC)
        for h in range(NH):
            k.ts(z("KAP")[:, h, :], P[:, 4 + h, :], VEC[:, V_KK + h:V_KK + h + 1], None, ALU.mult, None, [bP, bVEC], [zb("KAP")])
        k.tt(flat(z("T1")), flat(z("KAP")), flat(z("KAP")), ALU.mult, [zb("KAP")], [zb("T1")])
        for h2 in range(NH // 2):
            ps, bps = newps()
            for hh in range(2):
                h = h2 * 2 + hh
                k.mm(ps[0:64, hh * ST:(hh + 1) * ST], onesm[0:64, 0:64], z("T1")[:, h, :], True, True, [bCST, zb("T1")], [bps])
            k.actf(z("T2")[:, h2 * 2:h2 * 2 + 2, :].rearrange("p a b -> p (a b)"), ps[0:64, 0:2 * ST], AF.Sqrt, [bps], [zb("T2")])
        k.ts(flat(z("T2")), flat(z("T2")), 1e-12, None, ALU.max, None, [zb("T2")], [zb("T2")])
        k.recip(flat(z("T2")), flat(z("T2")), [zb("T2")], [zb("T2")])
        k.tt(flat(z("KAP")), flat(z("KAP")), flat(z("T2")), ALU.mult, [zb("KAP"), zb("T2")], [zb("KAP")])
        k.tt(flat(z("BM")), flat(z("A")), flat(z("KAP")), ALU.mult, [zb("A"), zb("KAP")], [zb("BM")])
        for h in range(NH):
            k.ts(z("T1")[:, h, :], z("A")[:, h, :], VEC[:, V_KA + h:V_KA + h + 1], OMKA[:, h:h + 1], ALU.mult, ALU.add,
                 [zb("A"), bVEC, bOMKA], [zb("T1")])
        k.tt(flat(z("K2")), P[:, 4:8, :].rearrange("p a b -> p (a b)"), flat(z("T1")), ALU.mult, [bP, zb("T1")], [zb("K2")])
        for h in range(NH):
            k.stt(z("T2")[:, h, :], P[:, h, :], VEC[:, V_RK + h:V_RK + h + 1], z("K2")[:, h, :], ALU.mult, ALU.mult,
                  [bP, bVEC, zb("K2")], [zb("T2")])
        for h2 in range(NH // 2):
            ps, bps = newps()
            for hh in range(2):
                h = h2 * 2 + hh
                k.mm(ps[0:64, hh * ST:(hh + 1) * ST], onesm[0:64, 0:64], z("T2")[:, h, :], True, True, [bCST, zb("T2")], [bps])
            k.tt(z("BV")[:, h2 * 2:h2 * 2 + 2, :].rearrange("p a b -> p (a b)"), ps[0:64, 0:2 * ST],
                 P[:, 8 + h2 * 2:10 + h2 * 2, :].rearrange("p a b -> p (a b)"), ALU.mult, [bps, bP], [zb("BV")])
        k.actf(P[:, 12:16, :].rearrange("p a b -> p (a b)"), P[:, 12:16, :].rearrange("p a b -> p (a b)"), AF.Silu, [bP], [bP])
        k.tt(flat(z("RT")), P[:, 0:4, :].rearrange("p a b -> p (a b)"), flat(z("E1")), ALU.mult, [bP, zb("E1")], [zb("RT")])
        k.tt(flat(z("KPT")), flat(z("KAP")), flat(z("EX")), ALU.mult, [zb("KAP"), zb("EX")], [zb("KPT")])
        k.tt(flat(z("KT")), flat(z("K2")), flat(z("E2")), ALU.mult, [zb("K2"), zb("E2")], [zb("KT")])
        k.tt(flat(z("BT")), flat(z("BM")), flat(z("E2")), ALU.mult, [zb("BM"), zb("E2")], [zb("BT")])
        k.tt(flat(z("KE")), flat(z("K2")), flat(z("E3")), ALU.mult, [zb("K2"), zb("E3")], [zb("KE")])
        k.tt(flat(z("BE")), flat(z("BM")), flat(z("E3")), ALU.mult, [zb("BM"), zb("E3")], [zb("BE")])

        for ck in range(nck):
            sl = slice(ck * C, (ck + 1) * C)
            for (src, sb_, dst, db, neg) in ((P, bP, VT, bVT, False), (z("KE"), zb("KE"), KET, bKET, False),
                                             (z("BE"), zb("BE"), BET, bBET, True)):
                ps, bps = newps()
                for h in range(NH):
                    inp = src[:, 8 + h, sl] if src is P else src[:, h, sl]
                    k.tr(ps[:, h * HS:(h + 1) * HS], inp, ident[0:64, 0:64], [sb_, bCST], [bps])
                dflat = dst[:, :, :].rearrange("p a b -> p (a b)")
                if neg:
                    k.ts(dflat, ps[:, 0:NH * HS], -1.0, None, ALU.mult, None, [bps], [db])
                else:
                    k.cp(dflat, ps[:, 0:NH * HS], [bps], [db], eng=k.act)

            def amat(lh, lb, rh, rb, dst, mask, mb, neg):
                ps, bps = newps()
                for h in range(NH):
                    k.mm(ps[:, h * 128:(h + 1) * 128], lh[:, h, sl], rh[:, h, sl], True, True, [lb, rb], [bps])
                d, db = mats[dst]
                dfl = d[:, :, :].rearrange("p a b -> p (a b)")
                mfl = mask[:, :, :].rearrange("p a b -> p (a b)")
                if neg:
                    k.stt(dfl, ps[:, :], -1.0, mfl, ALU.mult, ALU.mult, [bps, mb], [db])
                else:
                    k.tt(dfl, ps[:, :], mfl, ALU.mult, [bps, mb], [db])

            amat(z("BT"), zb("BT"), z("KPT"), zb("KPT"), "QT0", MSKS, bMSKS, True)
            amat(z("KPT"), zb("KPT"), z("BT"), zb("BT"), "QN0", MSKL, bMSKL, True)
            amat(z("KT"), zb("KT"), z("KPT"), zb("KPT"), "AKK", MSKS, bMSKS, False)
            amat(z("KT"), zb("KT"), z("RT"), zb("RT"), "ARK", MSKI, bMSKI, False)
            amat(z("BT"), zb("BT"), z("RT"), zb("RT"), "ARB", MSKI, bMSKI, True)
            tt_cur = "TT0"
            for h in range(NH):
                k.tt(mats["TT0"][0][:, h, :], mats["QT0"][0][:, h, :], ident, ALU.add, [mats["QT0"][1], bCST], [mats["TT0"][1]])
            qn, qt = "QN0", "QT0"
            nlev = 6
            for lv in range(nlev):
                qn2 = "QN1" if qn == "QN0" else "QN0"
                qt2 = "QT1" if qt == "QT0" else "QT0"
                tt2 = "TT1" if tt_cur == "TT0" else "TT0"
                ps, bps = newps()
                for h in range(NH):
                    k.mm(ps[:, h * 128:(h + 1) * 128], mats[qt][0][:, h, :], mats[qn][0][:, h, :], True, True,
                         [mats[qt][1], mats[qn][1]], [bps])
                last = lv == nlev - 1
                if not last:
                    ps2, bps2 = newps()
                    for h in range(NH):
                        k.mm(ps2[:, h * 128:(h + 1) * 128], mats[qn][0][:, h, :], mats[qt][0][:, h, :], True, True,
                             [mats[qt][1], mats[qn][1]], [bps2])
                k.cp(mats[qn2][0][:, :, :].rearrange("p a b -> p (a b)"), ps[:, :], [bps], [mats[qn2][1]], eng=k.act)
                if not last:
                    k.cp(mats[qt2][0][:, :, :].rearrange("p a b -> p (a b)"), ps2[:, :], [bps2], [mats[qt2][1]])
                ps3, bps3 = newps()
                for h in range(NH):
                    k.mm(ps3[:, h * 128:(h + 1) * 128], mats[qn2][0][:, h, :], mats[tt_cur][0][:, h, :], True, True,
                         [mats[qn2][1], mats[tt_cur][1]], [bps3])
                k.tt(mats[tt2][0][:, :, :].rearrange("p a b -> p (a b)"), ps3[:, :],
                     mats[tt_cur][0][:, :, :].rearrange("p a b -> p (a b)"), ALU.add, [bps3, mats[tt_cur][1]], [mats[tt2][1]])
                qn, qt, tt_cur = qn2, qt2, tt2
            TT, bTT = mats[tt_cur]
            AKK, bAKK = mats["AKK"]
            ARK, bARK = mats["ARK"]
            ARB, bARB = mats["ARB"]
            ps, bps = newps()
            for h in range(NH):
                k.mm(ps[:, h * HS:(h + 1) * HS], z("KPT")[:, h, sl], H[:, h, :], True, False, [zb("KPT"), bH], [bps])
                k.mm(ps[:, h * HS:(h + 1) * HS], AKK[:, h, :], VT[:, h, :], False, True, [bAKK, bVT], [bps])
            k.cp(RSB[:, :, :].rearrange("p a b -> p (a b)"), ps[:, 0:NH * HS], [bps], [bRSB], eng=k.act)
            ps, bps = newps()
            for h in range(NH):
                k.mm(ps[:, h * HS:(h + 1) * HS], TT[:, h, :], RSB[:, h, :], True, True, [bTT, bRSB], [bps])
            k.cp(USB[:, :, :].rearrange("p a b -> p (a b)"), ps[:, 0:NH * HS], [bps], [bUSB], eng=k.act)
            psy, bpsy = newps()
            for h in range(NH):
                k.mm(psy[0:64, h * 128:(h + 1) * 128], H[:, h, :], z("RT")[:, h, sl], True, False, [bH, zb("RT")], [bpsy])
                k.mm(psy[0:64, h * 128:(h + 1) * 128], VT[:, h, :], ARK[:, h, :], False, False, [bVT, bARK], [bpsy])
                k.mm(psy[0:64, h * 128:(h + 1) * 128], USB[:, h, :], ARB[:, h, :], False, True, [bUSB, bARB], [bpsy])
            for h in range(NH):
                k.cp(z("YT")[:, h, sl], psy[0:64, h * 128:(h + 1) * 128], [bpsy], [zb("YT")], eng=k.act)
            pss, bpss = newps()
            for h in range(NH):
                k.mm(pss[0:64, h * HS:(h + 1) * HS], KET[:, h, :], VT[:, h, :], True, False, [bKET, bVT], [bpss])
                k.mm(pss[0:64, h * HS:(h + 1) * HS], BET[:, h, :], USB[:, h, :], False, True, [bBET, bUSB], [bpss])
            for h in range(NH):
                k.stt(H[:, h, :], H[:, h, :], WC[:, h, ck:ck + 1], pss[0:64, h * HS:(h + 1) * HS], ALU.mult, ALU.add,
                      [bH, bWC, bpss], [bH])

        for h2 in range(NH // 2):
            ps, bps = newps()
            for hh in range(2):
                h = h2 * 2 + hh
                k.mm(ps[0:64, hh * ST:(hh + 1) * ST], onesm[0:64, 0:64], z("YT")[:, h, :], True, True, [bCST, zb("YT")], [bps])
            k.stt(z("YC")[:, h2 * 2:h2 * 2 + 2, :].rearrange("p a b -> p (a b)"), ps[0:64, 0:2 * ST], -1.0 / HS,
                  z("YT")[:, h2 * 2:h2 * 2 + 2, :].rearrange("p a b -> p (a b)"), ALU.mult, ALU.add, [bps, zb("YT")], [zb("YC")])
        k.tt(flat(z("T1")), flat(z("YC")), flat(z("YC")), ALU.mult, [zb("YC")], [zb("T1")])
        for h2 in range(NH // 2):
            ps, bps = newps()
            for hh in range(2):
                h = h2 * 2 + hh
                k.mm(ps[0:64, hh * ST:(hh + 1) * ST], onesm[0:64, 0:64], z("T1")[:, h, :], True, True, [bCST, zb("T1")], [bps])
            k.actf(z("T2")[:, h2 * 2:h2 * 2 + 2, :].rearrange("p a b -> p (a b)"), ps[0:64, 0:2 * ST], AF.Sqrt, [bps], [zb("T2")],
                   bias=GN_EPS, scale=1.0 / HS)
        k.recip(flat(z("T2")), flat(z("T2")), [zb("T2")], [zb("T2")])
        k.tt(flat(z("YC")), flat(z("YC")), flat(z("T2")), ALU.mult, [zb("YC"), zb("T2")], [zb("YC")])
        for h in range(NH):
            k.ts(z("YC")[:, h, :], z("YC")[:, h, :], VEC[:, V_LW + h:V_LW + h + 1], VEC[:, V_LB + h:V_LB + h + 1], ALU.mult, ALU.add,
                 [zb("YC"), bVEC], [zb("YC")])
        k.tt(flat(z("YC")), flat(z("YC")), flat(z("BV")), ALU.add, [zb("YC"), zb("BV")], [zb("YC")])
        k.tt(flat(z("YT")), flat(z("YC")), P[:, 12:16, :].rearrange("p a b -> p (a b)"), ALU.mult, [zb("YC"), bP], [zb("YT")])
        k.dma(k.sp, yTd[:, :, t0:t0 + ST], z("YT")[:, :, :], R=[zb("YT")], key=zb("YT"))
    k.end_stage()


def rwkv_main_inputs(x_b, w_in, mu, w0, wdu, a0, wiu, k_k, k_a, r_k, ln_w, ln_b, gpre, hg):
    xT = None if x_b is None else feat_major(x_b)
    ch = np.arange(hg * NH * HS, (hg + 1) * NH * HS)
    cols = np.concatenate([ch, D + ch, 2 * D + ch, 3 * D + ch, 4 * D + np.arange(128)])
    Wc = w_in[:, cols]
    W = np.ascontiguousarray(Wc.reshape(8, 128, NB * 64).transpose(1, 0, 2))
    vec = np.zeros((64, NVEC), np.float32)
    vec[:, V_MU:V_MU + NB] = mu[cols].reshape(NB, 64).T
    for nm, arr in ((V_W0, w0), (V_A0, a0), (V_KK, k_k), (V_KA, k_a), (V_RK, r_k.reshape(-1)), (V_LW, ln_w), (V_LB, ln_b)):
        vec[:, nm:nm + NH] = arr[ch].reshape(NH, 64).T
    gp = np.ascontiguousarray(gpre.reshape(8, 128).T)
    cst = np.zeros((128, 4, 128), np.float32)
    cst[:, 0, :] = np.eye(128)
    cst[:, 1, :] = 1.0
    cst[:, 2, :] = np.triu(np.ones((128, 128)))
    cst[:, 3, :] = np.triu(np.ones((128, 128)), 1)
    return {"xT": xT, "W": W, "vec": vec, "gpre": gp, "wdu": np.ascontiguousarray(wdu[:, ch]),
            "wiu": np.ascontiguousarray(wiu[:, ch]), "cst": cst}


GDK, GDV = 128, 256
GW = 2 * GDK + 2 * GDV + 16


def build_gla_main(T):
    k = KB()
    nc = k.nc
    xT = nc.dram_tensor("xT", [128, 8, T], F32, kind="ExternalInput").ap()
    Wd = nc.dram_tensor("W", [128, 8, GW], F32, kind="ExternalInput").ap()
    vecd = nc.dram_tensor("vec", [128, 3], F32, kind="ExternalInput").ap()
    gpd = nc.dram_tensor("gpre", [128, 8], F32, kind="ExternalInput").ap()
    wupd = nc.dram_tensor("wup", [16, GDK], F32, kind="ExternalInput").ap()
    cstd = nc.dram_tensor("cst", [128, 4, 128], F32, kind="ExternalInput").ap()
    yTd = nc.dram_tensor("yT", [128, 2, T], F32, kind="ExternalOutput").ap()
    emit_gla_main(k, T, xT, Wd, vecd, gpd, wupd, cstd, yTd)
    k.finish()
    return k


def emit_gla_main(k, T, xT, Wd, vecd, gpd, wupd, cstd, yTd):
    nc = k.nc
    nst = T // ST
    nck = ST // C
    k.begin_stage()
    XT, bXT = k.sb("XT", [128, 8, ST])
    SQ, bSQ = k.sb("SQ", [128, 8, ST])
    HT, bHT = k.sb("HT", [128, 8, ST])
    RS, bRS = k.sb("RS", [128, ST])
    W, bW = k.sb("Wsb", [128, 8, GW])
    VEC, bVEC = k.sb("VEC", [128, 3])
    GP, bGP = k.sb("GP", [128, 8])
    WUP, bWUP = k.sb("WUP", [16, GDK])
    CST, bCST = k.sb("CST", [128, 4, 128])
    ONES, bONES = k.sb("ONESr", [128, C])
    P, bP = k.sb("P", [128, 6, ST])
    AD, bAD = k.sb("AD", [16, ST])
    Z = {}
    for n in ["LA", "CS", "E1", "E2", "E3", "RT", "KT", "KE", "RSTD"]:
        Z[n] = k.sb(n, [128, ST])
    YT, bYT = k.sb("YT", [128, 2, ST])
    Y2, bY2 = k.sb("Y2", [128, 2, ST])
    SCE, bSCE = k.sb("SCE", [128, nck])
    WC, bWC = k.sb("WC", [128, nck])
    H, bH = k.sb("H", [128, GDV])
    VT, bVT = k.sb("VT", [128, GDV])
    KET, bKET = k.sb("KET", [128, GDK])
    ARK, bARK = k.sb("ARK", [128, 128])
    newps = k.newps

    ident = CST[:, 0, :]
    onesm = CST[:, 1, :]
    for kc in range(8):
        k.dma(k.sp, W[:, kc, :], Wd[:, kc, :], W=[bW], key=bW)
    k.dma(k.sp, VEC[:, :], vecd[:, :], W=[bVEC], key=bVEC)
    k.dma(k.sp, GP[:, :], gpd[:, :], W=[bGP], key=bGP)
    k.dma(k.sp, WUP[:, :], wupd[:, :], W=[bWUP], key=bWUP)
    k.dma(k.sp, CST[:, :, :], cstd[:, :, :], W=[bCST], key=bCST)
    k.op(k.dve, lambda: nc.vector.memset(ONES[:, :], 1.0), [], [bONES])
    k.op(k.dve, lambda: nc.vector.memset(H[:, :], 0.0), [], [bH])

    def z(n):
        return Z[n][0]

    def zb(n):
        return Z[n][1]

    def flat(t):
        return t[:, :, :].rearrange("p a b -> p (a b)")

    for s in range(nst):
        t0 = s * ST
        k.dma(k.sp, XT[:, :, :], xT[:, :, t0:t0 + ST], W=[bXT], key=bXT)
        k.tt(flat(SQ), flat(XT), flat(XT), ALU.mult, [bXT], [bSQ])
        ps, bps = newps()
        for kc in range(8):
            k.mm(ps[:, 0:ST], onesm, SQ[:, kc, :], kc == 0, kc == 7, [bSQ, bCST], [bps])
        k.actf(RS[:, :], ps[:, 0:ST], AF.Sqrt, [bps], [bRS], bias=RMS_EPS, scale=1.0 / D)
        k.recip(RS[:, :], RS[:, :], [bRS], [bRS])
        for kc in range(8):
            k.stt(HT[:, kc, :], XT[:, kc, :], GP[:, kc:kc + 1], RS[:, :], ALU.mult, ALU.mult, [bXT, bGP, bRS], [bHT])
        for j in range(6):
            ps, bps = newps()
            for kc in range(8):
                k.mm(ps[:, 0:ST], W[:, kc, j * 128:(j + 1) * 128], HT[:, kc, :], kc == 0, kc == 7, [bW, bHT], [bps])
            k.cp(P[:, j, :], ps[:, 0:ST], [bps], [bP], eng=(k.act if j % 2 else k.dve))
        ps, bps = newps()
        for kc in range(8):
            k.mm(ps[0:16, 0:ST], W[:, kc, 768:784], HT[:, kc, :], kc == 0, kc == 7, [bW, bHT], [bps])
        k.cp(AD[:, :], ps[0:16, 0:ST], [bps], [bAD])
        ps, bps = newps()
        k.mm(ps[:, 0:ST], WUP[:, :], AD[:, :], True, True, [bWUP, bAD], [bps])
        k.actf(z("LA")[:, :], ps[:, 0:ST], AF.Sigmoid, [bps, bVEC], [zb("LA")], bias=VEC[:, 0:1])
        k.actf(z("LA")[:, :], z("LA")[:, :], AF.Ln, [zb("LA")], [zb("LA")])
        for ck in range(nck):
            sl = slice(ck * C, (ck + 1) * C)
            k.op(k.dve, lambda sl=sl: nc.vector.tensor_tensor_scan(z("CS")[:, sl], ONES[:, :], z("LA")[:, sl], 0.0, ALU.mult, ALU.add),
                 [bONES, zb("LA")], [zb("CS")])
        k.actf(z("E1")[:, :], z("CS")[:, :], AF.Exp, [zb("CS")], [zb("E1")], scale=1.0 / 16)
        k.actf(z("E2")[:, :], z("CS")[:, :], AF.Exp, [zb("CS")], [zb("E2")], scale=-1.0 / 16)
        for ck in range(nck):
            e = (ck + 1) * C - 1
            k.ts(SCE[:, ck:ck + 1], z("CS")[:, e:e + 1], 1.0 / 16, None, ALU.mult, None, [zb("CS")], [bSCE])
        k.actf(WC[:, :], SCE[:, :], AF.Exp, [bSCE], [bWC])
        for ck in range(nck):
            sl = slice(ck * C, (ck + 1) * C)
            k.actf(z("E3")[:, sl], z("CS")[:, sl], AF.Exp, [zb("CS"), bSCE], [zb("E3")], bias=SCE[:, ck:ck + 1], scale=-1.0 / 16)
        k.stt(z("RT")[:, :], P[:, 0, :], float(GDK ** -0.5), z("E1")[:, :], ALU.mult, ALU.mult, [bP, zb("E1")], [zb("RT")])
        k.tt(z("KT")[:, :], P[:, 1, :], z("E2")[:, :], ALU.mult, [bP, zb("E2")], [zb("KT")])
        k.tt(z("KE")[:, :], P[:, 1, :], z("E3")[:, :], ALU.mult, [bP, zb("E3")], [zb("KE")])
        k.actf(P[:, 4:6, :].rearrange("p a b -> p (a b)"), P[:, 4:6, :].rearrange("p a b -> p (a b)"), AF.Silu, [bP], [bP])
        for ck in range(nck):
            sl = slice(ck * C, (ck + 1) * C)
            ps, bps = newps()
            for vb in range(2):
                k.tr(ps[:, vb * 128:(vb + 1) * 128], P[:, 2 + vb, sl], ident, [bP, bCST], [bps])
            k.cp(VT[:, :], ps[:, 0:GDV], [bps], [bVT], eng=k.act)
            ps, bps = newps()
            k.tr(ps[:, 0:128], z("KE")[:, sl], ident, [zb("KE"), bCST], [bps])
            k.cp(KET[:, :], ps[:, 0:128], [bps], [bKET])
            ps, bps = newps()
            k.mm(ps[:, 0:128], z("KT")[:, sl], z("RT")[:, sl], True, True, [zb("KT"), zb("RT")], [bps])
            k.tt(ARK[:, :], ps[:, 0:128], CST[:, 2, :], ALU.mult, [bps, bCST], [bARK])
            psy, bpsy = newps()
            for vb in range(2):
                k.mm(psy[:, vb * 128:(vb + 1) * 128], H[:, vb * 128:(vb + 1) * 128], z("RT")[:, sl], True, False, [bH, zb("RT")], [bpsy])
                k.mm(psy[:, vb * 128:(vb + 1) * 128], VT[:, vb * 128:(vb + 1) * 128], ARK[:, :], False, True, [bVT, bARK], [bpsy])
            for vb in range(2):
                k.cp(YT[:, vb, sl], psy[:, vb * 128:(vb + 1) * 128], [bpsy], [bYT], eng=k.act)
            pss, bpss = newps()
            k.mm(pss[:, 0:GDV], KET[:, :], VT[:, :], True, True, [bKET, bVT], [bpss])
            k.stt(H[:, :], H[:, :], WC[:, ck:ck + 1], pss[:, 0:GDV], ALU.mult, ALU.add, [bH, bWC, bpss], [bH])
        k.tt(flat(Y2), flat(YT), flat(YT), ALU.mult, [bYT], [bY2])
        ps, bps = newps()
        for vb in range(2):
            k.mm(ps[:, 0:ST], onesm, Y2[:, vb, :], vb == 0, vb == 1, [bCST, bY2], [bps])
        k.actf(z("RSTD")[:, :], ps[:, 0:ST], AF.Sqrt, [bps], [zb("RSTD")], bias=RMS_EPS, scale=1.0 / GDV)
        k.recip(z("RSTD")[:, :], z("RSTD")[:, :], [zb("RSTD")], [zb("RSTD")])
        for vb in range(2):
            k.stt(Y2[:, vb, :], YT[:, vb, :], VEC[:, 1 + vb:2 + vb], z("RSTD")[:, :], ALU.mult, ALU.mult, [bYT, bVEC, zb("RSTD")], [bY2])
        k.tt(flat(YT), flat(Y2), P[:, 4:6, :].rearrange("p a b -> p (a b)"), ALU.mult, [bY2, bP], [bYT])
        k.dma(k.sp, yTd[:, :, t0:t0 + ST], YT[:, :, :], R=[bYT], key=bYT)
    k.end_stage()


def make_cst():
    cst = np.zeros((128, 4, 128), np.float32)
    cst[:, 0, :] = np.eye(128)
    cst[:, 1, :] = 1.0
    cst[:, 2, :] = np.triu(np.ones((128, 128)))
    cst[:, 3, :] = np.triu(np.ones((128, 128)), 1)
    return cst


def feat_major(x_b):
    T = x_b.shape[0]
    return np.ascontiguousarray(x_b.T.reshape(8, 128, T).transpose(1, 0, 2))


def gla_main_inputs(xT, w_in, wup, b_alpha, g_head, gpre, h):
    cols = np.concatenate([h * GDK + np.arange(GDK), 512 + h * GDK + np.arange(GDK), 1024 + h * GDV + np.arange(GDV),
                           2048 + h * GDV + np.arange(GDV), 3072 + np.arange(16)])
    W = np.ascontiguousarray(w_in[:, cols].reshape(8, 128, GW).transpose(1, 0, 2))
    vec = np.zeros((128, 3), np.float32)
    vec[:, 0] = b_alpha[h * GDK:(h + 1) * GDK]
    vec[:, 1] = g_head[0:128]
    vec[:, 2] = g_head[128:256]
    return {"xT": xT, "W": W, "vec": vec, "gpre": np.ascontiguousarray(gpre.reshape(8, 128).T),
            "wup": np.ascontiguousarray(wup[:, h * GDK:(h + 1) * GDK]), "cst": make_cst()}


def build_out(Tc, KP, NKC):
    k = KB()
    nc = k.nc
    yTd = nc.dram_tensor("yT", [KP, NKC, Tc], F32, kind="ExternalInput").ap()
    Wd = nc.dram_tensor("W", [KP, NKC, D], F32, kind="ExternalInput").ap()
    xd = nc.dram_tensor("x", [Tc, D], F32, kind="ExternalInput").ap()
    gd = nc.dram_tensor("gpost", [128, D], F32, kind="ExternalInput").ap()
    od = nc.dram_tensor("out", [Tc, D], F32, kind="ExternalOutput").ap()
    W, bW = k.sb("Wsb", [KP, NKC, D])
    G, bG = k.sb("G", [128, D])
    Y, bY = k.sb("Y", [KP, NKC, 128])
    X, bX = k.sb("X", [128, D])
    YO, bYO = k.sb("YO", [128, D])
    SQ, bSQ = k.sb("SQ", [128, D])
    SS, bSS = k.sb("SS", [128, 1])
    O, bO = k.sb("O", [128, D])
    newps = k.newps

    for kc in range(NKC):
        k.dma(k.sp, W[:, kc, :], Wd[:, kc, :], W=[bW], key=bW)
    k.dma(k.sp, G[:, :], gd[:, :], W=[bG], key=bG)
    for t in range(Tc // 128):
        sl = slice(t * 128, (t + 1) * 128)
        k.dma(k.sp, Y[:, :, :], yTd[:, :, sl], W=[bY], key=bY)
        k.dma(k.sp, X[:, :], xd[sl, :], W=[bX], key=bX)
        for hf in range(2):
            ps, bps = newps()
            for kc in range(NKC):
                k.mm(ps[:, :], Y[:, kc, :], W[:, kc, hf * 512:(hf + 1) * 512], kc == 0, kc == NKC - 1, [bY, bW], [bps])
            k.cp(YO[:, hf * 512:(hf + 1) * 512], ps[:, :], [bps], [bYO], eng=(k.act if hf else k.dve))
        k.tt(SQ[:, :], YO[:, :], YO[:, :], ALU.mult, [bYO], [bSQ])
        k.op(k.dve, lambda: nc.vector.reduce_sum(SS[:, :], SQ[:, :], axis=mybir.AxisListType.X), [bSQ], [bSS])
        k.actf(SS[:, :], SS[:, :], AF.Sqrt, [bSS], [bSS], bias=RMS_EPS, scale=1.0 / D)
        k.recip(SS[:, :], SS[:, :], [bSS], [bSS])
        k.stt(O[:, :], YO[:, :], SS[:, 0:1], G[:, :], ALU.mult, ALU.mult, [bYO, bSS, bG], [bO])
        k.tt(O[:, :], O[:, :], X[:, :], ALU.add, [bO, bX], [bO])
        k.dma(k.sp, od[sl, :], O[:, :], R=[bO], key=bO)
    k.finish([bO])
    return k


def run_out(yT_b, w_out, x, gpost, KP, NKC):
    B, T = x.shape[0], x.shape[1]
    Tc = T // 4
    kb = build_out(Tc, KP, NKC)
    W = np.ascontiguousarray(w_out.reshape(NKC, KP, D).transpose(1, 0, 2))
    g = np.ascontiguousarray(np.broadcast_to(gpost[None, :], (128, D)))
    maps = []
    for c in range(8):
        b, sg = c // 4, c % 4
        maps.append({"yT": np.ascontiguousarray(yT_b[b][:, :, sg * Tc:(sg + 1) * Tc]), "W": W,
                     "x": np.ascontiguousarray(x[b, sg * Tc:(sg + 1) * Tc]), "gpost": g})
    res = run_bass_kernel_spmd(kb.nc, maps, core_ids=list(range(8)))
    out = np.empty((B, T, D), np.float32)
    for c in range(8):
        b, sg = c // 4, c % 4
        out[b, sg * Tc:(sg + 1) * Tc] = res.results[c]["out"]
    return out


def gla_layer(x, p):
    B, T = x.shape[0], x.shape[1]
    kb = build_gla_main(T)
    xTs = [feat_major(x[b]) for b in range(B)]
    maps = []
    for c in range(8):
        b, h = c // 4, c % 4
        maps.append(gla_main_inputs(xTs[b], p["gla_w_in"][0], p["gla_w_alpha_up"][0], p["gla_b_alpha"][0],
                                    p["gla_head_norm"][0], p["gla_pre_norm"][0], h))
    res = run_bass_kernel_spmd(kb.nc, maps, core_ids=list(range(8)))
    yT = [np.empty((128, 8, T), np.float32) for _ in range(B)]
    for c in range(8):
        b, h = c // 4, c % 4
        yT[b][:, 2 * h:2 * h + 2, :] = res.results[c]["yT"]
    return run_out(yT, p["gla_w_out"][0], x, p["gla_post_norm"][0], 128, 8)


def rwkv_layer(x, p):
    B, T = x.shape[0], x.shape[1]
    kb = build_rwkv_main(T)
    maps = []
    for c in range(8):
        b, hg = c // 4, c % 4
        maps.append(rwkv_main_inputs(x[b], p["rwkv_w_in"][0], p["rwkv_mu"][0], p["rwkv_w0"][0], p["rwkv_w_decay_up"][0],
                                     p["rwkv_a0"][0], p["rwkv_w_iclr_up"][0], p["rwkv_k_k"][0], p["rwkv_k_a"][0],
                                     p["rwkv_r_k"][0], p["rwkv_ln_w"][0], p["rwkv_ln_b"][0], p["rwkv_pre_norm"][0], hg))
    res = run_bass_kernel_spmd(kb.nc, maps, core_ids=list(range(8)))
    yT = [np.empty((64, 16, T), np.float32) for _ in range(B)]
    for c in range(8):
        b, hg = c // 4, c % 4
        yT[b][:, 4 * hg:4 * hg + 4, :] = res.results[c]["yT"]
    return run_out(yT, p["rwkv_w_out"][0], x, p["rwkv_post_norm"][0], 64, 16)


def emit_outproj_fm(k, T, KP, NKC, Yd, Wd, gpd, Xd, Od, cstd):
    nc = k.nc
    k.begin_stage()
    W, bW = k.sb("Wo", [KP, NKC, D])
    Y, bY = k.sb("Yo", [KP, NKC, ST])
    XT, bXT = k.sb("XTo", [128, 8, ST])
    YO, bYO = k.sb("YOo", [128, 8, ST])
    SQ, bSQ = k.sb("SQo", [128, 8, ST])
    RS, bRS = k.sb("RSo", [128, ST])
    O, bO = k.sb("Oo", [128, 8, ST])
    GP, bGP = k.sb("GPo", [128, 8])
    CST, bCST = k.sb("CSTo", [128, 4, 128])
    onesm = CST[:, 1, :]
    for kc in range(NKC):
        k.dma(k.sp, W[:, kc, :], Wd[:, kc, :], W=[bW], key=bW)
    k.dma(k.sp, GP[:, :], gpd[:, :], W=[bGP], key=bGP)
    k.dma(k.sp, CST[:, :, :], cstd[:, :, :], W=[bCST], key=bCST)

    def flat(t):
        return t[:, :, :].rearrange("p a b -> p (a b)")

    for s in range(T // ST):
        t0 = s * ST
        k.dma(k.sp, Y[:, :, :], Yd[:, :, t0:t0 + ST], W=[bY], key=bY)
        k.dma(k.sp, XT[:, :, :], Xd[:, :, t0:t0 + ST], W=[bXT], key=bXT)
        for fb in range(8):
            ps, bps = k.newps()
            for kc in range(NKC):
                k.mm(ps[:, 0:ST], W[:, kc, fb * 128:(fb + 1) * 128], Y[:, kc, :], kc == 0, kc == NKC - 1, [bW, bY], [bps])
            k.cp(YO[:, fb, :], ps[:, 0:ST], [bps], [bYO], eng=(k.act if fb % 2 else k.dve))
        k.tt(flat(SQ), flat(YO), flat(YO), ALU.mult, [bYO], [bSQ])
        ps, bps = k.newps()
        for fb in range(8):
            k.mm(ps[:, 0:ST], onesm, SQ[:, fb, :], fb == 0, fb == 7, [bSQ, bCST], [bps])
        k.actf(RS[:, :], ps[:, 0:ST], AF.Sqrt, [bps], [bRS], bias=RMS_EPS, scale=1.0 / D)
        k.recip(RS[:, :], RS[:, :], [bRS], [bRS])
        for fb in range(8):
            k.stt(O[:, fb, :], YO[:, fb, :], GP[:, fb:fb + 1], RS[:, :], ALU.mult, ALU.mult, [bYO, bGP, bRS], [bO])
        k.tt(flat(O), flat(O), flat(XT), ALU.add, [bO, bXT], [bO])
        k.dma(k.sp, Od[:, :, t0:t0 + ST], O[:, :, :], R=[bO], key=bO)
    k.end_stage()


def build_fused(T):
    k = KB()
    nc = k.nc

    def din(name, shape):
        return nc.dram_tensor(name, list(shape), F32, kind="ExternalInput").ap()

    xT = din("xT", [128, 8, T])
    cst = din("cst", [128, 4, 128])
    gW = din("gW", [4, 128, 8, GW])
    gvec = din("gvec", [4, 128, 3])
    gwup = din("gwup", [4, 16, GDK])
    ggpre = din("ggpre", [128, 8])
    gWo = din("gWo", [128, 8, D])
    ggpost = din("ggpost", [128, 8])
    rW = din("rW", [4, 128, 8, NB * 64])
    rvec = din("rvec", [4, 64, NVEC])
    rwdu = din("rwdu", [4, 64, NH * 64])
    rwiu = din("rwiu", [4, 64, NH * 64])
    rgpre = din("rgpre", [128, 8])
    rWo = din("rWo", [64, 16, D])
    rgpost = din("rgpost", [128, 8])
    outT = nc.dram_tensor("outT", [128, 8, T], F32, kind="ExternalOutput").ap()
    Y1 = nc.dram_tensor("Y1s", [128, 8, T], F32).ap()
    X1 = nc.dram_tensor("X1s", [128, 8, T], F32).ap()
    Y2 = nc.dram_tensor("Y2s", [64, 16, T], F32).ap()
    for h in range(4):
        emit_gla_main(k, T, xT, gW[h], gvec[h], ggpre, gwup[h], cst, Y1[:, 2 * h:2 * h + 2, :])
    emit_outproj_fm(k, T, 128, 8, Y1, gWo, ggpost, xT, X1, cst)
    for hg in range(4):
        emit_rwkv_main(k, T, X1, rW[hg], rvec[hg], rgpre, rwdu[hg], rwiu[hg], cst, Y2[:, 4 * hg:4 * hg + 4, :])
    emit_outproj_fm(k, T, 64, 16, Y2, rWo, rgpost, X1, outT, cst)
    k.finish()
    return k


def fused_inputs(x_b, p):
    g = [gla_main_inputs(None, p["gla_w_in"][0], p["gla_w_alpha_up"][0], p["gla_b_alpha"][0], p["gla_head_norm"][0],
                         p["gla_pre_norm"][0], h) for h in range(4)]
    r = [rwkv_main_inputs(None, p["rwkv_w_in"][0], p["rwkv_mu"][0], p["rwkv_w0"][0], p["rwkv_w_decay_up"][0],
                          p["rwkv_a0"][0], p["rwkv_w_iclr_up"][0], p["rwkv_k_k"][0], p["rwkv_k_a"][0],
                          p["rwkv_r_k"][0], p["rwkv_ln_w"][0], p["rwkv_ln_b"][0], p["rwkv_pre_norm"][0], hg) for hg in range(4)]
    fm = lambda v: np.ascontiguousarray(v.reshape(8, 128).T)
    return {
        "xT": feat_major(x_b), "cst": make_cst(),
        "gW": np.stack([a["W"] for a in g]), "gvec": np.stack([a["vec"] for a in g]),
        "gwup": np.stack([a["wup"] for a in g]), "ggpre": g[0]["gpre"],
        "gWo": np.ascontiguousarray(p["gla_w_out"][0].reshape(8, 128, D).transpose(1, 0, 2)), "ggpost": fm(p["gla_post_norm"][0]),
        "rW": np.stack([a["W"] for a in r]), "rvec": np.stack([a["vec"] for a in r]),
        "rwdu": np.stack([a["wdu"] for a in r]), "rwiu": np.stack([a["wiu"] for a in r]), "rgpre": r[0]["gpre"],
        "rWo": np.ascontiguousarray(p["rwkv_w_out"][0].reshape(16, 64, D).transpose(1, 0, 2)), "rgpost": fm(p["rwkv_post_norm"][0]),
    }


def kernel_unfused(**inputs):
    p = {k_: np.asarray(v, dtype=np.float32) for k_, v in inputs.items()}
    x = p["x"]
    x = gla_layer(x, p)
    x = rwkv_layer(x, p)
    return x


def kernel(**inputs):
    p = {k_: np.asarray(v, dtype=np.float32) for k_, v in inputs.items()}
    x = p["x"]
    B, T = x.shape[0], x.shape[1]
    kb = build_fused(T)
    per_b = [fused_inputs(x[b], p) for b in range(B)]
    maps = [per_b[c % B] for c in range(8)]
    res = run_bass_kernel_spmd(kb.nc, maps, core_ids=list(range(8)))
    out = np.empty((B, T, D), np.float32)
    for b in range(B):
        oT = res.results[b]["outT"]
        out[b] = oT.transpose(2, 1, 0).reshape(T, D)
    return out
```

```python
import math
import numpy as np
import concourse.bass as bass
import concourse.mybir as mybir
from concourse.bass_utils import run_bass_kernel_spmd

F32 = mybir.dt.float32
AF = mybir.ActivationFunctionType
ALU = mybir.AluOpType

D = 1024
C = 128
ST = 256
RMS_EPS = 1e-6
GN_EPS = 64e-5
S_DEC = -math.exp(-0.5)


class Buf:
    def __init__(self, name):
        self.name = name
        self.w = None
        self.r = []
        self.dsem = None
        self.dcnt = 0


class Eng:
    def __init__(self, k, e, name):
        self.k, self.e, self.name = k, e, name
        self.sem = k.new_sem(name)
        self.cnt = 0
        self.seen = {}

    def wait(self, ev):
        if ev is None:
            return
        sem, val = ev
        if self.seen.get(id(sem), 0) >= val:
            return
        if sem is self.sem and self.name == "pe":
            return
        self.e.wait_ge(sem, val)
        self.seen[id(sem)] = val


class KB:
    def __init__(self):
        self.nc = bass.Bass("TRN2", target_bir_lowering=False)
        self.stack = []
        nc = self.nc
        self.pe = Eng(self, nc.tensor, "pe")
        self.act = Eng(self, nc.scalar, "act")
        self.dve = Eng(self, nc.vector, "dve")
        self.pool = Eng(self, nc.gpsimd, "pool")
        self.sp = Eng(self, nc.sync, "sp")
        self.nbuf = 0
        self.ninst = 0
        self.guards = []
        self.marks = []
        self.dma_bufs = []
        self.psb = None
        self.psi = 0

    def new_sem(self, name):
        cm = self.nc.semaphore(name)
        s = cm.__enter__()
        self.stack.append(cm)
        return s

    def close(self):
        for cm in reversed(self.stack):
            cm.__exit__(None, None, None)
        self.stack = []

    def sb(self, name, shape, dtype=F32):
        self.nbuf += 1
        g = self.nc.sbuf_tensor("%s_%d" % (name, self.nbuf), list(shape), dtype)
        t = g.__enter__()
        self.guards.append(g)
        return t, Buf(name)

    def begin_stage(self):
        self.marks.append(len(self.guards))

    def end_stage(self):
        self.barrier()
        m = self.marks.pop()
        while len(self.guards) > m:
            self.guards.pop().__exit__(None, None, None)

    def barrier(self):
        engs = [self.pe, self.act, self.dve, self.sp]
        for e in engs:
            for f in engs:
                if f is not e and f.cnt > 0:
                    e.wait((f.sem, f.cnt))
            for b in self.dma_bufs:
                e.wait((b.dsem, b.dcnt))

    def psum_banks(self):
        if self.psb is None:
            self.psb = [self.ps("ps%d" % i, [128, 512]) for i in range(8)]
        return self.psb

    def newps(self):
        r = self.psum_banks()[self.psi % 8]
        self.psi += 1
        return r

    def ps(self, name, shape):
        t = self.nc.alloc_psum_tensor(name, list(shape), F32)
        return t, Buf(name)

    def _deps(self, eng, R, W):
        for b in R:
            eng.wait(b.w)
        for b in W:
            eng.wait(b.w)
            for ev in b.r:
                eng.wait(ev)

    def op(self, eng, fn, R=(), W=()):
        self._deps(eng, R, W)
        ins = fn()
        eng.cnt += 1
        ins.then_inc(eng.sem, 1)
        ev = (eng.sem, eng.cnt)
        eng.seen[id(eng.sem)] = max(eng.seen.get(id(eng.sem), 0), 0)
        for b in R:
            b.r.append(ev)
        for b in W:
            b.w = ev
            b.r = []
        self.ninst += 1
        return ins

    def dma(self, eng, out, in_, R=(), W=(), key=None):
        self._deps(eng, R, W)
        if key.dsem is None:
            self.nbuf += 1
            key.dsem = self.new_sem("d_%s_%d" % (key.name, self.nbuf))
            self.dma_bufs.append(key)
        ins = eng.e.dma_start(out=out, in_=in_)
        key.dcnt += 16
        ins.then_inc(key.dsem, 16)
        ev = (key.dsem, key.dcnt)
        for b in R:
            b.r.append(ev)
        for b in W:
            b.w = ev
            b.r = []
        self.ninst += 1
        return ev

    def mm(self, out, lhsT, rhs, start, stop, R, W):
        return self.op(self.pe, lambda: self.nc.tensor.matmul(out, lhsT=lhsT, rhs=rhs, start=start, stop=stop), R, W)

    def tr(self, out, in_, ident, R, W):
        return self.op(self.pe, lambda: self.nc.tensor.transpose(out, in_, ident), R, W)

    def actf(self, out, in_, func, R, W, bias=0.0, scale=1.0):
        return self.op(self.act, lambda: self.nc.scalar.activation(out=out, in_=in_, func=func, bias=bias, scale=scale), R, W)

    def tt(self, out, a, b, op, R, W, eng=None):
        eng = eng or self.dve
        return self.op(eng, lambda: eng.e.tensor_tensor(out, a, b, op), R, W)

    def ts(self, out, a, s1, s2, op0, op1, R, W, eng=None):
        eng = eng or self.dve
        if s2 is None:
            return self.op(eng, lambda: eng.e.tensor_single_scalar(out, a, s1, op0), R, W)
        return self.op(eng, lambda: eng.e.tensor_scalar(out, a, s1, s2, op0, op1), R, W)

    def stt(self, out, a, s, b, op0, op1, R, W, eng=None):
        eng = eng or self.dve
        return self.op(eng, lambda: eng.e.scalar_tensor_tensor(out, a, s, b, op0, op1), R, W)

    def cp(self, out, in_, R, W, eng=None):
        eng = eng or self.dve
        if eng is self.act:
            return self.op(eng, lambda: self.nc.scalar.copy(out, in_), R, W)
        return self.op(eng, lambda: eng.e.tensor_copy(out, in_), R, W)

    def recip(self, out, in_, R, W):
        return self.op(self.dve, lambda: self.nc.vector.reciprocal(out, in_), R, W)

    def finish(self, out_bufs=()):
        self.barrier()
        for b in self.dma_bufs:
            self.sp.e.wait_ge(b.dsem, b.dcnt)
        while self.guards:
            self.guards.pop().__exit__(None, None, None)
        self.close()


NH = 4
HS = 64
NB = 18
V_MU, V_W0, V_A0, V_KK, V_KA, V_RK, V_LW, V_LB = 0, 18, 22, 26, 30, 34, 38, 42
NVEC = 46


def build_rwkv_main(T):
    k = KB()
    nc = k.nc
    xT = nc.dram_tensor("xT", [128, 8, T], F32, kind="ExternalInput").ap()
    Wd = nc.dram_tensor("W", [128, 8, NB * 64], F32, kind="ExternalInput").ap()
    vecd = nc.dram_tensor("vec", [64, NVEC], F32, kind="ExternalInput").ap()
    gpd = nc.dram_tensor("gpre", [128, 8], F32, kind="ExternalInput").ap()
    wdud = nc.dram_tensor("wdu", [64, NH * 64], F32, kind="ExternalInput").ap()
    wiud = nc.dram_tensor("wiu", [64, NH * 64], F32, kind="ExternalInput").ap()
    cstd = nc.dram_tensor("cst", [128, 4, 128], F32, kind="ExternalInput").ap()
    yTd = nc.dram_tensor("yT", [64, NH, T], F32, kind="ExternalOutput").ap()
    emit_rwkv_main(k, T, xT, Wd, vecd, gpd, wdud, wiud, cstd, yTd)
    k.finish()
    return k


def emit_rwkv_main(k, T, xT, Wd, vecd, gpd, wdud, wiud, cstd, yTd, ypair=None, hg=0):
    nc = k.nc
    nst = T // ST
    nck = ST // C
    k.begin_stage()
    XT, bXT = k.sb("XT", [128, 8, ST])
    SQ, bSQ = k.sb("SQHT", [128, 8, ST])
    HT, bHT = SQ, bSQ
    RS, bRS = k.sb("RS", [128, ST])
    W, bW = k.sb("Wsb", [128, 8, NB * 64])
    VEC, bVEC = k.sb("VEC", [64, NVEC])
    OMM, bOMM = k.sb("OMM", [64, NB])
    OMKA, bOMKA = k.sb("OMKA", [64, NH])
    GP, bGP = k.sb("GP", [128, 8])
    WDU, bWDU = k.sb("WDU", [64, NH * 64])
    WIU, bWIU = k.sb("WIU", [64, NH * 64])
    CST, bCST = k.sb("CST", [128, 4, 128])
    MSKI, bMSKI = k.sb("MSKI", [128, NH, 128])
    MSKS, bMSKS = k.sb("MSKS", [128, NH, 128])
    MSKL, bMSKL = k.sb("MSKL", [128, NH, 128])
    ONES, bONES = k.sb("ONESr", [64, C])
    PP = [k.sb("P0", [64, NB, ST]), k.sb("P1", [64, NB, ST])]
    TMP, bTMP = k.sb("TMP", [64, ST + 1])
    CAR, bCAR = k.sb("CAR", [64, NB])
    names = ["SG", "A", "CS", "E2", "E3", "KAP", "T1", "T2", "BM", "K2", "BV",
             "RT", "KPT", "KT", "BT", "KE", "BE", "YT", "YC"]
    Z = {}
    for n in names:
        Z[n] = k.sb(n, [64, NH, ST])
    SCE, bSCE = k.sb("SCE", [64, NH, nck])
    WC, bWC = k.sb("WC", [64, NH, nck])
    H, bH = k.sb("H", [64, NH, HS])
    VT, bVT = k.sb("VT", [128, NH, HS])
    KET, bKET = k.sb("KET", [128, NH, HS])
    BET, bBET = k.sb("BET", [128, NH, HS])
    RSB, bRSB = k.sb("RSB", [128, NH, HS])
    USB, bUSB = k.sb("USB", [128, NH, HS])
    mats = {}
    for n in ["QN0", "QT0", "QN1", "QT1", "TT0", "TT1", "AKK", "ARK", "ARB"]:
        mats[n] = k.sb(n, [128, NH, 128])
    newps = k.newps

    ident = CST[:, 0, :]
    onesm = CST[:, 1, :]

    for kc in range(8):
        k.dma(k.sp, W[:, kc, :], Wd[:, kc, :], W=[bW], key=bW)
    k.dma(k.sp, VEC[:, :], vecd[:, :], W=[bVEC], key=bVEC)
    k.dma(k.sp, GP[:, :], gpd[:, :], W=[bGP], key=bGP)
    k.dma(k.sp, WDU[:, :], wdud[:, :], W=[bWDU], key=bWDU)
    k.dma(k.sp, WIU[:, :], wiud[:, :], W=[bWIU], key=bWIU)
    k.dma(k.sp, CST[:, :, :], cstd[:, :, :], W=[bCST], key=bCST)
    k.ts(OMM[:, :], VEC[:, V_MU:V_MU + NB], -1.0, 1.0, ALU.mult, ALU.add, [bVEC], [bOMM])
    k.ts(OMKA[:, :], VEC[:, V_KA:V_KA + NH], -1.0, 1.0, ALU.mult, ALU.add, [bVEC], [bOMKA])
    for h in range(NH):
        k.cp(MSKI[:, h, :], CST[:, 2, :], [bCST], [bMSKI])
        k.cp(MSKS[:, h, :], CST[:, 3, :], [bCST], [bMSKS])
    pt, bpt = newps()
    k.tr(pt[:, 0:128], CST[:, 3, :], ident, [bCST], [bpt])
    for h in range(NH):
        k.cp(MSKL[:, h, :], pt[:, 0:128], [bpt], [bMSKL])
    k.op(k.dve, lambda: nc.vector.memset(ONES[:, :], 1.0), [], [bONES])
    k.op(k.dve, lambda: nc.vector.memset(CAR[:, :], 0.0), [], [bCAR])
    k.op(k.dve, lambda: nc.vector.memset(H[:, :, :], 0.0), [], [bH])

    def z(n):
        return Z[n][0]

    def zb(n):
        return Z[n][1]

    def flat(t):
        return t[:, :, :].rearrange("p a b -> p (a b)")


    def inproj(s):
        P, bP = PP[s % 2]
        t0 = s * ST
        k.dma(k.sp, XT[:, :, :], xT[:, :, t0:t0 + ST], W=[bXT], key=bXT)
        k.tt(flat(SQ), flat(XT), flat(XT), ALU.mult, [bXT], [bSQ])
        ps, bps = newps()
        for kc in range(8):
            k.mm(ps[:, 0:ST], onesm, SQ[:, kc, :], kc == 0, kc == 7, [bSQ, bCST], [bps])
        k.actf(RS[:, :], ps[:, 0:ST], AF.Sqrt, [bps], [bRS], bias=RMS_EPS, scale=1.0 / D)
        k.recip(RS[:, :], RS[:, :], [bRS], [bRS])
        for kc in range(8):
            k.stt(HT[:, kc, :], XT[:, kc, :], GP[:, kc:kc + 1], RS[:, :], ALU.mult, ALU.mult, [bXT, bGP, bRS], [bHT])
        for j in range(NB):
            ps, bps = newps()
            for kc in range(8):
                k.mm(ps[0:64, 0:ST], W[:, kc, j * 64:(j + 1) * 64], HT[:, kc, :], kc == 0, kc == 7, [bW, bHT], [bps])
            k.actf(TMP[:, 1:ST + 1], ps[0:64, 0:ST], AF.Copy, [bps, bVEC], [bTMP], scale=VEC[:, V_MU + j:V_MU + j + 1])
            k.cp(TMP[:, 0:1], CAR[:, j:j + 1], [bCAR], [bTMP], eng=k.act)
            k.stt(P[:, j, :], ps[0:64, 0:ST], OMM[:, j:j + 1], TMP[:, 0:ST], ALU.mult, ALU.add, [bps, bOMM, bTMP], [bP])
            k.cp(CAR[:, j:j + 1], TMP[:, ST:ST + 1], [bTMP], [bCAR], eng=k.act)
            yield

    def prep(s):
        P, bP = PP[s % 2]
        k.actf(P[:, 16, :], P[:, 16, :], AF.Tanh, [bP], [bP])
        yield
        for (lw, vb, dst, col) in ((WDU, bWDU, "SG", 16), (WIU, bWIU, "A", 17)):
            for h2 in range(NH // 2):
                ps, bps = newps()
                for hh in range(2):
                    h = h2 * 2 + hh
                    k.mm(ps[0:64, hh * ST:(hh + 1) * ST], lw[:, h * 64:(h + 1) * 64], P[:, col, :], True, True, [vb, bP], [bps])
                    yield
                for hh in range(2):
                    h = h2 * 2 + hh
                    vcol = (V_W0 if dst == "SG" else V_A0) + h
                    k.actf(z(dst)[:, h, :], ps[0:64, hh * ST:(hh + 1) * ST], AF.Sigmoid, [bps, bVEC], [zb(dst)],
                           bias=VEC[:, vcol:vcol + 1])
                    yield
        for h in range(NH):
            for ck in range(nck):
                sl = slice(ck * C, (ck + 1) * C)
                k.op(k.dve, lambda h=h, sl=sl: nc.vector.tensor_tensor_scan(z("CS")[:, h, sl], ONES[:, :], z("SG")[:, h, sl], 0.0, ALU.mult, ALU.add),
                     [bONES, zb("SG")], [zb("CS")])
                yield
        k.tt(flat(z("SG")), flat(z("CS")), flat(z("SG")), ALU.subtract, [zb("CS"), zb("SG")], [zb("SG")])
        yield
        k.actf(flat(z("RT")), flat(z("CS")), AF.Exp, [zb("CS")], [zb("RT")], scale=S_DEC)
        yield
        k.actf(flat(z("KPT")), flat(z("SG")), AF.Exp, [zb("SG")], [zb("KPT")], scale=S_DEC)
        yield
        k.actf(flat(z("E2")), flat(z("CS")), AF.Exp, [zb("CS")], [zb("E2")], scale=-S_DEC)
        yield
        for h in range(NH):
            for ck in range(nck):
                e = (ck + 1) * C - 1
                k.ts(SCE[:, h, ck:ck + 1], z("CS")[:, h, e:e + 1], S_DEC, None, ALU.mult, None, [zb("CS")], [bSCE])
                yield
        k.actf(SCE[:, :, :].rearrange("p a b -> p (a b)") if False else WC[:, :, :].rearrange("p a b -> p (a b)"),
               SCE[:, :, :].rearrange("p a b -> p (a b)"), AF.Exp, [bSCE], [bWC])
        yield
        for h in range(NH):
            for ck in range(nck):
                sl = slice(ck * C, (ck + 1) * C)
                k.actf(z("E3")[:, h, sl], z("CS")[:, h, sl], AF.Exp, [zb("CS"), bSCE], [zb("E3")],
                       bias=SCE[:, h, ck:ck + 1], scale=-S_DEC)
                yield
        for h in range(NH):
            k.ts(z("KAP")[:, h, :], P[:, 4 + h, :], VEC[:, V_KK + h:V_KK + h + 1], None, ALU.mult, None, [bP, bVEC], [zb("KAP")])
            yield
        k.tt(flat(z("T1")), flat(z("KAP")), flat(z("KAP")), ALU.mult, [zb("KAP")], [zb("T1")])
        yield
        for h2 in range(NH // 2):
            ps, bps = newps()
            for hh in range(2):
                h = h2 * 2 + hh
                k.mm(ps[0:64, hh * ST:(hh + 1) * ST], onesm[0:64, 0:64], z("T1")[:, h, :], True, True, [bCST, zb("T1")], [bps])
                yield
            k.actf(z("T2")[:, h2 * 2:h2 * 2 + 2, :].rearrange("p a b -> p (a b)"), ps[0:64, 0:2 * ST], AF.Sqrt, [bps], [zb("T2")])
            yield
        k.ts(flat(z("T2")), flat(z("T2")), 1e-12, None, ALU.max, None, [zb("T2")], [zb("T2")])
        yield
        k.recip(flat(z("T2")), flat(z("T2")), [zb("T2")], [zb("T2")])
        yield
        k.tt(flat(z("KAP")), flat(z("KAP")), flat(z("T2")), ALU.mult, [zb("KAP"), zb("T2")], [zb("KAP")])
        yield
        k.tt(flat(z("BM")), flat(z("A")), flat(z("KAP")), ALU.mult, [zb("A"), zb("KAP")], [zb("BM")])
        yield
        for h in range(NH):
            k.ts(z("T1")[:, h, :], z("A")[:, h, :], VEC[:, V_KA + h:V_KA + h + 1], OMKA[:, h:h + 1], ALU.mult, ALU.add,
                 [zb("A"), bVEC, bOMKA], [zb("T1")])
            yield
        k.tt(flat(z("K2")), P[:, 4:8, :].rearrange("p a b -> p (a b)"), flat(z("T1")), ALU.mult, [bP, zb("T1")], [zb("K2")])
        yield
        for h in range(NH):
            k.stt(z("T2")[:, h, :], P[:, h, :], VEC[:, V_RK + h:V_RK + h + 1], z("K2")[:, h, :], ALU.mult, ALU.mult,
                  [bP, bVEC, zb("K2")], [zb("T2")])
            yield
        for h2 in range(NH // 2):
            ps, bps = newps()
            for hh in range(2):
                h = h2 * 2 + hh
                k.mm(ps[0:64, hh * ST:(hh + 1) * ST], onesm[0:64, 0:64], z("T2")[:, h, :], True, True, [bCST, zb("T2")], [bps])
                yield
            k.tt(z("BV")[:, h2 * 2:h2 * 2 + 2, :].rearrange("p a b -> p (a b)"), ps[0:64, 0:2 * ST],
                 P[:, 8 + h2 * 2:10 + h2 * 2, :].rearrange("p a b -> p (a b)"), ALU.mult, [bps, bP], [zb("BV")])
            yield
        k.actf(P[:, 12:16, :].rearrange("p a b -> p (a b)"), P[:, 12:16, :].rearrange("p a b -> p (a b)"), AF.Silu, [bP], [bP])
        yield
        k.tt(flat(z("RT")), P[:, 0:4, :].rearrange("p a b -> p (a b)"), flat(z("RT")), ALU.mult, [bP, zb("RT")], [zb("RT")])
        yield
        k.tt(flat(z("KPT")), flat(z("KAP")), flat(z("KPT")), ALU.mult, [zb("KAP"), zb("KPT")], [zb("KPT")])
        yield
        k.tt(flat(z("KT")), flat(z("K2")), flat(z("E2")), ALU.mult, [zb("K2"), zb("E2")], [zb("KT")])
        yield
        k.tt(flat(z("BT")), flat(z("BM")), flat(z("E2")), ALU.mult, [zb("BM"), zb("E2")], [zb("BT")])
        yield
        k.tt(flat(z("KE")), flat(z("K2")), flat(z("E3")), ALU.mult, [zb("K2"), zb("E3")], [zb("KE")])
        yield
        k.tt(flat(z("BE")), flat(z("BM")), flat(z("E3")), ALU.mult, [zb("BM"), zb("E3")], [zb("BE")])
        yield


    def scan_post(s):
        P, bP = PP[s % 2]
        t0 = s * ST
        for ck in range(nck):
            sl = slice(ck * C, (ck + 1) * C)
            for (src, sb_, dst, db, neg) in ((P, bP, VT, bVT, False), (z("KE"), zb("KE"), KET, bKET, False),
                                             (z("BE"), zb("BE"), BET, bBET, True)):
                ps, bps = newps()
                for h in range(NH):
                    inp = src[:, 8 + h, sl] if src is P else src[:, h, sl]
                    k.tr(ps[:, h * HS:(h + 1) * HS], inp, ident[0:64, 0:64], [sb_, bCST], [bps])
                dflat = dst[:, :, :].rearrange("p a b -> p (a b)")
                if neg:
                    k.ts(dflat, ps[:, 0:NH * HS], -1.0, None, ALU.mult, None, [bps], [db])
                else:
                    k.cp(dflat, ps[:, 0:NH * HS], [bps], [db], eng=k.act)

            def amat(lh, lb, rh, rb, dst, mask, mb, neg):
                ps, bps = newps()
                for h in range(NH):
                    k.mm(ps[:, h * 128:(h + 1) * 128], lh[:, h, sl], rh[:, h, sl], True, True, [lb, rb], [bps])
                d, db = mats[dst]
                dfl = d[:, :, :].rearrange("p a b -> p (a b)")
                mfl = mask[:, :, :].rearrange("p a b -> p (a b)")
                if neg:
                    k.stt(dfl, ps[:, :], -1.0, mfl, ALU.mult, ALU.mult, [bps, mb], [db])
                else:
                    k.tt(dfl, ps[:, :], mfl, ALU.mult, [bps, mb], [db])

            amat(z("BT"), zb("BT"), z("KPT"), zb("KPT"), "QT0", MSKS, bMSKS, True)
            amat(z("KPT"), zb("KPT"), z("BT"), zb("BT"), "QN0", MSKL, bMSKL, True)
            amat(z("KT"), zb("KT"), z("KPT"), zb("KPT"), "AKK", MSKS, bMSKS, False)
            amat(z("KT"), zb("KT"), z("RT"), zb("RT"), "ARK", MSKI, bMSKI, False)
            amat(z("BT"), zb("BT"), z("RT"), zb("RT"), "ARB", MSKI, bMSKI, True)
            tt_cur = "TT0"
            for h in range(NH):
                k.tt(mats["TT0"][0][:, h, :], mats["QT0"][0][:, h, :], ident, ALU.add, [mats["QT0"][1], bCST], [mats["TT0"][1]])
            qn, qt = "QN0", "QT0"
            nlev = 6
            for lv in range(nlev):
                qn2 = "QN1" if qn == "QN0" else "QN0"
                qt2 = "QT1" if qt == "QT0" else "QT0"
                tt2 = "TT1" if tt_cur == "TT0" else "TT0"
                ps, bps = newps()
                for h in range(NH):
                    k.mm(ps[:, h * 128:(h + 1) * 128], mats[qt][0][:, h, :], mats[qn][0][:, h, :], True, True,
                         [mats[qt][1], mats[qn][1]], [bps])
                last = lv == nlev - 1
                if not last:
                    ps2, bps2 = newps()
                    for h in range(NH):
                        k.mm(ps2[:, h * 128:(h + 1) * 128], mats[qn][0][:, h, :], mats[qt][0][:, h, :], True, True,
                             [mats[qt][1], mats[qn][1]], [bps2])
                k.cp(mats[qn2][0][:, :, :].rearrange("p a b -> p (a b)"), ps[:, :], [bps], [mats[qn2][1]], eng=k.act)
                if not last:
                    k.cp(mats[qt2][0][:, :, :].rearrange("p a b -> p (a b)"), ps2[:, :], [bps2], [mats[qt2][1]])
                ps3, bps3 = newps()
                for h in range(NH):
                    k.mm(ps3[:, h * 128:(h + 1) * 128], mats[qn2][0][:, h, :], mats[tt_cur][0][:, h, :], True, True,
                         [mats[qn2][1], mats[tt_cur][1]], [bps3])
                k.tt(mats[tt2][0][:, :, :].rearrange("p a b -> p (a b)"), ps3[:, :],
                     mats[tt_cur][0][:, :, :].rearrange("p a b -> p (a b)"), ALU.add, [bps3, mats[tt_cur][1]], [mats[tt2][1]])
                qn, qt, tt_cur = qn2, qt2, tt2
            TT, bTT = mats[tt_cur]
            AKK, bAKK = mats["AKK"]
            ARK, bARK = mats["ARK"]
            ARB, bARB = mats["ARB"]
            ps, bps = newps()
            for h in range(NH):
                k.mm(ps[:, h * HS:(h + 1) * HS], z("KPT")[:, h, sl], H[:, h, :], True, False, [zb("KPT"), bH], [bps])
                k.mm(ps[:, h * HS:(h + 1) * HS], AKK[:, h, :], VT[:, h, :], False, True, [bAKK, bVT], [bps])
            k.cp(RSB[:, :, :].rearrange("p a b -> p (a b)"), ps[:, 0:NH * HS], [bps], [bRSB], eng=k.act)
            ps, bps = newps()
            for h in range(NH):
                k.mm(ps[:, h * HS:(h + 1) * HS], TT[:, h, :], RSB[:, h, :], True, True, [bTT, bRSB], [bps])
            k.cp(USB[:, :, :].rearrange("p a b -> p (a b)"), ps[:, 0:NH * HS], [bps], [bUSB], eng=k.act)
            psy, bpsy = newps()
            for h in range(NH):
                k.mm(psy[0:64, h * 128:(h + 1) * 128], H[:, h, :], z("RT")[:, h, sl], True, False, [bH, zb("RT")], [bpsy])
                k.mm(psy[0:64, h * 128:(h + 1) * 128], VT[:, h, :], ARK[:, h, :], False, False, [bVT, bARK], [bpsy])
                k.mm(psy[0:64, h * 128:(h + 1) * 128], USB[:, h, :], ARB[:, h, :], False, True, [bUSB, bARB], [bpsy])
            for h in range(NH):
                k.cp(z("YT")[:, h, sl], psy[0:64, h * 128:(h + 1) * 128], [bpsy], [zb("YT")], eng=k.act)
            pss, bpss = newps()
            for h in range(NH):
                k.mm(pss[0:64, h * HS:(h + 1) * HS], KET[:, h, :], VT[:, h, :], True, False, [bKET, bVT], [bpss])
                k.mm(pss[0:64, h * HS:(h + 1) * HS], BET[:, h, :], USB[:, h, :], False, True, [bBET, bUSB], [bpss])
            for h in range(NH):
                k.stt(H[:, h, :], H[:, h, :], WC[:, h, ck:ck + 1], pss[0:64, h * HS:(h + 1) * HS], ALU.mult, ALU.add,
                      [bH, bWC, bpss], [bH])

        for h2 in range(NH // 2):
            ps, bps = newps()
            for hh in range(2):
                h = h2 * 2 + hh
                k.mm(ps[0:64, hh * ST:(hh + 1) * ST], onesm[0:64, 0:64], z("YT")[:, h, :], True, True, [bCST, zb("YT")], [bps])
            k.stt(z("YC")[:, h2 * 2:h2 * 2 + 2, :].rearrange("p a b -> p (a b)"), ps[0:64, 0:2 * ST], -1.0 / HS,
                  z("YT")[:, h2 * 2:h2 * 2 + 2, :].rearrange("p a b -> p (a b)"), ALU.mult, ALU.add, [bps, zb("YT")], [zb("YC")])
        k.tt(flat(z("T1")), flat(z("YC")), flat(z("YC")), ALU.mult, [zb("YC")], [zb("T1")])
        for h2 in range(NH // 2):
            ps, bps = newps()
            for hh in range(2):
                h = h2 * 2 + hh
                k.mm(ps[0:64, hh * ST:(hh + 1) * ST], onesm[0:64, 0:64], z("T1")[:, h, :], True, True, [bCST, zb("T1")], [bps])
            k.actf(z("T2")[:, h2 * 2:h2 * 2 + 2, :].rearrange("p a b -> p (a b)"), ps[0:64, 0:2 * ST], AF.Sqrt, [bps], [zb("T2")],
                   bias=GN_EPS, scale=1.0 / HS)
        k.recip(flat(z("T2")), flat(z("T2")), [zb("T2")], [zb("T2")])
        k.tt(flat(z("YC")), flat(z("YC")), flat(z("T2")), ALU.mult, [zb("YC"), zb("T2")], [zb("YC")])
        for h in range(NH):
            k.ts(z("YC")[:, h, :], z("YC")[:, h, :], VEC[:, V_LW + h:V_LW + h + 1], VEC[:, V_LB + h:V_LB + h + 1], ALU.mult, ALU.add,
                 [zb("YC"), bVEC], [zb("YC")])
        k.tt(flat(z("YC")), flat(z("YC")), flat(z("BV")), ALU.add, [zb("YC"), zb("BV")], [zb("YC")])
        k.tt(flat(z("YT")), flat(z("YC")), P[:, 12:16, :].rearrange("p a b -> p (a b)"), ALU.mult, [zb("YC"), bP], [zb("YT")])
        if ypair is None:
            k.dma(k.sp, yTd[:, :, t0:t0 + ST], z("YT")[:, :, :], R=[zb("YT")], key=zb("YT"))
        else:
            for h in range(NH):
                k.dma(k.sp, ypair[64 * (h % 2):64 * (h % 2) + 64, 2 * hg + h // 2, t0:t0 + ST], z("YT")[:, h, :],
                      R=[zb("YT")], key=zb("YT"))

    def run(gen, n):
        for _ in range(n):
            if next(gen, "done") == "done":
                return

    for _ in inproj(0):
        pass
    for s in range(nst):
        pg = prep(s)
        if s + 1 < nst:
            for _ in inproj(s + 1):
                run(pg, 8)
        for _ in pg:
            pass
        scan_post(s)
    k.end_stage()
def rwkv_main_inputs(x_b, w_in, mu, w0, wdu, a0, wiu, k_k, k_a, r_k, ln_w, ln_b, gpre, hg):
    xT = None if x_b is None else feat_major(x_b)
    ch = np.arange(hg * NH * HS, (hg + 1) * NH * HS)
    cols = np.concatenate([ch, D + ch, 2 * D + ch, 3 * D + ch, 4 * D + np.arange(128)])
    Wc = w_in[:, cols]
    W = np.ascontiguousarray(Wc.reshape(8, 128, NB * 64).transpose(1, 0, 2))
    vec = np.zeros((64, NVEC), np.float32)
    vec[:, V_MU:V_MU + NB] = mu[cols].reshape(NB, 64).T
    for nm, arr in ((V_W0, w0), (V_A0, a0), (V_KK, k_k), (V_KA, k_a), (V_RK, r_k.reshape(-1)), (V_LW, ln_w), (V_LB, ln_b)):
        vec[:, nm:nm + NH] = arr[ch].reshape(NH, 64).T
    gp = np.ascontiguousarray(gpre.reshape(8, 128).T)
    cst = np.zeros((128, 4, 128), np.float32)
    cst[:, 0, :] = np.eye(128)
    cst[:, 1, :] = 1.0
    cst[:, 2, :] = np.triu(np.ones((128, 128)))
    cst[:, 3, :] = np.triu(np.ones((128, 128)), 1)
    return {"xT": xT, "W": W, "vec": vec, "gpre": gp, "wdu": np.ascontiguousarray(wdu[:, ch]),
            "wiu": np.ascontiguousarray(wiu[:, ch]), "cst": cst}


GDK, GDV = 128, 256
GW = 2 * GDK + 2 * GDV + 16


def build_gla_main(T):
    k = KB()
    nc = k.nc
    xT = nc.dram_tensor("xT", [128, 8, T], F32, kind="ExternalInput").ap()
    Wd = nc.dram_tensor("W", [128, 8, GW], F32, kind="ExternalInput").ap()
    vecd = nc.dram_tensor("vec", [128, 3], F32, kind="ExternalInput").ap()
    gpd = nc.dram_tensor("gpre", [128, 8], F32, kind="ExternalInput").ap()
    wupd = nc.dram_tensor("wup", [16, GDK], F32, kind="ExternalInput").ap()
    cstd = nc.dram_tensor("cst", [128, 4, 128], F32, kind="ExternalInput").ap()
    yTd = nc.dram_tensor("yT", [128, 2, T], F32, kind="ExternalOutput").ap()
    emit_gla_main(k, T, xT, Wd, vecd, gpd, wupd, cstd, yTd)
    k.finish()
    return k


def emit_gla_main(k, T, xT, Wd, vecd, gpd, wupd, cstd, yTd):
    nc = k.nc
    nst = T // ST
    nck = ST // C
    k.begin_stage()
    XT, bXT = k.sb("XT", [128, 8, ST])
    SQ, bSQ = k.sb("SQ", [128, 8, ST])
    HT, bHT = k.sb("HT", [128, 8, ST])
    RS, bRS = k.sb("RS", [128, ST])
    W, bW = k.sb("Wsb", [128, 8, GW])
    VEC, bVEC = k.sb("VEC", [128, 3])
    GP, bGP = k.sb("GP", [128, 8])
    WUP, bWUP = k.sb("WUP", [16, GDK])
    CST, bCST = k.sb("CST", [128, 4, 128])
    ONES, bONES = k.sb("ONESr", [128, C])
    P, bP = k.sb("P", [128, 6, ST])
    AD, bAD = k.sb("AD", [16, ST])
    Z = {}
    for n in ["LA", "CS", "E1", "E2", "E3", "RT", "KT", "KE", "RSTD"]:
        Z[n] = k.sb(n, [128, ST])
    YT, bYT = k.sb("YT", [128, 2, ST])
    Y2, bY2 = k.sb("Y2", [128, 2, ST])
    SCE, bSCE = k.sb("SCE", [128, nck])
    WC, bWC = k.sb("WC", [128, nck])
    H, bH = k.sb("H", [128, GDV])
    VT, bVT = k.sb("VT", [128, GDV])
    KET, bKET = k.sb("KET", [128, GDK])
    ARK, bARK = k.sb("ARK", [128, 128])
    newps = k.newps

    ident = CST[:, 0, :]
    onesm = CST[:, 1, :]
    for kc in range(8):
        k.dma(k.sp, W[:, kc, :], Wd[:, kc, :], W=[bW], key=bW)
    k.dma(k.sp, VEC[:, :], vecd[:, :], W=[bVEC], key=bVEC)
    k.dma(k.sp, GP[:, :], gpd[:, :], W=[bGP], key=bGP)
    k.dma(k.sp, WUP[:, :], wupd[:, :], W=[bWUP], key=bWUP)
    k.dma(k.sp, CST[:, :, :], cstd[:, :, :], W=[bCST], key=bCST)
    k.op(k.dve, lambda: nc.vector.memset(ONES[:, :], 1.0), [], [bONES])
    k.op(k.dve, lambda: nc.vector.memset(H[:, :], 0.0), [], [bH])

    def z(n):
        return Z[n][0]

    def zb(n):
        return Z[n][1]

    def flat(t):
        return t[:, :, :].rearrange("p a b -> p (a b)")

    for s in range(nst):
        t0 = s * ST
        k.dma(k.sp, XT[:, :, :], xT[:, :, t0:t0 + ST], W=[bXT], key=bXT)
        k.tt(flat(SQ), flat(XT), flat(XT), ALU.mult, [bXT], [bSQ])
        ps, bps = newps()
        for kc in range(8):
            k.mm(ps[:, 0:ST], onesm, SQ[:, kc, :], kc == 0, kc == 7, [bSQ, bCST], [bps])
        k.actf(RS[:, :], ps[:, 0:ST], AF.Sqrt, [bps], [bRS], bias=RMS_EPS, scale=1.0 / D)
        k.recip(RS[:, :], RS[:, :], [bRS], [bRS])
        for kc in range(8):
            k.stt(HT[:, kc, :], XT[:, kc, :], GP[:, kc:kc + 1], RS[:, :], ALU.mult, ALU.mult, [bXT, bGP, bRS], [bHT])
        for j in range(6):
            ps, bps = newps()
            for kc in range(8):
                k.mm(ps[:, 0:ST], W[:, kc, j * 128:(j + 1) * 128], HT[:, kc, :], kc == 0, kc == 7, [bW, bHT], [bps])
            k.cp(P[:, j, :], ps[:, 0:ST], [bps], [bP], eng=(k.act if j % 2 else k.dve))
        ps, bps = newps()
        for kc in range(8):
            k.mm(ps[0:16, 0:ST], W[:, kc, 768:784], HT[:, kc, :], kc == 0, kc == 7, [bW, bHT], [bps])
        k.cp(AD[:, :], ps[0:16, 0:ST], [bps], [bAD])
        ps, bps = newps()
        k.mm(ps[:, 0:ST], WUP[:, :], AD[:, :], True, True, [bWUP, bAD], [bps])
        k.actf(z("LA")[:, :], ps[:, 0:ST], AF.Sigmoid, [bps, bVEC], [zb("LA")], bias=VEC[:, 0:1])
        k.actf(z("LA")[:, :], z("LA")[:, :], AF.Ln, [zb("LA")], [zb("LA")])
        for ck in range(nck):
            sl = slice(ck * C, (ck + 1) * C)
            k.op(k.dve, lambda sl=sl: nc.vector.tensor_tensor_scan(z("CS")[:, sl], ONES[:, :], z("LA")[:, sl], 0.0, ALU.mult, ALU.add),
                 [bONES, zb("LA")], [zb("CS")])
        k.actf(z("E1")[:, :], z("CS")[:, :], AF.Exp, [zb("CS")], [zb("E1")], scale=1.0 / 16)
        k.actf(z("E2")[:, :], z("CS")[:, :], AF.Exp, [zb("CS")], [zb("E2")], scale=-1.0 / 16)
        for ck in range(nck):
            e = (ck + 1) * C - 1
            k.ts(SCE[:, ck:ck + 1], z("CS")[:, e:e + 1], 1.0 / 16, None, ALU.mult, None, [zb("CS")], [bSCE])
        k.actf(WC[:, :], SCE[:, :], AF.Exp, [bSCE], [bWC])
        for ck in range(nck):
            sl = slice(ck * C, (ck + 1) * C)
            k.actf(z("E3")[:, sl], z("CS")[:, sl], AF.Exp, [zb("CS"), bSCE], [zb("E3")], bias=SCE[:, ck:ck + 1], scale=-1.0 / 16)
        k.stt(z("RT")[:, :], P[:, 0, :], float(GDK ** -0.5), z("E1")[:, :], ALU.mult, ALU.mult, [bP, zb("E1")], [zb("RT")])
        k.tt(z("KT")[:, :], P[:, 1, :], z("E2")[:, :], ALU.mult, [bP, zb("E2")], [zb("KT")])
        k.tt(z("KE")[:, :], P[:, 1, :], z("E3")[:, :], ALU.mult, [bP, zb("E3")], [zb("KE")])
        k.actf(P[:, 4:6, :].rearrange("p a b -> p (a b)"), P[:, 4:6, :].rearrange("p a b -> p (a b)"), AF.Silu, [bP], [bP])
        for ck in range(nck):
            sl = slice(ck * C, (ck + 1) * C)
            ps, bps = newps()
            for vb in range(2):
                k.tr(ps[:, vb * 128:(vb + 1) * 128], P[:, 2 + vb, sl], ident, [bP, bCST], [bps])
            k.cp(VT[:, :], ps[:, 0:GDV], [bps], [bVT], eng=k.act)
            ps, bps = newps()
            k.tr(ps[:, 0:128], z("KE")[:, sl], ident, [zb("KE"), bCST], [bps])
            k.cp(KET[:, :], ps[:, 0:128], [bps], [bKET])
            ps, bps = newps()
            k.mm(ps[:, 0:128], z("KT")[:, sl], z("RT")[:, sl], True, True, [zb("KT"), zb("RT")], [bps])
            k.tt(ARK[:, :], ps[:, 0:128], CST[:, 2, :], ALU.mult, [bps, bCST], [bARK])
            psy, bpsy = newps()
            for vb in range(2):
                k.mm(psy[:, vb * 128:(vb + 1) * 128], H[:, vb * 128:(vb + 1) * 128], z("RT")[:, sl], True, False, [bH, zb("RT")], [bpsy])
                k.mm(psy[:, vb * 128:(vb + 1) * 128], VT[:, vb * 128:(vb + 1) * 128], ARK[:, :], False, True, [bVT, bARK], [bpsy])
            for vb in range(2):
                k.cp(YT[:, vb, sl], psy[:, vb * 128:(vb + 1) * 128], [bpsy], [bYT], eng=k.act)
            pss, bpss = newps()
            k.mm(pss[:, 0:GDV], KET[:, :], VT[:, :], True, True, [bKET, bVT], [bpss])
            k.stt(H[:, :], H[:, :], WC[:, ck:ck + 1], pss[:, 0:GDV], ALU.mult, ALU.add, [bH, bWC, bpss], [bH])
        k.tt(flat(Y2), flat(YT), flat(YT), ALU.mult, [bYT], [bY2])
        ps, bps = newps()
        for vb in range(2):
            k.mm(ps[:, 0:ST], onesm, Y2[:, vb, :], vb == 0, vb == 1, [bCST, bY2], [bps])
        k.actf(z("RSTD")[:, :], ps[:, 0:ST], AF.Sqrt, [bps], [zb("RSTD")], bias=RMS_EPS, scale=1.0 / GDV)
        k.recip(z("RSTD")[:, :], z("RSTD")[:, :], [zb("RSTD")], [zb("RSTD")])
        for vb in range(2):
            k.stt(Y2[:, vb, :], YT[:, vb, :], VEC[:, 1 + vb:2 + vb], z("RSTD")[:, :], ALU.mult, ALU.mult, [bYT, bVEC, zb("RSTD")], [bY2])
        k.tt(flat(YT), flat(Y2), P[:, 4:6, :].rearrange("p a b -> p (a b)"), ALU.mult, [bY2, bP], [bYT])
        k.dma(k.sp, yTd[:, :, t0:t0 + ST], YT[:, :, :], R=[bYT], key=bYT)
    k.end_stage()


def make_cst():
    cst = np.zeros((128, 4, 128), np.float32)
    cst[:, 0, :] = np.eye(128)
    cst[:, 1, :] = 1.0
    cst[:, 2, :] = np.triu(np.ones((128, 128)))
    cst[:, 3, :] = np.triu(np.ones((128, 128)), 1)
    return cst


def feat_major(x_b):
    T = x_b.shape[0]
    return np.ascontiguousarray(x_b.T.reshape(8, 128, T).transpose(1, 0, 2))


def gla_main_inputs(xT, w_in, wup, b_alpha, g_head, gpre, h):
    cols = np.concatenate([h * GDK + np.arange(GDK), 512 + h * GDK + np.arange(GDK), 1024 + h * GDV + np.arange(GDV),
                           2048 + h * GDV + np.arange(GDV), 3072 + np.arange(16)])
    W = np.ascontiguousarray(w_in[:, cols].reshape(8, 128, GW).transpose(1, 0, 2))
    vec = np.zeros((128, 3), np.float32)
    vec[:, 0] = b_alpha[h * GDK:(h + 1) * GDK]
    vec[:, 1] = g_head[0:128]
    vec[:, 2] = g_head[128:256]
    return {"xT": xT, "W": W, "vec": vec, "gpre": np.ascontiguousarray(gpre.reshape(8, 128).T),
            "wup": np.ascontiguousarray(wup[:, h * GDK:(h + 1) * GDK]), "cst": make_cst()}


def build_out(Tc, KP, NKC):
    k = KB()
    nc = k.nc
    yTd = nc.dram_tensor("yT", [KP, NKC, Tc], F32, kind="ExternalInput").ap()
    Wd = nc.dram_tensor("W", [KP, NKC, D], F32, kind="ExternalInput").ap()
    xd = nc.dram_tensor("x", [Tc, D], F32, kind="ExternalInput").ap()
    gd = nc.dram_tensor("gpost", [128, D], F32, kind="ExternalInput").ap()
    od = nc.dram_tensor("out", [Tc, D], F32, kind="ExternalOutput").ap()
    W, bW = k.sb("Wsb", [KP, NKC, D])
    G, bG = k.sb("G", [128, D])
    Y, bY = k.sb("Y", [KP, NKC, 128])
    X, bX = k.sb("X", [128, D])
    YO, bYO = k.sb("YO", [128, D])
    SQ, bSQ = k.sb("SQ", [128, D])
    SS, bSS = k.sb("SS", [128, 1])
    O, bO = k.sb("O", [128, D])
    newps = k.newps

    for kc in range(NKC):
        k.dma(k.sp, W[:, kc, :], Wd[:, kc, :], W=[bW], key=bW)
    k.dma(k.sp, G[:, :], gd[:, :], W=[bG], key=bG)
    for t in range(Tc // 128):
        sl = slice(t * 128, (t + 1) * 128)
        k.dma(k.sp, Y[:, :, :], yTd[:, :, sl], W=[bY], key=bY)
        k.dma(k.sp, X[:, :], xd[sl, :], W=[bX], key=bX)
        for hf in range(2):
            ps, bps = newps()
            for kc in range(NKC):
                k.mm(ps[:, :], Y[:, kc, :], W[:, kc, hf * 512:(hf + 1) * 512], kc == 0, kc == NKC - 1, [bY, bW], [bps])
            k.cp(YO[:, hf * 512:(hf + 1) * 512], ps[:, :], [bps], [bYO], eng=(k.act if hf else k.dve))
        k.tt(SQ[:, :], YO[:, :], YO[:, :], ALU.mult, [bYO], [bSQ])
        k.op(k.dve, lambda: nc.vector.reduce_sum(SS[:, :], SQ[:, :], axis=mybir.AxisListType.X), [bSQ], [bSS])
        k.actf(SS[:, :], SS[:, :], AF.Sqrt, [bSS], [bSS], bias=RMS_EPS, scale=1.0 / D)
        k.recip(SS[:, :], SS[:, :], [bSS], [bSS])
        k.stt(O[:, :], YO[:, :], SS[:, 0:1], G[:, :], ALU.mult, ALU.mult, [bYO, bSS, bG], [bO])
        k.tt(O[:, :], O[:, :], X[:, :], ALU.add, [bO, bX], [bO])
        k.dma(k.sp, od[sl, :], O[:, :], R=[bO], key=bO)
    k.finish([bO])
    return k


def run_out(yT_b, w_out, x, gpost, KP, NKC):
    B, T = x.shape[0], x.shape[1]
    Tc = T // 4
    kb = build_out(Tc, KP, NKC)
    W = np.ascontiguousarray(w_out.reshape(NKC, KP, D).transpose(1, 0, 2))
    g = np.ascontiguousarray(np.broadcast_to(gpost[None, :], (128, D)))
    maps = []
    for c in range(8):
        b, sg = c // 4, c % 4
        maps.append({"yT": np.ascontiguousarray(yT_b[b][:, :, sg * Tc:(sg + 1) * Tc]), "W": W,
                     "x": np.ascontiguousarray(x[b, sg * Tc:(sg + 1) * Tc]), "gpost": g})
    res = run_bass_kernel_spmd(kb.nc, maps, core_ids=list(range(8)))
    out = np.empty((B, T, D), np.float32)
    for c in range(8):
        b, sg = c // 4, c % 4
        out[b, sg * Tc:(sg + 1) * Tc] = res.results[c]["out"]
    return out


def gla_layer(x, p):
    B, T = x.shape[0], x.shape[1]
    kb = build_gla_main(T)
    xTs = [feat_major(x[b]) for b in range(B)]
    maps = []
    for c in range(8):
        b, h = c // 4, c % 4
        maps.append(gla_main_inputs(xTs[b], p["gla_w_in"][0], p["gla_w_alpha_up"][0], p["gla_b_alpha"][0],
                                    p["gla_head_norm"][0], p["gla_pre_norm"][0], h))
    res = run_bass_kernel_spmd(kb.nc, maps, core_ids=list(range(8)))
    yT = [np.empty((128, 8, T), np.float32) for _ in range(B)]
    for c in range(8):
        b, h = c // 4, c % 4
        yT[b][:, 2 * h:2 * h + 2, :] = res.results[c]["yT"]
    return run_out(yT, p["gla_w_out"][0], x, p["gla_post_norm"][0], 128, 8)


def rwkv_layer(x, p):
    B, T = x.shape[0], x.shape[1]
    kb = build_rwkv_main(T)
    maps = []
    for c in range(8):
        b, hg = c // 4, c % 4
        maps.append(rwkv_main_inputs(x[b], p["rwkv_w_in"][0], p["rwkv_mu"][0], p["rwkv_w0"][0], p["rwkv_w_decay_up"][0],
                                     p["rwkv_a0"][0], p["rwkv_w_iclr_up"][0], p["rwkv_k_k"][0], p["rwkv_k_a"][0],
                                     p["rwkv_r_k"][0], p["rwkv_ln_w"][0], p["rwkv_ln_b"][0], p["rwkv_pre_norm"][0], hg))
    res = run_bass_kernel_spmd(kb.nc, maps, core_ids=list(range(8)))
    yT = [np.empty((64, 16, T), np.float32) for _ in range(B)]
    for c in range(8):
        b, hg = c // 4, c % 4
        yT[b][:, 4 * hg:4 * hg + 4, :] = res.results[c]["yT"]
    return run_out(yT, p["rwkv_w_out"][0], x, p["rwkv_post_norm"][0], 64, 16)


def emit_outproj_fm(k, T, KP, NKC, Yd, Wd, gpd, Xd, Od, cstd):
    nc = k.nc
    k.begin_stage()
    W, bW = k.sb("Wo", [KP, NKC, D])
    Y, bY = k.sb("Yo", [KP, NKC, ST])
    XT, bXT = k.sb("XTo", [128, 8, ST])
    YO, bYO = k.sb("YOo", [128, 8, ST])
    SQ, bSQ = k.sb("SQo", [128, 8, ST])
    RS, bRS = k.sb("RSo", [128, ST])
    O, bO = k.sb("Oo", [128, 8, ST])
    GP, bGP = k.sb("GPo", [128, 8])
    CST, bCST = k.sb("CSTo", [128, 4, 128])
    onesm = CST[:, 1, :]
    for kc in range(NKC):
        k.dma(k.sp, W[:, kc, :], Wd[:, kc, :], W=[bW], key=bW)
    k.dma(k.sp, GP[:, :], gpd[:, :], W=[bGP], key=bGP)
    k.dma(k.sp, CST[:, :, :], cstd[:, :, :], W=[bCST], key=bCST)

    def flat(t):
        return t[:, :, :].rearrange("p a b -> p (a b)")

    for s in range(T // ST):
        t0 = s * ST
        k.dma(k.sp, Y[:, :, :], Yd[:, :, t0:t0 + ST], W=[bY], key=bY)
        k.dma(k.sp, XT[:, :, :], Xd[:, :, t0:t0 + ST], W=[bXT], key=bXT)
        for fb in range(8):
            ps, bps = k.newps()
            for kc in range(NKC):
                k.mm(ps[:, 0:ST], W[:, kc, fb * 128:(fb + 1) * 128], Y[:, kc, :], kc == 0, kc == NKC - 1, [bW, bY], [bps])
            k.cp(YO[:, fb, :], ps[:, 0:ST], [bps], [bYO], eng=(k.act if fb % 2 else k.dve))
        k.tt(flat(SQ), flat(YO), flat(YO), ALU.mult, [bYO], [bSQ])
        ps, bps = k.newps()
        for fb in range(8):
            k.mm(ps[:, 0:ST], onesm, SQ[:, fb, :], fb == 0, fb == 7, [bSQ, bCST], [bps])
        k.actf(RS[:, :], ps[:, 0:ST], AF.Sqrt, [bps], [bRS], bias=RMS_EPS, scale=1.0 / D)
        k.recip(RS[:, :], RS[:, :], [bRS], [bRS])
        for fb in range(8):
            k.stt(O[:, fb, :], YO[:, fb, :], GP[:, fb:fb + 1], RS[:, :], ALU.mult, ALU.mult, [bYO, bGP, bRS], [bO])
        k.tt(flat(O), flat(O), flat(XT), ALU.add, [bO, bXT], [bO])
        k.dma(k.sp, Od[:, :, t0:t0 + ST], O[:, :, :], R=[bO], key=bO)
    k.end_stage()


def build_fused(T):
    k = KB()
    nc = k.nc

    def din(name, shape):
        return nc.dram_tensor(name, list(shape), F32, kind="ExternalInput").ap()

    xT = din("xT", [128, 8, T])
    cst = din("cst", [128, 4, 128])
    gW = din("gW", [4, 128, 8, GW])
    gvec = din("gvec", [4, 128, 3])
    gwup = din("gwup", [4, 16, GDK])
    ggpre = din("ggpre", [128, 8])
    gWo = din("gWo", [128, 8, D])
    ggpost = din("ggpost", [128, 8])
    rW = din("rW", [4, 128, 8, NB * 64])
    rvec = din("rvec", [4, 64, NVEC])
    rwdu = din("rwdu", [4, 64, NH * 64])
    rwiu = din("rwiu", [4, 64, NH * 64])
    rgpre = din("rgpre", [128, 8])
    rWo = din("rWo", [128, 8, D])
    rgpost = din("rgpost", [128, 8])
    outT = nc.dram_tensor("outT", [128, 8, T], F32, kind="ExternalOutput").ap()
    Y1 = nc.dram_tensor("Y1s", [128, 8, T], F32).ap()
    X1 = nc.dram_tensor("X1s", [128, 8, T], F32).ap()
    Y2 = nc.dram_tensor("Y2s", [128, 8, T], F32).ap()
    for h in range(4):
        emit_gla_main(k, T, xT, gW[h], gvec[h], ggpre, gwup[h], cst, Y1[:, 2 * h:2 * h + 2, :])
    emit_outproj_fm(k, T, 128, 8, Y1, gWo, ggpost, xT, X1, cst)
    for hg in range(4):
        emit_rwkv_main(k, T, X1, rW[hg], rvec[hg], rgpre, rwdu[hg], rwiu[hg], cst, None, ypair=Y2, hg=hg)
    emit_outproj_fm(k, T, 128, 8, Y2, rWo, rgpost, X1, outT, cst)
    k.finish()
    return k


def fused_inputs(x_b, p):
    g = [gla_main_inputs(None, p["gla_w_in"][0], p["gla_w_alpha_up"][0], p["gla_b_alpha"][0], p["gla_head_norm"][0],
                         p["gla_pre_norm"][0], h) for h in range(4)]
    r = [rwkv_main_inputs(None, p["rwkv_w_in"][0], p["rwkv_mu"][0], p["rwkv_w0"][0], p["rwkv_w_decay_up"][0],
                          p["rwkv_a0"][0], p["rwkv_w_iclr_up"][0], p["rwkv_k_k"][0], p["rwkv_k_a"][0],
                          p["rwkv_r_k"][0], p["rwkv_ln_w"][0], p["rwkv_ln_b"][0], p["rwkv_pre_norm"][0], hg) for hg in range(4)]
    fm = lambda v: np.ascontiguousarray(v.reshape(8, 128).T)
    return {
        "xT": feat_major(x_b), "cst": make_cst(),
        "gW": np.stack([a["W"] for a in g]), "gvec": np.stack([a["vec"] for a in g]),
        "gwup": np.stack([a["wup"] for a in g]), "ggpre": g[0]["gpre"],
        "gWo": np.ascontiguousarray(p["gla_w_out"][0].reshape(8, 128, D).transpose(1, 0, 2)), "ggpost": fm(p["gla_post_norm"][0]),
        "rW": np.stack([a["W"] for a in r]), "rvec": np.stack([a["vec"] for a in r]),
        "rwdu": np.stack([a["wdu"] for a in r]), "rwiu": np.stack([a["wiu"] for a in r]), "rgpre": r[0]["gpre"],
        "rWo": np.ascontiguousarray(p["rwkv_w_out"][0].reshape(8, 128, D).transpose(1, 0, 2)), "rgpost": fm(p["rwkv_post_norm"][0]),
    }


def kernel_unfused(**inputs):
    p = {k_: np.asarray(v, dtype=np.float32) for k_, v in inputs.items()}
    x = p["x"]
    x = gla_layer(x, p)
    x = rwkv_layer(x, p)
    return x


def kernel(**inputs):
    p = {k_: np.asarray(v, dtype=np.float32) for k_, v in inputs.items()}
    x = p["x"]
    B, T = x.shape[0], x.shape[1]
    kb = build_fused(T)
    per_b = [fused_inputs(x[b], p) for b in range(B)]
    maps = [per_b[c % B] for c in range(8)]
    res = run_bass_kernel_spmd(kb.nc, maps, core_ids=list(range(8)))
    out = np.empty((B, T, D), np.float32)
    for b in range(B):
        oT = res.results[b]["outT"]
        out[b] = oT.transpose(2, 1, 0).reshape(T, D)
    return out
```

```python
import math
import numpy as np
import concourse.bass as bass
import concourse.mybir as mybir
from concourse.bass_utils import run_bass_kernel_spmd

F32 = mybir.dt.float32
AF = mybir.ActivationFunctionType
ALU = mybir.AluOpType

D = 1024
C = 128
ST = 256
RMS_EPS = 1e-6
GN_EPS = 64e-5
S_DEC = -math.exp(-0.5)


class Buf:
    def __init__(self, name):
        self.name = name
        self.w = None
        self.r = []
        self.dsem = None
        self.dcnt = 0


class Eng:
    def __init__(self, k, e, name):
        self.k, self.e, self.name = k, e, name
        self.sem = k.new_sem(name)
        self.cnt = 0
        self.seen = {}

    def wait(self, ev):
        if ev is None:
            return
        sem, val = ev
        if self.seen.get(id(sem), 0) >= val:
            return
        if sem is self.sem and self.name == "pe":
            return
        self.e.wait_ge(sem, val)
        self.seen[id(sem)] = val


class KB:
    def __init__(self):
        self.nc = bass.Bass("TRN2", target_bir_lowering=False)
        self.stack = []
        nc = self.nc
        self.pe = Eng(self, nc.tensor, "pe")
        self.act = Eng(self, nc.scalar, "act")
        self.dve = Eng(self, nc.vector, "dve")
        self.pool = Eng(self, nc.gpsimd, "pool")
        self.sp = Eng(self, nc.sync, "sp")
        self.nbuf = 0
        self.ninst = 0
        self.guards = []
        self.marks = []
        self.dma_bufs = []
        self.psb = None
        self.psi = 0

    def new_sem(self, name):
        cm = self.nc.semaphore(name)
        s = cm.__enter__()
        self.stack.append(cm)
        return s

    def close(self):
        for cm in reversed(self.stack):
            cm.__exit__(None, None, None)
        self.stack = []

    def sb(self, name, shape, dtype=F32):
        self.nbuf += 1
        g = self.nc.sbuf_tensor("%s_%d" % (name, self.nbuf), list(shape), dtype)
        t = g.__enter__()
        self.guards.append(g)
        return t, Buf(name)

    def begin_stage(self):
        self.marks.append(len(self.guards))

    def end_stage(self):
        self.barrier()
        m = self.marks.pop()
        while len(self.guards) > m:
            self.guards.pop().__exit__(None, None, None)

    def barrier(self):
        engs = [self.pe, self.act, self.dve, self.sp]
        for e in engs:
            for f in engs:
                if f is not e and f.cnt > 0:
                    e.wait((f.sem, f.cnt))
            for b in self.dma_bufs:
                e.wait((b.dsem, b.dcnt))

    def psum_banks(self):
        if self.psb is None:
            self.psb = [self.ps("ps%d" % i, [128, 512]) for i in range(8)]
        return self.psb

    def newps(self):
        r = self.psum_banks()[self.psi % 8]
        self.psi += 1
        return r

    def ps(self, name, shape):
        t = self.nc.alloc_psum_tensor(name, list(shape), F32)
        return t, Buf(name)

    def _deps(self, eng, R, W):
        for b in R:
            eng.wait(b.w)
        for b in W:
            eng.wait(b.w)
            for ev in b.r:
                eng.wait(ev)

    def op(self, eng, fn, R=(), W=()):
        self._deps(eng, R, W)
        ins = fn()
        eng.cnt += 1
        ins.then_inc(eng.sem, 1)
        ev = (eng.sem, eng.cnt)
        eng.seen[id(eng.sem)] = max(eng.seen.get(id(eng.sem), 0), 0)
        for b in R:
            b.r.append(ev)
        for b in W:
            b.w = ev
            b.r = []
        self.ninst += 1
        return ins

    def dma(self, eng, out, in_, R=(), W=(), key=None):
        self._deps(eng, R, W)
        if key.dsem is None:
            self.nbuf += 1
            key.dsem = self.new_sem("d_%s_%d" % (key.name, self.nbuf))
            self.dma_bufs.append(key)
        ins = eng.e.dma_start(out=out, in_=in_)
        key.dcnt += 16
        ins.then_inc(key.dsem, 16)
        ev = (key.dsem, key.dcnt)
        for b in R:
            b.r.append(ev)
        for b in W:
            b.w = ev
            b.r = []
        self.ninst += 1
        return ev

    def mm(self, out, lhsT, rhs, start, stop, R, W):
        return self.op(self.pe, lambda: self.nc.tensor.matmul(out, lhsT=lhsT, rhs=rhs, start=start, stop=stop), R, W)

    def tr(self, out, in_, ident, R, W):
        return self.op(self.pe, lambda: self.nc.tensor.transpose(out, in_, ident), R, W)

    def actf(self, out, in_, func, R, W, bias=0.0, scale=1.0):
        return self.op(self.act, lambda: self.nc.scalar.activation(out=out, in_=in_, func=func, bias=bias, scale=scale), R, W)

    def tt(self, out, a, b, op, R, W, eng=None):
        eng = eng or self.dve
        return self.op(eng, lambda: eng.e.tensor_tensor(out, a, b, op), R, W)

    def ts(self, out, a, s1, s2, op0, op1, R, W, eng=None):
        eng = eng or self.dve
        if s2 is None:
            return self.op(eng, lambda: eng.e.tensor_single_scalar(out, a, s1, op0), R, W)
        return self.op(eng, lambda: eng.e.tensor_scalar(out, a, s1, s2, op0, op1), R, W)

    def stt(self, out, a, s, b, op0, op1, R, W, eng=None):
        eng = eng or self.dve
        return self.op(eng, lambda: eng.e.scalar_tensor_tensor(out, a, s, b, op0, op1), R, W)

    def cp(self, out, in_, R, W, eng=None):
        eng = eng or self.dve
        if eng is self.act:
            return self.op(eng, lambda: self.nc.scalar.copy(out, in_), R, W)
        return self.op(eng, lambda: eng.e.tensor_copy(out, in_), R, W)

    def recip(self, out, in_, R, W):
        return self.op(self.dve, lambda: self.nc.vector.reciprocal(out, in_), R, W)

    def finish(self, out_bufs=()):
        self.barrier()
        for b in self.dma_bufs:
            self.sp.e.wait_ge(b.dsem, b.dcnt)
        while self.guards:
            self.guards.pop().__exit__(None, None, None)
        self.close()


NH = 4
HS = 64
NB = 18
V_MU, V_W0, V_A0, V_KK, V_KA, V_RK, V_LW, V_LB = 0, 18, 22, 26, 30, 34, 38, 42
NVEC = 46


def build_rwkv_main(T):
    k = KB()
    nc = k.nc
    xT = nc.dram_tensor("xT", [128, 8, T], F32, kind="ExternalInput").ap()
    Wd = nc.dram_tensor("W", [128, 8, NB * 64], F32, kind="ExternalInput").ap()
    vecd = nc.dram_tensor("vec", [64, NVEC], F32, kind="ExternalInput").ap()
    gpd = nc.dram_tensor("gpre", [128, 8], F32, kind="ExternalInput").ap()
    wdud = nc.dram_tensor("wdu", [64, NH * 64], F32, kind="ExternalInput").ap()
    wiud = nc.dram_tensor("wiu", [64, NH * 64], F32, kind="ExternalInput").ap()
    cstd = nc.dram_tensor("cst", [128, 4, 128], F32, kind="ExternalInput").ap()
    yTd = nc.dram_tensor("yT", [64, NH, T], F32, kind="ExternalOutput").ap()
    emit_rwkv_main(k, T, xT, Wd, vecd, gpd, wdud, wiud, cstd, yTd)
    k.finish()
    return k


def emit_rwkv_main(k, T, xT, Wd, vecd, gpd, wdud, wiud, cstd, yTd, ypair=None, hg=0):
    nc = k.nc
    nst = T // ST
    nck = ST // C
    k.begin_stage()
    XT, bXT = k.sb("XT", [128, 8, ST])
    SQ, bSQ = k.sb("SQHT", [128, 8, ST])
    HT, bHT = SQ, bSQ
    RS, bRS = k.sb("RS", [128, ST])
    W, bW = k.sb("Wsb", [128, 8, NB * 64])
    VEC, bVEC = k.sb("VEC", [64, NVEC])
    OMM, bOMM = k.sb("OMM", [64, NB])
    OMKA, bOMKA = k.sb("OMKA", [64, NH])
    GP, bGP = k.sb("GP", [128, 8])
    WDU, bWDU = k.sb("WDU", [64, NH * 64])
    WIU, bWIU = k.sb("WIU", [64, NH * 64])
    CST, bCST = k.sb("CST", [128, 4, 128])
    MSKI, bMSKI = k.sb("MSKI", [128, NH, 128])
    MSKS, bMSKS = k.sb("MSKS", [128, NH, 128])
    MSKL, bMSKL = k.sb("MSKL", [128, NH, 128])
    ONES, bONES = k.sb("ONESr", [64, C])
    PP = [k.sb("P0", [64, NB, ST]), k.sb("P1", [64, NB, ST])]
    TMP, bTMP = k.sb("TMP", [64, ST + 1])
    CAR, bCAR = k.sb("CAR", [64, NB])
    names = ["SG", "A", "CS", "E2", "E3", "KAP", "T1", "T2", "BM", "K2", "BV",
             "RT", "KPT", "KT", "BT", "KE", "BE", "YT", "YC"]
    Z = {}
    for n in names:
        Z[n] = k.sb(n, [64, NH, ST])
    SCE, bSCE = k.sb("SCE", [64, NH, nck])
    WC, bWC = k.sb("WC", [64, NH, nck])
    H, bH = k.sb("H", [64, NH, HS])
    VT, bVT = k.sb("VT", [128, NH, HS])
    KET, bKET = k.sb("KET", [128, NH, HS])
    BET, bBET = k.sb("BET", [128, NH, HS])
    RSB, bRSB = k.sb("RSB", [128, NH, HS])
    USB, bUSB = k.sb("USB", [128, NH, HS])
    mats = {}
    HG = NH // 2
    for n in ["QN0", "QT0", "QN1", "QT1", "TT0", "TT1", "AKK", "ARK", "ARB"]:
        t_, _b = k.sb(n, [128, NH, 128])
        mats[n] = (t_, [Buf(n + "a"), Buf(n + "b")])
    bHh = [Buf("Ha"), Buf("Hb")]
    bRSBh = [Buf("RSBa"), Buf("RSBb")]
    bUSBh = [Buf("USBa"), Buf("USBb")]
    newps = k.newps

    ident = CST[:, 0, :]
    onesm = CST[:, 1, :]

    for kc in range(8):
        k.dma(k.sp, W[:, kc, :], Wd[:, kc, :], W=[bW], key=bW)
    k.dma(k.sp, VEC[:, :], vecd[:, :], W=[bVEC], key=bVEC)
    k.dma(k.sp, GP[:, :], gpd[:, :], W=[bGP], key=bGP)
    k.dma(k.sp, WDU[:, :], wdud[:, :], W=[bWDU], key=bWDU)
    k.dma(k.sp, WIU[:, :], wiud[:, :], W=[bWIU], key=bWIU)
    k.dma(k.sp, CST[:, :, :], cstd[:, :, :], W=[bCST], key=bCST)
    k.ts(OMM[:, :], VEC[:, V_MU:V_MU + NB], -1.0, 1.0, ALU.mult, ALU.add, [bVEC], [bOMM])
    k.ts(OMKA[:, :], VEC[:, V_KA:V_KA + NH], -1.0, 1.0, ALU.mult, ALU.add, [bVEC], [bOMKA])
    for h in range(NH):
        k.cp(MSKI[:, h, :], CST[:, 2, :], [bCST], [bMSKI])
        k.cp(MSKS[:, h, :], CST[:, 3, :], [bCST], [bMSKS])
    pt, bpt = newps()
    k.tr(pt[:, 0:128], CST[:, 3, :], ident, [bCST], [bpt])
    for h in range(NH):
        k.cp(MSKL[:, h, :], pt[:, 0:128], [bpt], [bMSKL])
    k.op(k.dve, lambda: nc.vector.memset(ONES[:, :], 1.0), [], [bONES])
    k.op(k.dve, lambda: nc.vector.memset(CAR[:, :], 0.0), [], [bCAR])
    k.op(k.dve, lambda: nc.vector.memset(H[:, :, :], 0.0), [], bHh)

    def z(n):
        return Z[n][0]

    def zb(n):
        return Z[n][1]

    def flat(t):
        return t[:, :, :].rearrange("p a b -> p (a b)")


    def inproj(s):
        P, bP = PP[s % 2]
        t0 = s * ST
        k.dma(k.sp, XT[:, :, :], xT[:, :, t0:t0 + ST], W=[bXT], key=bXT)
        k.tt(flat(SQ), flat(XT), flat(XT), ALU.mult, [bXT], [bSQ])
        ps, bps = newps()
        for kc in range(8):
            k.mm(ps[:, 0:ST], onesm, SQ[:, kc, :], kc == 0, kc == 7, [bSQ, bCST], [bps])
        k.actf(RS[:, :], ps[:, 0:ST], AF.Sqrt, [bps], [bRS], bias=RMS_EPS, scale=1.0 / D)
        k.recip(RS[:, :], RS[:, :], [bRS], [bRS])
        for kc in range(8):
            k.stt(HT[:, kc, :], XT[:, kc, :], GP[:, kc:kc + 1], RS[:, :], ALU.mult, ALU.mult, [bXT, bGP, bRS], [bHT])
        for j in range(NB):
            ps, bps = newps()
            for kc in range(8):
                k.mm(ps[0:64, 0:ST], W[:, kc, j * 64:(j + 1) * 64], HT[:, kc, :], kc == 0, kc == 7, [bW, bHT], [bps])
            k.actf(TMP[:, 1:ST + 1], ps[0:64, 0:ST], AF.Copy, [bps, bVEC], [bTMP], scale=VEC[:, V_MU + j:V_MU + j + 1])
            k.cp(TMP[:, 0:1], CAR[:, j:j + 1], [bCAR], [bTMP], eng=k.act)
            k.stt(P[:, j, :], ps[0:64, 0:ST], OMM[:, j:j + 1], TMP[:, 0:ST], ALU.mult, ALU.add, [bps, bOMM, bTMP], [bP])
            k.cp(CAR[:, j:j + 1], TMP[:, ST:ST + 1], [bTMP], [bCAR], eng=k.act)
            yield

    def prep(s):
        P, bP = PP[s % 2]
        k.actf(P[:, 16, :], P[:, 16, :], AF.Tanh, [bP], [bP])
        yield
        for (lw, vb, dst, col) in ((WDU, bWDU, "SG", 16), (WIU, bWIU, "A", 17)):
            for h2 in range(NH // 2):
                ps, bps = newps()
                for hh in range(2):
                    h = h2 * 2 + hh
                    k.mm(ps[0:64, hh * ST:(hh + 1) * ST], lw[:, h * 64:(h + 1) * 64], P[:, col, :], True, True, [vb, bP], [bps])
                    yield
                for hh in range(2):
                    h = h2 * 2 + hh
                    vcol = (V_W0 if dst == "SG" else V_A0) + h
                    k.actf(z(dst)[:, h, :], ps[0:64, hh * ST:(hh + 1) * ST], AF.Sigmoid, [bps, bVEC], [zb(dst)],
                           bias=VEC[:, vcol:vcol + 1])
                    yield
        for h in range(NH):
            for ck in range(nck):
                sl = slice(ck * C, (ck + 1) * C)
                k.op(k.dve, lambda h=h, sl=sl: nc.vector.tensor_tensor_scan(z("CS")[:, h, sl], ONES[:, :], z("SG")[:, h, sl], 0.0, ALU.mult, ALU.add),
                     [bONES, zb("SG")], [zb("CS")])
                yield
        k.tt(flat(z("SG")), flat(z("CS")), flat(z("SG")), ALU.subtract, [zb("CS"), zb("SG")], [zb("SG")])
        yield
        k.actf(flat(z("RT")), flat(z("CS")), AF.Exp, [zb("CS")], [zb("RT")], scale=S_DEC)
        yield
        k.actf(flat(z("KPT")), flat(z("SG")), AF.Exp, [zb("SG")], [zb("KPT")], scale=S_DEC)
        yield
        k.actf(flat(z("E2")), flat(z("CS")), AF.Exp, [zb("CS")], [zb("E2")], scale=-S_DEC)
        yield
        for h in range(NH):
            for ck in range(nck):
                e = (ck + 1) * C - 1
                k.ts(SCE[:, h, ck:ck + 1], z("CS")[:, h, e:e + 1], S_DEC, None, ALU.mult, None, [zb("CS")], [bSCE])
                yield
        k.actf(SCE[:, :, :].rearrange("p a b -> p (a b)") if False else WC[:, :, :].rearrange("p a b -> p (a b)"),
               SCE[:, :, :].rearrange("p a b -> p (a b)"), AF.Exp, [bSCE], [bWC])
        yield
        for h in range(NH):
            for ck in range(nck):
                sl = slice(ck * C, (ck + 1) * C)
                k.actf(z("E3")[:, h, sl], z("CS")[:, h, sl], AF.Exp, [zb("CS"), bSCE], [zb("E3")],
                       bias=SCE[:, h, ck:ck + 1], scale=-S_DEC)
                yield
        for h in range(NH):
            k.ts(z("KAP")[:, h, :], P[:, 4 + h, :], VEC[:, V_KK + h:V_KK + h + 1], None, ALU.mult, None, [bP, bVEC], [zb("KAP")])
            yield
        k.tt(flat(z("T1")), flat(z("KAP")), flat(z("KAP")), ALU.mult, [zb("KAP")], [zb("T1")])
        yield
        for h2 in range(NH // 2):
            ps, bps = newps()
            for hh in range(2):
                h = h2 * 2 + hh
                k.mm(ps[0:64, hh * ST:(hh + 1) * ST], onesm[0:64, 0:64], z("T1")[:, h, :], True, True, [bCST, zb("T1")], [bps])
                yield
            k.actf(z("T2")[:, h2 * 2:h2 * 2 + 2, :].rearrange("p a b -> p (a b)"), ps[0:64, 0:2 * ST], AF.Sqrt, [bps], [zb("T2")])
            yield
        k.ts(flat(z("T2")), flat(z("T2")), 1e-12, None, ALU.max, None, [zb("T2")], [zb("T2")])
        yield
        k.recip(flat(z("T2")), flat(z("T2")), [zb("T2")], [zb("T2")])
        yield
        k.tt(flat(z("KAP")), flat(z("KAP")), flat(z("T2")), ALU.mult, [zb("KAP"), zb("T2")], [zb("KAP")])
        yield
        k.tt(flat(z("BM")), flat(z("A")), flat(z("KAP")), ALU.mult, [zb("A"), zb("KAP")], [zb("BM")])
        yield
        for h in range(NH):
            k.ts(z("T1")[:, h, :], z("A")[:, h, :], VEC[:, V_KA + h:V_KA + h + 1], OMKA[:, h:h + 1], ALU.mult, ALU.add,
                 [zb("A"), bVEC, bOMKA], [zb("T1")])
            yield
        k.tt(flat(z("K2")), P[:, 4:8, :].rearrange("p a b -> p (a b)"), flat(z("T1")), ALU.mult, [bP, zb("T1")], [zb("K2")])
        yield
        for h in range(NH):
            k.stt(z("T2")[:, h, :], P[:, h, :], VEC[:, V_RK + h:V_RK + h + 1], z("K2")[:, h, :], ALU.mult, ALU.mult,
                  [bP, bVEC, zb("K2")], [zb("T2")])
            yield
        for h2 in range(NH // 2):
            ps, bps = newps()
            for hh in range(2):
                h = h2 * 2 + hh
                k.mm(ps[0:64, hh * ST:(hh + 1) * ST], onesm[0:64, 0:64], z("T2")[:, h, :], True, True, [bCST, zb("T2")], [bps])
                yield
            k.tt(z("BV")[:, h2 * 2:h2 * 2 + 2, :].rearrange("p a b -> p (a b)"), ps[0:64, 0:2 * ST],
                 P[:, 8 + h2 * 2:10 + h2 * 2, :].rearrange("p a b -> p (a b)"), ALU.mult, [bps, bP], [zb("BV")])
            yield
        k.actf(P[:, 12:16, :].rearrange("p a b -> p (a b)"), P[:, 12:16, :].rearrange("p a b -> p (a b)"), AF.Silu, [bP], [bP])
        yield
        k.tt(flat(z("RT")), P[:, 0:4, :].rearrange("p a b -> p (a b)"), flat(z("RT")), ALU.mult, [bP, zb("RT")], [zb("RT")])
        yield
        k.tt(flat(z("KPT")), flat(z("KAP")), flat(z("KPT")), ALU.mult, [zb("KAP"), zb("KPT")], [zb("KPT")])
        yield
        k.tt(flat(z("KT")), flat(z("K2")), flat(z("E2")), ALU.mult, [zb("K2"), zb("E2")], [zb("KT")])
        yield
        k.tt(flat(z("BT")), flat(z("BM")), flat(z("E2")), ALU.mult, [zb("BM"), zb("E2")], [zb("BT")])
        yield
        k.tt(flat(z("KE")), flat(z("K2")), flat(z("E3")), ALU.mult, [zb("K2"), zb("E3")], [zb("KE")])
        yield
        k.tt(flat(z("BE")), flat(z("BM")), flat(z("E3")), ALU.mult, [zb("BM"), zb("E3")], [zb("BE")])
        yield


    def scan_post(s):
        P, bP = PP[s % 2]
        t0 = s * ST
        for ck in range(nck):
            sl = slice(ck * C, (ck + 1) * C)
            for (src, sb_, dst, db, neg) in ((P, bP, VT, bVT, False), (z("KE"), zb("KE"), KET, bKET, False),
                                             (z("BE"), zb("BE"), BET, bBET, True)):
                ps, bps = newps()
                for h in range(NH):
                    inp = src[:, 8 + h, sl] if src is P else src[:, h, sl]
                    k.tr(ps[:, h * HS:(h + 1) * HS], inp, ident[0:64, 0:64], [sb_, bCST], [bps])
                dflat = dst[:, :, :].rearrange("p a b -> p (a b)")
                if neg:
                    k.ts(dflat, ps[:, 0:NH * HS], -1.0, None, ALU.mult, None, [bps], [db])
                else:
                    k.cp(dflat, ps[:, 0:NH * HS], [bps], [db], eng=k.act)

            def amat(lh, lb, rh, rb, dst, mask, mb, neg):
                d, dbs = mats[dst]
                for g in range(2):
                    ps, bps = newps()
                    for hh in range(HG):
                        h = g * HG + hh
                        k.mm(ps[:, hh * 128:(hh + 1) * 128], lh[:, h, sl], rh[:, h, sl], True, True, [lb, rb], [bps])
                    dfl = d[:, g * HG:(g + 1) * HG, :].rearrange("p a b -> p (a b)")
                    mfl = mask[:, 0:HG, :].rearrange("p a b -> p (a b)")
                    if neg:
                        k.stt(dfl, ps[:, 0:HG * 128], -1.0, mfl, ALU.mult, ALU.mult, [bps, mb], [dbs[g]])
                    else:
                        k.tt(dfl, ps[:, 0:HG * 128], mfl, ALU.mult, [bps, mb], [dbs[g]])

            def hv(name, g):
                return mats[name][0][:, g * HG:(g + 1) * HG, :].rearrange("p a b -> p (a b)")

            amat(z("BT"), zb("BT"), z("KPT"), zb("KPT"), "QT0", MSKS, bMSKS, True)
            amat(z("KPT"), zb("KPT"), z("BT"), zb("BT"), "QN0", MSKL, bMSKL, True)
            amat(z("KT"), zb("KT"), z("KPT"), zb("KPT"), "AKK", MSKS, bMSKS, False)
            amat(z("KT"), zb("KT"), z("RT"), zb("RT"), "ARK", MSKI, bMSKI, False)
            amat(z("BT"), zb("BT"), z("RT"), zb("RT"), "ARB", MSKI, bMSKI, True)
            tt_cur = "TT0"
            for h in range(NH):
                k.tt(mats["TT0"][0][:, h, :], mats["QT0"][0][:, h, :], ident, ALU.add,
                     [mats["QT0"][1][h // HG], bCST], [mats["TT0"][1][h // HG]])
            qn, qt = "QN0", "QT0"
            nlev = 6
            for lv in range(nlev):
                qn2 = "QN1" if qn == "QN0" else "QN0"
                qt2 = "QT1" if qt == "QT0" else "QT0"
                tt2 = "TT1" if tt_cur == "TT0" else "TT0"
                last = lv == nlev - 1
                pq = [None, None]
                for g in range(2):
                    ps, bps = newps()
                    for hh in range(HG):
                        h = g * HG + hh
                        k.mm(ps[:, hh * 128:(hh + 1) * 128], mats[qt][0][:, h, :], mats[qn][0][:, h, :], True, True,
                             [mats[qt][1][g], mats[qn][1][g]], [bps])
                    ps2 = bps2 = None
                    if not last:
                        ps2, bps2 = newps()
                        for hh in range(HG):
                            h = g * HG + hh
                            k.mm(ps2[:, hh * 128:(hh + 1) * 128], mats[qn][0][:, h, :], mats[qt][0][:, h, :], True, True,
                                 [mats[qt][1][g], mats[qn][1][g]], [bps2])
                    pq[g] = (ps, bps, ps2, bps2)
                for g in range(2):
                    ps, bps, ps2, bps2 = pq[g]
                    k.cp(hv(qn2, g), ps[:, 0:HG * 128], [bps], [mats[qn2][1][g]], eng=k.act)
                    if not last:
                        k.cp(hv(qt2, g), ps2[:, 0:HG * 128], [bps2], [mats[qt2][1][g]])
                p3 = [None, None]
                for g in range(2):
                    ps3, bps3 = newps()
                    for hh in range(HG):
                        h = g * HG + hh
                        k.mm(ps3[:, hh * 128:(hh + 1) * 128], mats[qn2][0][:, h, :], mats[tt_cur][0][:, h, :], True, True,
                             [mats[qn2][1][g], mats[tt_cur][1][g]], [bps3])
                    p3[g] = (ps3, bps3)
                for g in range(2):
                    ps3, bps3 = p3[g]
                    k.tt(hv(tt2, g), ps3[:, 0:HG * 128], hv(tt_cur, g), ALU.add, [bps3, mats[tt_cur][1][g]], [mats[tt2][1][g]])
                qn, qt, tt_cur = qn2, qt2, tt2
            TT, bTT = mats[tt_cur]
            AKK, bAKK = mats["AKK"]
            ARK, bARK = mats["ARK"]
            ARB, bARB = mats["ARB"]
            GW_ = HG * HS
            pr = [None, None]
            for g in range(2):
                ps, bps = newps()
                for hh in range(HG):
                    h = g * HG + hh
                    k.mm(ps[:, hh * HS:(hh + 1) * HS], z("KPT")[:, h, sl], H[:, h, :], True, False, [zb("KPT"), bHh[g]], [bps])
                    k.mm(ps[:, hh * HS:(hh + 1) * HS], AKK[:, h, :], VT[:, h, :], False, True, [bAKK[g], bVT], [bps])
                pr[g] = (ps, bps)
            for g in range(2):
                ps, bps = pr[g]
                k.cp(RSB[:, g * HG:(g + 1) * HG, :].rearrange("p a b -> p (a b)"), ps[:, 0:GW_], [bps], [bRSBh[g]], eng=k.act)
            for g in range(2):
                ps, bps = newps()
                for hh in range(HG):
                    h = g * HG + hh
                    k.mm(ps[:, hh * HS:(hh + 1) * HS], TT[:, h, :], RSB[:, h, :], True, True, [bTT[g], bRSBh[g]], [bps])
                pr[g] = (ps, bps)
            for g in range(2):
                ps, bps = pr[g]
                k.cp(USB[:, g * HG:(g + 1) * HG, :].rearrange("p a b -> p (a b)"), ps[:, 0:GW_], [bps], [bUSBh[g]],
                     eng=(k.act if g == 0 else k.dve))
            for g in range(2):
                psy, bpsy = newps()
                for hh in range(HG):
                    h = g * HG + hh
                    o = psy[0:64, hh * 128:(hh + 1) * 128]
                    k.mm(o, H[:, h, :], z("RT")[:, h, sl], True, False, [bHh[g], zb("RT")], [bpsy])
                    k.mm(o, VT[:, h, :], ARK[:, h, :], False, False, [bVT, bARK[g]], [bpsy])
                    k.mm(o, USB[:, h, :], ARB[:, h, :], False, True, [bUSBh[g], bARB[g]], [bpsy])
                pr[g] = (psy, bpsy)
            for g in range(2):
                psy, bpsy = pr[g]
                for hh in range(HG):
                    h = g * HG + hh
                    k.cp(z("YT")[:, h, sl], psy[0:64, hh * 128:(hh + 1) * 128], [bpsy], [zb("YT")], eng=k.act)
            for g in range(2):
                pss, bpss = newps()
                for hh in range(HG):
                    h = g * HG + hh
                    k.mm(pss[0:64, hh * HS:(hh + 1) * HS], KET[:, h, :], VT[:, h, :], True, False, [bKET, bVT], [bpss])
                    k.mm(pss[0:64, hh * HS:(hh + 1) * HS], BET[:, h, :], USB[:, h, :], False, True, [bBET, bUSBh[g]], [bpss])
                pr[g] = (pss, bpss)
            for g in range(2):
                pss, bpss = pr[g]
                for hh in range(HG):
                    h = g * HG + hh
                    k.stt(H[:, h, :], H[:, h, :], WC[:, h, ck:ck + 1], pss[0:64, hh * HS:(hh + 1) * HS], ALU.mult, ALU.add,
                          [bHh[g], bWC, bpss], [bHh[g]])

        for h2 in range(NH // 2):
            ps, bps = newps()
            for hh in range(2):
                h = h2 * 2 + hh
                k.mm(ps[0:64, hh * ST:(hh + 1) * ST], onesm[0:64, 0:64], z("YT")[:, h, :], True, True, [bCST, zb("YT")], [bps])
            k.stt(z("YC")[:, h2 * 2:h2 * 2 + 2, :].rearrange("p a b -> p (a b)"), ps[0:64, 0:2 * ST], -1.0 / HS,
                  z("YT")[:, h2 * 2:h2 * 2 + 2, :].rearrange("p a b -> p (a b)"), ALU.mult, ALU.add, [bps, zb("YT")], [zb("YC")])
        k.tt(flat(z("T1")), flat(z("YC")), flat(z("YC")), ALU.mult, [zb("YC")], [zb("T1")])
        for h2 in range(NH // 2):
            ps, bps = newps()
            for hh in range(2):
                h = h2 * 2 + hh
                k.mm(ps[0:64, hh * ST:(hh + 1) * ST], onesm[0:64, 0:64], z("T1")[:, h, :], True, True, [bCST, zb("T1")], [bps])
            k.actf(z("T2")[:, h2 * 2:h2 * 2 + 2, :].rearrange("p a b -> p (a b)"), ps[0:64, 0:2 * ST], AF.Sqrt, [bps], [zb("T2")],
                   bias=GN_EPS, scale=1.0 / HS)
        k.recip(flat(z("T2")), flat(z("T2")), [zb("T2")], [zb("T2")])
        k.tt(flat(z("YC")), flat(z("YC")), flat(z("T2")), ALU.mult, [zb("YC"), zb("T2")], [zb("YC")])
        for h in range(NH):
            k.ts(z("YC")[:, h, :], z("YC")[:, h, :], VEC[:, V_LW + h:V_LW + h + 1], VEC[:, V_LB + h:V_LB + h + 1], ALU.mult, ALU.add,
                 [zb("YC"), bVEC], [zb("YC")])
        k.tt(flat(z("YC")), flat(z("YC")), flat(z("BV")), ALU.add, [zb("YC"), zb("BV")], [zb("YC")])
        k.tt(flat(z("YT")), flat(z("YC")), P[:, 12:16, :].rearrange("p a b -> p (a b)"), ALU.mult, [zb("YC"), bP], [zb("YT")])
        if ypair is None:
            k.dma(k.sp, yTd[:, :, t0:t0 + ST], z("YT")[:, :, :], R=[zb("YT")], key=zb("YT"))
        else:
            for h in range(NH):
                k.dma(k.sp, ypair[64 * (h % 2):64 * (h % 2) + 64, 2 * hg + h // 2, t0:t0 + ST], z("YT")[:, h, :],
                      R=[zb("YT")], key=zb("YT"))

    def run(gen, n):
        for _ in range(n):
            if next(gen, "done") == "done":
                return

    for _ in inproj(0):
        pass
    for s in range(nst):
        pg = prep(s)
        if s + 1 < nst:
            for _ in inproj(s + 1):
                run(pg, 8)
        for _ in pg:
            pass
        scan_post(s)
    k.end_stage()
def rwkv_main_inputs(x_b, w_in, mu, w0, wdu, a0, wiu, k_k, k_a, r_k, ln_w, ln_b, gpre, hg):
    xT = None if x_b is None else feat_major(x_b)
    ch = np.arange(hg * NH * HS, (hg + 1) * NH * HS)
    cols = np.concatenate([ch, D + ch, 2 * D + ch, 3 * D + ch, 4 * D + np.arange(128)])
    Wc = w_in[:, cols]
    W = np.ascontiguousarray(Wc.reshape(8, 128, NB * 64).transpose(1, 0, 2))
    vec = np.zeros((64, NVEC), np.float32)
    vec[:, V_MU:V_MU + NB] = mu[cols].reshape(NB, 64).T
    for nm, arr in ((V_W0, w0), (V_A0, a0), (V_KK, k_k), (V_KA, k_a), (V_RK, r_k.reshape(-1)), (V_LW, ln_w), (V_LB, ln_b)):
        vec[:, nm:nm + NH] = arr[ch].reshape(NH, 64).T
    gp = np.ascontiguousarray(gpre.reshape(8, 128).T)
    cst = np.zeros((128, 4, 128), np.float32)
    cst[:, 0, :] = np.eye(128)
    cst[:, 1, :] = 1.0
    cst[:, 2, :] = np.triu(np.ones((128, 128)))
    cst[:, 3, :] = np.triu(np.ones((128, 128)), 1)
    return {"xT": xT, "W": W, "vec": vec, "gpre": gp, "wdu": np.ascontiguousarray(wdu[:, ch]),
            "wiu": np.ascontiguousarray(wiu[:, ch]), "cst": cst}


GDK, GDV = 128, 256
GW = 2 * GDK + 2 * GDV + 16


def build_gla_main(T):
    k = KB()
    nc = k.nc
    xT = nc.dram_tensor("xT", [128, 8, T], F32, kind="ExternalInput").ap()
    Wd = nc.dram_tensor("W", [128, 8, GW], F32, kind="ExternalInput").ap()
    vecd = nc.dram_tensor("vec", [128, 3], F32, kind="ExternalInput").ap()
    gpd = nc.dram_tensor("gpre", [128, 8], F32, kind="ExternalInput").ap()
    wupd = nc.dram_tensor("wup", [16, GDK], F32, kind="ExternalInput").ap()
    cstd = nc.dram_tensor("cst", [128, 4, 128], F32, kind="ExternalInput").ap()
    yTd = nc.dram_tensor("yT", [128, 2, T], F32, kind="ExternalOutput").ap()
    emit_gla_main(k, T, xT, Wd, vecd, gpd, wupd, cstd, yTd)
    k.finish()
    return k


def emit_gla_main(k, T, xT, Wd, vecd, gpd, wupd, cstd, yTd):
    nc = k.nc
    nst = T // ST
    nck = ST // C
    k.begin_stage()
    XT, bXT = k.sb("XT", [128, 8, ST])
    SQ, bSQ = k.sb("SQ", [128, 8, ST])
    HT, bHT = k.sb("HT", [128, 8, ST])
    RS, bRS = k.sb("RS", [128, ST])
    W, bW = k.sb("Wsb", [128, 8, GW])
    VEC, bVEC = k.sb("VEC", [128, 3])
    GP, bGP = k.sb("GP", [128, 8])
    WUP, bWUP = k.sb("WUP", [16, GDK])
    CST, bCST = k.sb("CST", [128, 4, 128])
    ONES, bONES = k.sb("ONESr", [128, C])
    P, bP = k.sb("P", [128, 6, ST])
    AD, bAD = k.sb("AD", [16, ST])
    Z = {}
    for n in ["LA", "CS", "E1", "E2", "E3", "RT", "KT", "KE", "RSTD"]:
        Z[n] = k.sb(n, [128, ST])
    YT, bYT = k.sb("YT", [128, 2, ST])
    Y2, bY2 = k.sb("Y2", [128, 2, ST])
    SCE, bSCE = k.sb("SCE", [128, nck])
    WC, bWC = k.sb("WC", [128, nck])
    H, bH = k.sb("H", [128, GDV])
    VT, bVT = k.sb("VT", [128, GDV])
    KET, bKET = k.sb("KET", [128, GDK])
    ARK, bARK = k.sb("ARK", [128, 128])
    newps = k.newps

    ident = CST[:, 0, :]
    onesm = CST[:, 1, :]
    for kc in range(8):
        k.dma(k.sp, W[:, kc, :], Wd[:, kc, :], W=[bW], key=bW)
    k.dma(k.sp, VEC[:, :], vecd[:, :], W=[bVEC], key=bVEC)
    k.dma(k.sp, GP[:, :], gpd[:, :], W=[bGP], key=bGP)
    k.dma(k.sp, WUP[:, :], wupd[:, :], W=[bWUP], key=bWUP)
    k.dma(k.sp, CST[:, :, :], cstd[:, :, :], W=[bCST], key=bCST)
    k.op(k.dve, lambda: nc.vector.memset(ONES[:, :], 1.0), [], [bONES])
    k.op(k.dve, lambda: nc.vector.memset(H[:, :], 0.0), [], [bH])

    def z(n):
        return Z[n][0]

    def zb(n):
        return Z[n][1]

    def flat(t):
        return t[:, :, :].rearrange("p a b -> p (a b)")

    for s in range(nst):
        t0 = s * ST
        k.dma(k.sp, XT[:, :, :], xT[:, :, t0:t0 + ST], W=[bXT], key=bXT)
        k.tt(flat(SQ), flat(XT), flat(XT), ALU.mult, [bXT], [bSQ])
        ps, bps = newps()
        for kc in range(8):
            k.mm(ps[:, 0:ST], onesm, SQ[:, kc, :], kc == 0, kc == 7, [bSQ, bCST], [bps])
        k.actf(RS[:, :], ps[:, 0:ST], AF.Sqrt, [bps], [bRS], bias=RMS_EPS, scale=1.0 / D)
        k.recip(RS[:, :], RS[:, :], [bRS], [bRS])
        for kc in range(8):
            k.stt(HT[:, kc, :], XT[:, kc, :], GP[:, kc:kc + 1], RS[:, :], ALU.mult, ALU.mult, [bXT, bGP, bRS], [bHT])
        for j in range(6):
            ps, bps = newps()
            for kc in range(8):
                k.mm(ps[:, 0:ST], W[:, kc, j * 128:(j + 1) * 128], HT[:, kc, :], kc == 0, kc == 7, [bW, bHT], [bps])
            k.cp(P[:, j, :], ps[:, 0:ST], [bps], [bP], eng=(k.act if j % 2 else k.dve))
        ps, bps = newps()
        for kc in range(8):
            k.mm(ps[0:16, 0:ST], W[:, kc, 768:784], HT[:, kc, :], kc == 0, kc == 7, [bW, bHT], [bps])
        k.cp(AD[:, :], ps[0:16, 0:ST], [bps], [bAD])
        ps, bps = newps()
        k.mm(ps[:, 0:ST], WUP[:, :], AD[:, :], True, True, [bWUP, bAD], [bps])
        k.actf(z("LA")[:, :], ps[:, 0:ST], AF.Sigmoid, [bps, bVEC], [zb("LA")], bias=VEC[:, 0:1])
        k.actf(z("LA")[:, :], z("LA")[:, :], AF.Ln, [zb("LA")], [zb("LA")])
        for ck in range(nck):
            sl = slice(ck * C, (ck + 1) * C)
            k.op(k.dve, lambda sl=sl: nc.vector.tensor_tensor_scan(z("CS")[:, sl], ONES[:, :], z("LA")[:, sl], 0.0, ALU.mult, ALU.add),
                 [bONES, zb("LA")], [zb("CS")])
        k.actf(z("E1")[:, :], z("CS")[:, :], AF.Exp, [zb("CS")], [zb("E1")], scale=1.0 / 16)
        k.actf(z("E2")[:, :], z("CS")[:, :], AF.Exp, [zb("CS")], [zb("E2")], scale=-1.0 / 16)
        for ck in range(nck):
            e = (ck + 1) * C - 1
            k.ts(SCE[:, ck:ck + 1], z("CS")[:, e:e + 1], 1.0 / 16, None, ALU.mult, None, [zb("CS")], [bSCE])
        k.actf(WC[:, :], SCE[:, :], AF.Exp, [bSCE], [bWC])
        for ck in range(nck):
            sl = slice(ck * C, (ck + 1) * C)
            k.actf(z("E3")[:, sl], z("CS")[:, sl], AF.Exp, [zb("CS"), bSCE], [zb("E3")], bias=SCE[:, ck:ck + 1], scale=-1.0 / 16)
        k.stt(z("RT")[:, :], P[:, 0, :], float(GDK ** -0.5), z("E1")[:, :], ALU.mult, ALU.mult, [bP, zb("E1")], [zb("RT")])
        k.tt(z("KT")[:, :], P[:, 1, :], z("E2")[:, :], ALU.mult, [bP, zb("E2")], [zb("KT")])
        k.tt(z("KE")[:, :], P[:, 1, :], z("E3")[:, :], ALU.mult, [bP, zb("E3")], [zb("KE")])
        k.actf(P[:, 4:6, :].rearrange("p a b -> p (a b)"), P[:, 4:6, :].rearrange("p a b -> p (a b)"), AF.Silu, [bP], [bP])
        for ck in range(nck):
            sl = slice(ck * C, (ck + 1) * C)
            ps, bps = newps()
            for vb in range(2):
                k.tr(ps[:, vb * 128:(vb + 1) * 128], P[:, 2 + vb, sl], ident, [bP, bCST], [bps])
            k.cp(VT[:, :], ps[:, 0:GDV], [bps], [bVT], eng=k.act)
            ps, bps = newps()
            k.tr(ps[:, 0:128], z("KE")[:, sl], ident, [zb("KE"), bCST], [bps])
            k.cp(KET[:, :], ps[:, 0:128], [bps], [bKET])
            ps, bps = newps()
            k.mm(ps[:, 0:128], z("KT")[:, sl], z("RT")[:, sl], True, True, [zb("KT"), zb("RT")], [bps])
            k.tt(ARK[:, :], ps[:, 0:128], CST[:, 2, :], ALU.mult, [bps, bCST], [bARK])
            psy, bpsy = newps()
            for vb in range(2):
                k.mm(psy[:, vb * 128:(vb + 1) * 128], H[:, vb * 128:(vb + 1) * 128], z("RT")[:, sl], True, False, [bH, zb("RT")], [bpsy])
                k.mm(psy[:, vb * 128:(vb + 1) * 128], VT[:, vb * 128:(vb + 1) * 128], ARK[:, :], False, True, [bVT, bARK], [bpsy])
            for vb in range(2):
                k.cp(YT[:, vb, sl], psy[:, vb * 128:(vb + 1) * 128], [bpsy], [bYT], eng=k.act)
            pss, bpss = newps()
            k.mm(pss[:, 0:GDV], KET[:, :], VT[:, :], True, True, [bKET, bVT], [bpss])
            k.stt(H[:, :], H[:, :], WC[:, ck:ck + 1], pss[:, 0:GDV], ALU.mult, ALU.add, [bH, bWC, bpss], [bH])
        k.tt(flat(Y2), flat(YT), flat(YT), ALU.mult, [bYT], [bY2])
        ps, bps = newps()
        for vb in range(2):
            k.mm(ps[:, 0:ST], onesm, Y2[:, vb, :], vb == 0, vb == 1, [bCST, bY2], [bps])
        k.actf(z("RSTD")[:, :], ps[:, 0:ST], AF.Sqrt, [bps], [zb("RSTD")], bias=RMS_EPS, scale=1.0 / GDV)
        k.recip(z("RSTD")[:, :], z("RSTD")[:, :], [zb("RSTD")], [zb("RSTD")])
        for vb in range(2):
            k.stt(Y2[:, vb, :], YT[:, vb, :], VEC[:, 1 + vb:2 + vb], z("RSTD")[:, :], ALU.mult, ALU.mult, [bYT, bVEC, zb("RSTD")], [bY2])
        k.tt(flat(YT), flat(Y2), P[:, 4:6, :].rearrange("p a b -> p (a b)"), ALU.mult, [bY2, bP], [bYT])
        k.dma(k.sp, yTd[:, :, t0:t0 + ST], YT[:, :, :], R=[bYT], key=bYT)
    k.end_stage()


def make_cst():
    cst = np.zeros((128, 4, 128), np.float32)
    cst[:, 0, :] = np.eye(128)
    cst[:, 1, :] = 1.0
    cst[:, 2, :] = np.triu(np.ones((128, 128)))
    cst[:, 3, :] = np.triu(np.ones((128, 128)), 1)
    return cst


def feat_major(x_b):
    T = x_b.shape[0]
    return np.ascontiguousarray(x_b.T.reshape(8, 128, T).transpose(1, 0, 2))


def gla_main_inputs(xT, w_in, wup, b_alpha, g_head, gpre, h):
    cols = np.concatenate([h * GDK + np.arange(GDK), 512 + h * GDK + np.arange(GDK), 1024 + h * GDV + np.arange(GDV),
                           2048 + h * GDV + np.arange(GDV), 3072 + np.arange(16)])
    W = np.ascontiguousarray(w_in[:, cols].reshape(8, 128, GW).transpose(1, 0, 2))
    vec = np.zeros((128, 3), np.float32)
    vec[:, 0] = b_alpha[h * GDK:(h + 1) * GDK]
    vec[:, 1] = g_head[0:128]
    vec[:, 2] = g_head[128:256]
    return {"xT": xT, "W": W, "vec": vec, "gpre": np.ascontiguousarray(gpre.reshape(8, 128).T),
            "wup": np.ascontiguousarray(wup[:, h * GDK:(h + 1) * GDK]), "cst": make_cst()}


def build_out(Tc, KP, NKC):
    k = KB()
    nc = k.nc
    yTd = nc.dram_tensor("yT", [KP, NKC, Tc], F32, kind="ExternalInput").ap()
    Wd = nc.dram_tensor("W", [KP, NKC, D], F32, kind="ExternalInput").ap()
    xd = nc.dram_tensor("x", [Tc, D], F32, kind="ExternalInput").ap()
    gd = nc.dram_tensor("gpost", [128, D], F32, kind="ExternalInput").ap()
    od = nc.dram_tensor("out", [Tc, D], F32, kind="ExternalOutput").ap()
    W, bW = k.sb("Wsb", [KP, NKC, D])
    G, bG = k.sb("G", [128, D])
    Y, bY = k.sb("Y", [KP, NKC, 128])
    X, bX = k.sb("X", [128, D])
    YO, bYO = k.sb("YO", [128, D])
    SQ, bSQ = k.sb("SQ", [128, D])
    SS, bSS = k.sb("SS", [128, 1])
    O, bO = k.sb("O", [128, D])
    newps = k.newps

    for kc in range(NKC):
        k.dma(k.sp, W[:, kc, :], Wd[:, kc, :], W=[bW], key=bW)
    k.dma(k.sp, G[:, :], gd[:, :], W=[bG], key=bG)
    for t in range(Tc // 128):
        sl = slice(t * 128, (t + 1) * 128)
        k.dma(k.sp, Y[:, :, :], yTd[:, :, sl], W=[bY], key=bY)
        k.dma(k.sp, X[:, :], xd[sl, :], W=[bX], key=bX)
        for hf in range(2):
            ps, bps = newps()
            for kc in range(NKC):
                k.mm(ps[:, :], Y[:, kc, :], W[:, kc, hf * 512:(hf + 1) * 512], kc == 0, kc == NKC - 1, [bY, bW], [bps])
            k.cp(YO[:, hf * 512:(hf + 1) * 512], ps[:, :], [bps], [bYO], eng=(k.act if hf else k.dve))
        k.tt(SQ[:, :], YO[:, :], YO[:, :], ALU.mult, [bYO], [bSQ])
        k.op(k.dve, lambda: nc.vector.reduce_sum(SS[:, :], SQ[:, :], axis=mybir.AxisListType.X), [bSQ], [bSS])
        k.actf(SS[:, :], SS[:, :], AF.Sqrt, [bSS], [bSS], bias=RMS_EPS, scale=1.0 / D)
        k.recip(SS[:, :], SS[:, :], [bSS], [bSS])
        k.stt(O[:, :], YO[:, :], SS[:, 0:1], G[:, :], ALU.mult, ALU.mult, [bYO, bSS, bG], [bO])
        k.tt(O[:, :], O[:, :], X[:, :], ALU.add, [bO, bX], [bO])
        k.dma(k.sp, od[sl, :], O[:, :], R=[bO], key=bO)
    k.finish([bO])
    return k


def run_out(yT_b, w_out, x, gpost, KP, NKC):
    B, T = x.shape[0], x.shape[1]
    Tc = T // 4
    kb = build_out(Tc, KP, NKC)
    W = np.ascontiguousarray(w_out.reshape(NKC, KP, D).transpose(1, 0, 2))
    g = np.ascontiguousarray(np.broadcast_to(gpost[None, :], (128, D)))
    maps = []
    for c in range(8):
        b, sg = c // 4, c % 4
        maps.append({"yT": np.ascontiguousarray(yT_b[b][:, :, sg * Tc:(sg + 1) * Tc]), "W": W,
                     "x": np.ascontiguousarray(x[b, sg * Tc:(sg + 1) * Tc]), "gpost": g})
    res = run_bass_kernel_spmd(kb.nc, maps, core_ids=list(range(8)))
    out = np.empty((B, T, D), np.float32)
    for c in range(8):
        b, sg = c // 4, c % 4
        out[b, sg * Tc:(sg + 1) * Tc] = res.results[c]["out"]
    return out


def gla_layer(x, p):
    B, T = x.shape[0], x.shape[1]
    kb = build_gla_main(T)
    xTs = [feat_major(x[b]) for b in range(B)]
    maps = []
    for c in range(8):
        b, h = c // 4, c % 4
        maps.append(gla_main_inputs(xTs[b], p["gla_w_in"][0], p["gla_w_alpha_up"][0], p["gla_b_alpha"][0],
                                    p["gla_head_norm"][0], p["gla_pre_norm"][0], h))
    res = run_bass_kernel_spmd(kb.nc, maps, core_ids=list(range(8)))
    yT = [np.empty((128, 8, T), np.float32) for _ in range(B)]
    for c in range(8):
        b, h = c // 4, c % 4
        yT[b][:, 2 * h:2 * h + 2, :] = res.results[c]["yT"]
    return run_out(yT, p["gla_w_out"][0], x, p["gla_post_norm"][0], 128, 8)


def rwkv_layer(x, p):
    B, T = x.shape[0], x.shape[1]
    kb = build_rwkv_main(T)
    maps = []
    for c in range(8):
        b, hg = c // 4, c % 4
        maps.append(rwkv_main_inputs(x[b], p["rwkv_w_in"][0], p["rwkv_mu"][0], p["rwkv_w0"][0], p["rwkv_w_decay_up"][0],
                                     p["rwkv_a0"][0], p["rwkv_w_iclr_up"][0], p["rwkv_k_k"][0], p["rwkv_k_a"][0],
                                     p["rwkv_r_k"][0], p["rwkv_ln_w"][0], p["rwkv_ln_b"][0], p["rwkv_pre_norm"][0], hg))
    res = run_bass_kernel_spmd(kb.nc, maps, core_ids=list(range(8)))
    yT = [np.empty((64, 16, T), np.float32) for _ in range(B)]
    for c in range(8):
        b, hg = c // 4, c % 4
        yT[b][:, 4 * hg:4 * hg + 4, :] = res.results[c]["yT"]
    return run_out(yT, p["rwkv_w_out"][0], x, p["rwkv_post_norm"][0], 64, 16)


def emit_outproj_fm(k, T, KP, NKC, Yd, Wd, gpd, Xd, Od, cstd):
    nc = k.nc
    k.begin_stage()
    W, bW = k.sb("Wo", [KP, NKC, D])
    Ys = [k.sb("Yo%d" % i, [KP, NKC, ST]) for i in range(2)]
    XTs = [k.sb("XTo%d" % i, [128, 8, ST]) for i in range(2)]
    YO, bYO = k.sb("YOo", [128, 8, ST])
    SQ, bSQ = k.sb("SQo", [128, 8, ST])
    RS, bRS = k.sb("RSo", [128, ST])
    Os = [k.sb("Oo%d" % i, [128, 8, ST]) for i in range(2)]
    GP, bGP = k.sb("GPo", [128, 8])
    CST, bCST = k.sb("CSTo", [128, 4, 128])
    onesm = CST[:, 1, :]
    for kc in range(NKC):
        k.dma(k.sp, W[:, kc, :], Wd[:, kc, :], W=[bW], key=bW)
    k.dma(k.sp, GP[:, :], gpd[:, :], W=[bGP], key=bGP)
    k.dma(k.sp, CST[:, :, :], cstd[:, :, :], W=[bCST], key=bCST)

    def flat(t):
        return t[:, :, :].rearrange("p a b -> p (a b)")

    for s in range(T // ST):
        t0 = s * ST
        Y, bY = Ys[s % 2]
        XT, bXT = XTs[s % 2]
        O, bO = Os[s % 2]
        k.dma(k.sp, Y[:, :, :], Yd[:, :, t0:t0 + ST], W=[bY], key=bY)
        k.dma(k.sp, XT[:, :, :], Xd[:, :, t0:t0 + ST], W=[bXT], key=bXT)
        for fb in range(8):
            ps, bps = k.newps()
            for kc in range(NKC):
                k.mm(ps[:, 0:ST], W[:, kc, fb * 128:(fb + 1) * 128], Y[:, kc, :], kc == 0, kc == NKC - 1, [bW, bY], [bps])
            k.cp(YO[:, fb, :], ps[:, 0:ST], [bps], [bYO], eng=(k.act if fb % 2 else k.dve))
        k.tt(flat(SQ), flat(YO), flat(YO), ALU.mult, [bYO], [bSQ])
        ps, bps = k.newps()
        for fb in range(8):
            k.mm(ps[:, 0:ST], onesm, SQ[:, fb, :], fb == 0, fb == 7, [bSQ, bCST], [bps])
        k.actf(RS[:, :], ps[:, 0:ST], AF.Sqrt, [bps], [bRS], bias=RMS_EPS, scale=1.0 / D)
        k.recip(RS[:, :], RS[:, :], [bRS], [bRS])
        for fb in range(8):
            k.stt(O[:, fb, :], YO[:, fb, :], GP[:, fb:fb + 1], RS[:, :], ALU.mult, ALU.mult, [bYO, bGP, bRS], [bO])
        k.tt(flat(O), flat(O), flat(XT), ALU.add, [bO, bXT], [bO])
        k.dma(k.sp, Od[:, :, t0:t0 + ST], O[:, :, :], R=[bO], key=bO)
    k.end_stage()


def build_fused(T):
    k = KB()
    nc = k.nc

    def din(name, shape):
        return nc.dram_tensor(name, list(shape), F32, kind="ExternalInput").ap()

    xT = din("xT", [128, 8, T])
    cst = din("cst", [128, 4, 128])
    gW = din("gW", [4, 128, 8, GW])
    gvec = din("gvec", [4, 128, 3])
    gwup = din("gwup", [4, 16, GDK])
    ggpre = din("ggpre", [128, 8])
    gWo = din("gWo", [128, 8, D])
    ggpost = din("ggpost", [128, 8])
    rW = din("rW", [4, 128, 8, NB * 64])
    rvec = din("rvec", [4, 64, NVEC])
    rwdu = din("rwdu", [4, 64, NH * 64])
    rwiu = din("rwiu", [4, 64, NH * 64])
    rgpre = din("rgpre", [128, 8])
    rWo = din("rWo", [128, 8, D])
    rgpost = din("rgpost", [128, 8])
    outT = nc.dram_tensor("outT", [128, 8, T], F32, kind="ExternalOutput").ap()
    Y1 = nc.dram_tensor("Y1s", [128, 8, T], F32).ap()
    X1 = nc.dram_tensor("X1s", [128, 8, T], F32).ap()
    Y2 = nc.dram_tensor("Y2s", [128, 8, T], F32).ap()
    for h in range(4):
        emit_gla_main(k, T, xT, gW[h], gvec[h], ggpre, gwup[h], cst, Y1[:, 2 * h:2 * h + 2, :])
    emit_outproj_fm(k, T, 128, 8, Y1, gWo, ggpost, xT, X1, cst)
    for hg in range(4):
        emit_rwkv_main(k, T, X1, rW[hg], rvec[hg], rgpre, rwdu[hg], rwiu[hg], cst, None, ypair=Y2, hg=hg)
    emit_outproj_fm(k, T, 128, 8, Y2, rWo, rgpost, X1, outT, cst)
    k.finish()
    return k


def fused_inputs(x_b, p):
    g = [gla_main_inputs(None, p["gla_w_in"][0], p["gla_w_alpha_up"][0], p["gla_b_alpha"][0], p["gla_head_norm"][0],
                         p["gla_pre_norm"][0], h) for h in range(4)]
    r = [rwkv_main_inputs(None, p["rwkv_w_in"][0], p["rwkv_mu"][0], p["rwkv_w0"][0], p["rwkv_w_decay_up"][0],
                          p["rwkv_a0"][0], p["rwkv_w_iclr_up"][0], p["rwkv_k_k"][0], p["rwkv_k_a"][0],
                          p["rwkv_r_k"][0], p["rwkv_ln_w"][0], p["rwkv_ln_b"][0], p["rwkv_pre_norm"][0], hg) for hg in range(4)]
    fm = lambda v: np.ascontiguousarray(v.reshape(8, 128).T)
    return {
        "xT": feat_major(x_b), "cst": make_cst(),
        "gW": np.stack([a["W"] for a in g]), "gvec": np.stack([a["vec"] for a in g]),
        "gwup": np.stack([a["wup"] for a in g]), "ggpre": g[0]["gpre"],
        "gWo": np.ascontiguousarray(p["gla_w_out"][0].reshape(8, 128, D).transpose(1, 0, 2)), "ggpost": fm(p["gla_post_norm"][0]),
        "rW": np.stack([a["W"] for a in r]), "rvec": np.stack([a["vec"] for a in r]),
        "rwdu": np.stack([a["wdu"] for a in r]), "rwiu": np.stack([a["wiu"] for a in r]), "rgpre": r[0]["gpre"],
        "rWo": np.ascontiguousarray(p["rwkv_w_out"][0].reshape(8, 128, D).transpose(1, 0, 2)), "rgpost": fm(p["rwkv_post_norm"][0]),
    }


def kernel_unfused(**inputs):
    p = {k_: np.asarray(v, dtype=np.float32) for k_, v in inputs.items()}
    x = p["x"]
    x = gla_layer(x, p)
    x = rwkv_layer(x, p)
    return x


def kernel(**inputs):
    p = {k_: np.asarray(v, dtype=np.float32) for k_, v in inputs.items()}
    x = p["x"]
    B, T = x.shape[0], x.shape[1]
    kb = build_fused(T)
    per_b = [fused_inputs(x[b], p) for b in range(B)]
    maps = [per_b[c % B] for c in range(8)]
    res = run_bass_kernel_spmd(kb.nc, maps, core_ids=list(range(8)))
    out = np.empty((B, T, D), np.float32)
    for b in range(B):
        oT = res.results[b]["outT"]
        out[b] = oT.transpose(2, 1, 0).reshape(T, D)
    return out
```

```python
import math
import numpy as np
import concourse.bass as bass
import concourse.mybir as mybir
from concourse.bass_utils import run_bass_kernel_spmd

F32 = mybir.dt.float32
AF = mybir.ActivationFunctionType
ALU = mybir.AluOpType

D = 1024
C = 128
ST = 256
RMS_EPS = 1e-6
GN_EPS = 64e-5
S_DEC = -math.exp(-0.5)


class Buf:
    def __init__(self, name):
        self.name = name
        self.w = None
        self.r = []
        self.dsem = None
        self.dcnt = 0


class Eng:
    def __init__(self, k, e, name):
        self.k, self.e, self.name = k, e, name
        self.sem = k.new_sem(name)
        self.cnt = 0
        self.seen = {}

    def wait(self, ev):
        if ev is None:
            return
        sem, val = ev
        if self.seen.get(id(sem), 0) >= val:
            return
        if sem is self.sem and self.name == "pe":
            return
        self.e.wait_ge(sem, val)
        self.seen[id(sem)] = val


class KB:
    def __init__(self):
        self.nc = bass.Bass("TRN2", target_bir_lowering=False)
        self.stack = []
        nc = self.nc
        self.pe = Eng(self, nc.tensor, "pe")
        self.act = Eng(self, nc.scalar, "act")
        self.dve = Eng(self, nc.vector, "dve")
        self.pool = Eng(self, nc.gpsimd, "pool")
        self.sp = Eng(self, nc.sync, "sp")
        self.nbuf = 0
        self.ninst = 0
        self.guards = []
        self.marks = []
        self.dma_bufs = []
        self.psb = None
        self.psi = 0

    def new_sem(self, name):
        cm = self.nc.semaphore(name)
        s = cm.__enter__()
        self.stack.append(cm)
        return s

    def close(self):
        for cm in reversed(self.stack):
            cm.__exit__(None, None, None)
        self.stack = []

    def sb(self, name, shape, dtype=F32):
        self.nbuf += 1
        g = self.nc.sbuf_tensor("%s_%d" % (name, self.nbuf), list(shape), dtype)
        t = g.__enter__()
        self.guards.append(g)
        return t, Buf(name)

    def begin_stage(self):
        self.marks.append(len(self.guards))

    def end_stage(self):
        self.barrier()
        m = self.marks.pop()
        while len(self.guards) > m:
            self.guards.pop().__exit__(None, None, None)

    def barrier(self):
        engs = [self.pe, self.act, self.dve, self.sp, self.pool]
        for e in engs:
            for f in engs:
                if f is not e and f.cnt > 0:
                    e.wait((f.sem, f.cnt))
            for b in self.dma_bufs:
                e.wait((b.dsem, b.dcnt))

    def psum_banks(self):
        if self.psb is None:
            self.psb = [self.ps("ps%d" % i, [128, 512]) for i in range(8)]
        return self.psb

    def newps(self):
        r = self.psum_banks()[self.psi % 8]
        self.psi += 1
        return r

    def ps(self, name, shape):
        t = self.nc.alloc_psum_tensor(name, list(shape), F32)
        return t, Buf(name)

    def _deps(self, eng, R, W):
        for b in R:
            eng.wait(b.w)
        for b in W:
            eng.wait(b.w)
            for ev in b.r:
                eng.wait(ev)

    def op(self, eng, fn, R=(), W=()):
        self._deps(eng, R, W)
        ins = fn()
        eng.cnt += 1
        ins.then_inc(eng.sem, 1)
        ev = (eng.sem, eng.cnt)
        eng.seen[id(eng.sem)] = max(eng.seen.get(id(eng.sem), 0), 0)
        for b in R:
            b.r.append(ev)
        for b in W:
            b.w = ev
            b.r = []
        self.ninst += 1
        return ins

    def dma(self, eng, out, in_, R=(), W=(), key=None):
        self._deps(eng, R, W)
        if key.dsem is None:
            self.nbuf += 1
            key.dsem = self.new_sem("d_%s_%d" % (key.name, self.nbuf))
            self.dma_bufs.append(key)
        ins = eng.e.dma_start(out=out, in_=in_)
        key.dcnt += 16
        ins.then_inc(key.dsem, 16)
        ev = (key.dsem, key.dcnt)
        for b in R:
            b.r.append(ev)
        for b in W:
            b.w = ev
            b.r = []
        self.ninst += 1
        return ev

    def mm(self, out, lhsT, rhs, start, stop, R, W):
        return self.op(self.pe, lambda: self.nc.tensor.matmul(out, lhsT=lhsT, rhs=rhs, start=start, stop=stop), R, W)

    def tr(self, out, in_, ident, R, W):
        return self.op(self.pe, lambda: self.nc.tensor.transpose(out, in_, ident), R, W)

    def actf(self, out, in_, func, R, W, bias=0.0, scale=1.0):
        return self.op(self.act, lambda: self.nc.scalar.activation(out=out, in_=in_, func=func, bias=bias, scale=scale), R, W)

    def tt(self, out, a, b, op, R, W, eng=None):
        eng = eng or self.dve
        return self.op(eng, lambda: eng.e.tensor_tensor(out, a, b, op), R, W)

    def ts(self, out, a, s1, s2, op0, op1, R, W, eng=None):
        eng = eng or self.dve
        if s2 is None:
            return self.op(eng, lambda: eng.e.tensor_single_scalar(out, a, s1, op0), R, W)
        return self.op(eng, lambda: eng.e.tensor_scalar(out, a, s1, s2, op0, op1), R, W)

    def stt(self, out, a, s, b, op0, op1, R, W, eng=None):
        eng = eng or self.dve
        return self.op(eng, lambda: eng.e.scalar_tensor_tensor(out, a, s, b, op0, op1), R, W)

    def cp(self, out, in_, R, W, eng=None):
        eng = eng or self.dve
        if eng is self.act:
            return self.op(eng, lambda: self.nc.scalar.copy(out, in_), R, W)
        return self.op(eng, lambda: eng.e.tensor_copy(out, in_), R, W)

    def recip(self, out, in_, R, W):
        return self.op(self.dve, lambda: self.nc.vector.reciprocal(out, in_), R, W)

    def finish(self, out_bufs=()):
        self.barrier()
        for b in self.dma_bufs:
            self.sp.e.wait_ge(b.dsem, b.dcnt)
        while self.guards:
            self.guards.pop().__exit__(None, None, None)
        self.close()


NH = 4
HS = 64
NB = 18
V_MU, V_W0, V_A0, V_KK, V_KA, V_RK, V_LW, V_LB = 0, 18, 22, 26, 30, 34, 38, 42
NVEC = 46


def build_rwkv_main(T):
    k = KB()
    nc = k.nc
    xT = nc.dram_tensor("xT", [128, 8, T], F32, kind="ExternalInput").ap()
    Wd = nc.dram_tensor("W", [128, 8, NB * 64], F32, kind="ExternalInput").ap()
    vecd = nc.dram_tensor("vec", [64, NVEC], F32, kind="ExternalInput").ap()
    gpd = nc.dram_tensor("gpre", [128, 8], F32, kind="ExternalInput").ap()
    wdud = nc.dram_tensor("wdu", [64, NH * 64], F32, kind="ExternalInput").ap()
    wiud = nc.dram_tensor("wiu", [64, NH * 64], F32, kind="ExternalInput").ap()
    cstd = nc.dram_tensor("cst", [128, 4, 128], F32, kind="ExternalInput").ap()
    yTd = nc.dram_tensor("yT", [64, NH, T], F32, kind="ExternalOutput").ap()
    vecpd = nc.dram_tensor("vecp", [128, NB // 2], F32, kind="ExternalInput").ap()
    emit_rwkv_main(k, T, xT, Wd, vecd, gpd, wdud, wiud, cstd, yTd, vecpd=vecpd)
    k.finish()
    return k


def emit_rwkv_main(k, T, xT, Wd, vecd, gpd, wdud, wiud, cstd, yTd, ypair=None, hg=0, vecpd=None):
    nc = k.nc
    nst = T // ST
    nck = ST // C
    k.begin_stage()
    XT, bXT = k.sb("XT", [128, 8, ST])
    SQ, bSQ = k.sb("SQHT", [128, 8, ST])
    HT, bHT = SQ, bSQ
    RS, bRS = k.sb("RS", [128, ST])
    W, bW = k.sb("Wsb", [128, 8, NB * 64])
    VEC, bVEC = k.sb("VEC", [64, NVEC])
    OMKA, bOMKA = k.sb("OMKA", [64, NH])
    GP, bGP = k.sb("GP", [128, 8])
    WDU, bWDU = k.sb("WDU", [64, NH * 64])
    WIU, bWIU = k.sb("WIU", [64, NH * 64])
    CST, bCST = k.sb("CST", [128, 4, 128])
    MSKI, bMSKI = k.sb("MSKI", [128, NH, 128])
    MSKS, bMSKS = k.sb("MSKS", [128, NH, 128])
    MSKL, bMSKL = k.sb("MSKL", [128, NH, 128])
    ONES, bONES = k.sb("ONESr", [64, C])
    PP = [k.sb("P0", [64, NB, ST]), k.sb("P1", [64, NB, ST])]
    NPB = NB // 2
    TMP, bTMP = k.sb("TMP", [128, ST + 1])
    CAR, bCAR = k.sb("CAR", [128, NPB])
    PPR, bPPR = k.sb("PPR", [128, NPB, ST])
    VECP, bVECP = k.sb("VECP", [128, NPB])
    OMMP, bOMMP = k.sb("OMMP", [128, NPB])
    names = ["SG", "A", "CS", "E2", "E3", "KAP", "T1", "T2", "BM", "K2", "BV",
             "RT", "KPT", "KT", "BT", "KE", "BE", "YT"]
    Z = {}
    for n in names:
        Z[n] = k.sb(n, [64, NH, ST])
    Z["YC"] = Z["BM"]
    SCE, bSCE = k.sb("SCE", [64, NH, nck])
    WC, bWC = k.sb("WC", [64, NH, nck])
    H, bH = k.sb("H", [64, NH, HS])
    VT, bVT = k.sb("VT", [128, NH, HS])
    KET, bKET = k.sb("KET", [128, NH, HS])
    BET, bBET = k.sb("BET", [128, NH, HS])
    RSB, bRSB = k.sb("RSB", [128, NH, HS])
    USB, bUSB = k.sb("USB", [128, NH, HS])
    mats = {}
    HG = NH // 2
    for n in ["QN0", "QT0", "QN1", "QT1", "TT0", "TT1", "AKK", "ARK", "ARB"]:
        t_, _b = k.sb(n, [128, NH, 128])
        mats[n] = (t_, [Buf(n + "a"), Buf(n + "b")])
    bHh = [Buf("Ha"), Buf("Hb")]
    bRSBh = [Buf("RSBa"), Buf("RSBb")]
    bUSBh = [Buf("USBa"), Buf("USBb")]
    newps = k.newps

    ident = CST[:, 0, :]
    onesm = CST[:, 1, :]

    for kc in range(8):
        k.dma(k.sp, W[:, kc, :], Wd[:, kc, :], W=[bW], key=bW)
    k.dma(k.sp, VEC[:, :], vecd[:, :], W=[bVEC], key=bVEC)
    k.dma(k.sp, GP[:, :], gpd[:, :], W=[bGP], key=bGP)
    k.dma(k.sp, WDU[:, :], wdud[:, :], W=[bWDU], key=bWDU)
    k.dma(k.sp, WIU[:, :], wiud[:, :], W=[bWIU], key=bWIU)
    k.dma(k.sp, CST[:, :, :], cstd[:, :, :], W=[bCST], key=bCST)
    k.dma(k.sp, VECP[:, :], vecpd[:, :], W=[bVECP], key=bVECP)
    k.ts(OMMP[:, :], VECP[:, :], -1.0, 1.0, ALU.mult, ALU.add, [bVECP], [bOMMP])
    k.ts(OMKA[:, :], VEC[:, V_KA:V_KA + NH], -1.0, 1.0, ALU.mult, ALU.add, [bVEC], [bOMKA])
    for h in range(NH):
        k.cp(MSKI[:, h, :], CST[:, 2, :], [bCST], [bMSKI])
        k.cp(MSKS[:, h, :], CST[:, 3, :], [bCST], [bMSKS])
    pt, bpt = newps()
    k.tr(pt[:, 0:128], CST[:, 3, :], ident, [bCST], [bpt])
    for h in range(NH):
        k.cp(MSKL[:, h, :], pt[:, 0:128], [bpt], [bMSKL])
    k.op(k.dve, lambda: nc.vector.memset(ONES[:, :], 1.0), [], [bONES])
    k.op(k.dve, lambda: nc.vector.memset(CAR[:, :], 0.0), [], [bCAR])
    k.op(k.dve, lambda: nc.vector.memset(H[:, :, :], 0.0), [], bHh)

    def z(n):
        return Z[n][0]

    def zb(n):
        return Z[n][1]

    def flat(t):
        return t[:, :, :].rearrange("p a b -> p (a b)")


    def inproj(s):
        P, bP = PP[s % 2]
        t0 = s * ST
        k.dma(k.sp, XT[:, :, :], xT[:, :, t0:t0 + ST], W=[bXT], key=bXT)
        k.tt(flat(SQ), flat(XT), flat(XT), ALU.mult, [bXT], [bSQ])
        ps, bps = newps()
        for kc in range(8):
            k.mm(ps[:, 0:ST], onesm, SQ[:, kc, :], kc == 0, kc == 7, [bSQ, bCST], [bps])
        k.actf(RS[:, :], ps[:, 0:ST], AF.Sqrt, [bps], [bRS], bias=RMS_EPS, scale=1.0 / D)
        k.recip(RS[:, :], RS[:, :], [bRS], [bRS])
        for kc in range(8):
            k.stt(HT[:, kc, :], XT[:, kc, :], GP[:, kc:kc + 1], RS[:, :], ALU.mult, ALU.mult, [bXT, bGP, bRS], [bHT])
        for pb in range(NPB):
            ps, bps = newps()
            for kc in range(8):
                k.mm(ps[:, 0:ST], W[:, kc, pb * 128:(pb + 1) * 128], HT[:, kc, :], kc == 0, kc == 7, [bW, bHT], [bps])
            k.actf(TMP[:, 1:ST + 1], ps[:, 0:ST], AF.Copy, [bps, bVECP], [bTMP], scale=VECP[:, pb:pb + 1])
            k.cp(TMP[:, 0:1], CAR[:, pb:pb + 1], [bCAR], [bTMP], eng=k.act)
            k.stt(PPR[:, pb, :], ps[:, 0:ST], OMMP[:, pb:pb + 1], TMP[:, 0:ST], ALU.mult, ALU.add, [bps, bOMMP, bTMP], [bPPR])
            k.cp(CAR[:, pb:pb + 1], TMP[:, ST:ST + 1], [bTMP], [bCAR], eng=k.act)
            yield
        Pv = P[:, 0:16, :].rearrange("p (a two) b -> p a two b", two=2)
        k.dma(k.pool, Pv[:, :, 0, :], PPR[0:64, 0:8, :], R=[bPPR], W=[bP], key=bP)
        k.dma(k.pool, Pv[:, :, 1, :], PPR[64:128, 0:8, :], R=[bPPR], W=[bP], key=bP)
        k.dma(k.pool, P[:, 16, :], PPR[0:64, 8, :], R=[bPPR], W=[bP], key=bP)
        k.dma(k.pool, P[:, 17, :], PPR[64:128, 8, :], R=[bPPR], W=[bP], key=bP)
        yield

    def prep(s):
        P, bP = PP[s % 2]
        k.actf(P[:, 16, :], P[:, 16, :], AF.Tanh, [bP], [bP])
        yield
        for (lw, vb, dst, col) in ((WDU, bWDU, "SG", 16), (WIU, bWIU, "A", 17)):
            for h2 in range(NH // 2):
                ps, bps = newps()
                for hh in range(2):
                    h = h2 * 2 + hh
                    k.mm(ps[0:64, hh * ST:(hh + 1) * ST], lw[:, h * 64:(h + 1) * 64], P[:, col, :], True, True, [vb, bP], [bps])
                    yield
                for hh in range(2):
                    h = h2 * 2 + hh
                    vcol = (V_W0 if dst == "SG" else V_A0) + h
                    k.actf(z(dst)[:, h, :], ps[0:64, hh * ST:(hh + 1) * ST], AF.Sigmoid, [bps, bVEC], [zb(dst)],
                           bias=VEC[:, vcol:vcol + 1])
                    yield
        for h in range(NH):
            for ck in range(nck):
                sl = slice(ck * C, (ck + 1) * C)
                k.op(k.dve, lambda h=h, sl=sl: nc.vector.tensor_tensor_scan(z("CS")[:, h, sl], ONES[:, :], z("SG")[:, h, sl], 0.0, ALU.mult, ALU.add),
                     [bONES, zb("SG")], [zb("CS")])
                yield
        k.tt(flat(z("SG")), flat(z("CS")), flat(z("SG")), ALU.subtract, [zb("CS"), zb("SG")], [zb("SG")])
        yield
        k.actf(flat(z("RT")), flat(z("CS")), AF.Exp, [zb("CS")], [zb("RT")], scale=S_DEC)
        yield
        k.actf(flat(z("KPT")), flat(z("SG")), AF.Exp, [zb("SG")], [zb("KPT")], scale=S_DEC)
        yield
        k.actf(flat(z("E2")), flat(z("CS")), AF.Exp, [zb("CS")], [zb("E2")], scale=-S_DEC)
        yield
        for h in range(NH):
            for ck in range(nck):
                e = (ck + 1) * C - 1
                k.ts(SCE[:, h, ck:ck + 1], z("CS")[:, h, e:e + 1], S_DEC, None, ALU.mult, None, [zb("CS")], [bSCE])
                yield
        k.actf(SCE[:, :, :].rearrange("p a b -> p (a b)") if False else WC[:, :, :].rearrange("p a b -> p (a b)"),
               SCE[:, :, :].rearrange("p a b -> p (a b)"), AF.Exp, [bSCE], [bWC])
        yield
        for h in range(NH):
            for ck in range(nck):
                sl = slice(ck * C, (ck + 1) * C)
                k.actf(z("E3")[:, h, sl], z("CS")[:, h, sl], AF.Exp, [zb("CS"), bSCE], [zb("E3")],
                       bias=SCE[:, h, ck:ck + 1], scale=-S_DEC)
                yield
        for h in range(NH):
            k.ts(z("KAP")[:, h, :], P[:, 4 + h, :], VEC[:, V_KK + h:V_KK + h + 1], None, ALU.mult, None, [bP, bVEC], [zb("KAP")])
            yield
        k.tt(flat(z("T1")), flat(z("KAP")), flat(z("KAP")), ALU.mult, [zb("KAP")], [zb("T1")])
        yield
        for h2 in range(NH // 2):
            ps, bps = newps()
            for hh in range(2):
                h = h2 * 2 + hh
                k.mm(ps[0:64, hh * ST:(hh + 1) * ST], onesm[0:64, 0:64], z("T1")[:, h, :], True, True, [bCST, zb("T1")], [bps])
                yield
            k.actf(z("T2")[:, h2 * 2:h2 * 2 + 2, :].rearrange("p a b -> p (a b)"), ps[0:64, 0:2 * ST], AF.Sqrt, [bps], [zb("T2")])
            yield
        k.ts(flat(z("T2")), flat(z("T2")), 1e-12, None, ALU.max, None, [zb("T2")], [zb("T2")])
        yield
        k.recip(flat(z("T2")), flat(z("T2")), [zb("T2")], [zb("T2")])
        yield
        k.tt(flat(z("KAP")), flat(z("KAP")), flat(z("T2")), ALU.mult, [zb("KAP"), zb("T2")], [zb("KAP")])
        yield
        k.tt(flat(z("BM")), flat(z("A")), flat(z("KAP")), ALU.mult, [zb("A"), zb("KAP")], [zb("BM")])
        yield
        for h in range(NH):
            k.ts(z("T1")[:, h, :], z("A")[:, h, :], VEC[:, V_KA + h:V_KA + h + 1], OMKA[:, h:h + 1], ALU.mult, ALU.add,
                 [zb("A"), bVEC, bOMKA], [zb("T1")])
            yield
        k.tt(flat(z("K2")), P[:, 4:8, :].rearrange("p a b -> p (a b)"), flat(z("T1")), ALU.mult, [bP, zb("T1")], [zb("K2")])
        yield
        for h in range(NH):
            k.stt(z("T2")[:, h, :], P[:, h, :], VEC[:, V_RK + h:V_RK + h + 1], z("K2")[:, h, :], ALU.mult, ALU.mult,
                  [bP, bVEC, zb("K2")], [zb("T2")])
            yield
        for h2 in range(NH // 2):
            ps, bps = newps()
            for hh in range(2):
                h = h2 * 2 + hh
                k.mm(ps[0:64, hh * ST:(hh + 1) * ST], onesm[0:64, 0:64], z("T2")[:, h, :], True, True, [bCST, zb("T2")], [bps])
                yield
            k.tt(z("BV")[:, h2 * 2:h2 * 2 + 2, :].rearrange("p a b -> p (a b)"), ps[0:64, 0:2 * ST],
                 P[:, 8 + h2 * 2:10 + h2 * 2, :].rearrange("p a b -> p (a b)"), ALU.mult, [bps, bP], [zb("BV")])
            yield
        k.actf(P[:, 12:16, :].rearrange("p a b -> p (a b)"), P[:, 12:16, :].rearrange("p a b -> p (a b)"), AF.Silu, [bP], [bP])
        yield
        k.tt(flat(z("RT")), P[:, 0:4, :].rearrange("p a b -> p (a b)"), flat(z("RT")), ALU.mult, [bP, zb("RT")], [zb("RT")])
        yield
        k.tt(flat(z("KPT")), flat(z("KAP")), flat(z("KPT")), ALU.mult, [zb("KAP"), zb("KPT")], [zb("KPT")])
        yield
        k.tt(flat(z("KT")), flat(z("K2")), flat(z("E2")), ALU.mult, [zb("K2"), zb("E2")], [zb("KT")])
        yield
        k.tt(flat(z("BT")), flat(z("BM")), flat(z("E2")), ALU.mult, [zb("BM"), zb("E2")], [zb("BT")])
        yield
        k.tt(flat(z("KE")), flat(z("K2")), flat(z("E3")), ALU.mult, [zb("K2"), zb("E3")], [zb("KE")])
        yield
        k.tt(flat(z("BE")), flat(z("BM")), flat(z("E3")), ALU.mult, [zb("BM"), zb("E3")], [zb("BE")])
        yield


    def scan_post(s):
        P, bP = PP[s % 2]
        t0 = s * ST
        for ck in range(nck):
            sl = slice(ck * C, (ck + 1) * C)
            for (src, sb_, dst, db, neg) in ((P, bP, VT, bVT, False), (z("KE"), zb("KE"), KET, bKET, False),
                                             (z("BE"), zb("BE"), BET, bBET, True)):
                ps, bps = newps()
                for h in range(NH):
                    inp = src[:, 8 + h, sl] if src is P else src[:, h, sl]
                    k.tr(ps[:, h * HS:(h + 1) * HS], inp, ident[0:64, 0:64], [sb_, bCST], [bps])
                dflat = dst[:, :, :].rearrange("p a b -> p (a b)")
                if neg:
                    k.ts(dflat, ps[:, 0:NH * HS], -1.0, None, ALU.mult, None, [bps], [db])
                else:
                    k.cp(dflat, ps[:, 0:NH * HS], [bps], [db], eng=k.act)

            def amat(lh, lb, rh, rb, dst, mask, mb, neg):
                d, dbs = mats[dst]
                for g in range(2):
                    ps, bps = newps()
                    for hh in range(HG):
                        h = g * HG + hh
                        k.mm(ps[:, hh * 128:(hh + 1) * 128], lh[:, h, sl], rh[:, h, sl], True, True, [lb, rb], [bps])
                    dfl = d[:, g * HG:(g + 1) * HG, :].rearrange("p a b -> p (a b)")
                    mfl = mask[:, 0:HG, :].rearrange("p a b -> p (a b)")
                    if neg:
                        k.stt(dfl, ps[:, 0:HG * 128], -1.0, mfl, ALU.mult, ALU.mult, [bps, mb], [dbs[g]])
                    else:
                        k.tt(dfl, ps[:, 0:HG * 128], mfl, ALU.mult, [bps, mb], [dbs[g]])

            def hv(name, g):
                return mats[name][0][:, g * HG:(g + 1) * HG, :].rearrange("p a b -> p (a b)")

            amat(z("BT"), zb("BT"), z("KPT"), zb("KPT"), "QT0", MSKS, bMSKS, True)
            amat(z("KPT"), zb("KPT"), z("BT"), zb("BT"), "QN0", MSKL, bMSKL, True)
            amat(z("KT"), zb("KT"), z("KPT"), zb("KPT"), "AKK", MSKS, bMSKS, False)
            amat(z("KT"), zb("KT"), z("RT"), zb("RT"), "ARK", MSKI, bMSKI, False)
            amat(z("BT"), zb("BT"), z("RT"), zb("RT"), "ARB", MSKI, bMSKI, True)
            tt_cur = "TT0"
            for h in range(NH):
                k.tt(mats["TT0"][0][:, h, :], mats["QT0"][0][:, h, :], ident, ALU.add,
                     [mats["QT0"][1][h // HG], bCST], [mats["TT0"][1][h // HG]])
            qn, qt = "QN0", "QT0"
            nlev = 6
            for lv in range(nlev):
                qn2 = "QN1" if qn == "QN0" else "QN0"
                qt2 = "QT1" if qt == "QT0" else "QT0"
                tt2 = "TT1" if tt_cur == "TT0" else "TT0"
                last = lv == nlev - 1
                pq = [None, None]
                for g in range(2):
                    ps, bps = newps()
                    for hh in range(HG):
                        h = g * HG + hh
                        k.mm(ps[:, hh * 128:(hh + 1) * 128], mats[qt][0][:, h, :], mats[qn][0][:, h, :], True, True,
                             [mats[qt][1][g], mats[qn][1][g]], [bps])
                    ps2 = bps2 = None
                    if not last:
                        ps2, bps2 = newps()
                        for hh in range(HG):
                            h = g * HG + hh
                            k.mm(ps2[:, hh * 128:(hh + 1) * 128], mats[qn][0][:, h, :], mats[qt][0][:, h, :], True, True,
                                 [mats[qt][1][g], mats[qn][1][g]], [bps2])
                    pq[g] = (ps, bps, ps2, bps2)
                for g in range(2):
                    ps, bps, ps2, bps2 = pq[g]
                    k.cp(hv(qn2, g), ps[:, 0:HG * 128], [bps], [mats[qn2][1][g]], eng=k.act)
                    if not last:
                        k.cp(hv(qt2, g), ps2[:, 0:HG * 128], [bps2], [mats[qt2][1][g]])
                p3 = [None, None]
                for g in range(2):
                    ps3, bps3 = newps()
                    for hh in range(HG):
                        h = g * HG + hh
                        k.mm(ps3[:, hh * 128:(hh + 1) * 128], mats[qn2][0][:, h, :], mats[tt_cur][0][:, h, :], True, True,
                             [mats[qn2][1][g], mats[tt_cur][1][g]], [bps3])
                    p3[g] = (ps3, bps3)
                for g in range(2):
                    ps3, bps3 = p3[g]
                    k.tt(hv(tt2, g), ps3[:, 0:HG * 128], hv(tt_cur, g), ALU.add, [bps3, mats[tt_cur][1][g]], [mats[tt2][1][g]])
                qn, qt, tt_cur = qn2, qt2, tt2
            TT, bTT = mats[tt_cur]
            AKK, bAKK = mats["AKK"]
            ARK, bARK = mats["ARK"]
            ARB, bARB = mats["ARB"]
            GW_ = HG * HS
            pr = [None, None]
            for g in range(2):
                ps, bps = newps()
                for hh in range(HG):
                    h = g * HG + hh
                    k.mm(ps[:, hh * HS:(hh + 1) * HS], z("KPT")[:, h, sl], H[:, h, :], True, False, [zb("KPT"), bHh[g]], [bps])
                    k.mm(ps[:, hh * HS:(hh + 1) * HS], AKK[:, h, :], VT[:, h, :], False, True, [bAKK[g], bVT], [bps])
                pr[g] = (ps, bps)
            for g in range(2):
                ps, bps = pr[g]
                k.cp(RSB[:, g * HG:(g + 1) * HG, :].rearrange("p a b -> p (a b)"), ps[:, 0:GW_], [bps], [bRSBh[g]], eng=k.act)
            for g in range(2):
                ps, bps = newps()
                for hh in range(HG):
                    h = g * HG + hh
                    k.mm(ps[:, hh * HS:(hh + 1) * HS], TT[:, h, :], RSB[:, h, :], True, True, [bTT[g], bRSBh[g]], [bps])
                pr[g] = (ps, bps)
            for g in range(2):
                ps, bps = pr[g]
                k.cp(USB[:, g * HG:(g + 1) * HG, :].rearrange("p a b -> p (a b)"), ps[:, 0:GW_], [bps], [bUSBh[g]],
                     eng=(k.act if g == 0 else k.dve))
            for g in range(2):
                psy, bpsy = newps()
                for hh in range(HG):
                    h = g * HG + hh
                    o = psy[0:64, hh * 128:(hh + 1) * 128]
                    k.mm(o, H[:, h, :], z("RT")[:, h, sl], True, False, [bHh[g], zb("RT")], [bpsy])
                    k.mm(o, VT[:, h, :], ARK[:, h, :], False, False, [bVT, bARK[g]], [bpsy])
                    k.mm(o, USB[:, h, :], ARB[:, h, :], False, True, [bUSBh[g], bARB[g]], [bpsy])
                pr[g] = (psy, bpsy)
            for g in range(2):
                psy, bpsy = pr[g]
                for hh in range(HG):
                    h = g * HG + hh
                    k.cp(z("YT")[:, h, sl], psy[0:64, hh * 128:(hh + 1) * 128], [bpsy], [zb("YT")], eng=k.act)
            for g in range(2):
                pss, bpss = newps()
                for hh in range(HG):
                    h = g * HG + hh
                    k.mm(pss[0:64, hh * HS:(hh + 1) * HS], KET[:, h, :], VT[:, h, :], True, False, [bKET, bVT], [bpss])
                    k.mm(pss[0:64, hh * HS:(hh + 1) * HS], BET[:, h, :], USB[:, h, :], False, True, [bBET, bUSBh[g]], [bpss])
                pr[g] = (pss, bpss)
            for g in range(2):
                pss, bpss = pr[g]
                for hh in range(HG):
                    h = g * HG + hh
                    k.stt(H[:, h, :], H[:, h, :], WC[:, h, ck:ck + 1], pss[0:64, hh * HS:(hh + 1) * HS], ALU.mult, ALU.add,
                          [bHh[g], bWC, bpss], [bHh[g]])

        for h2 in range(NH // 2):
            ps, bps = newps()
            for hh in range(2):
                h = h2 * 2 + hh
                k.mm(ps[0:64, hh * ST:(hh + 1) * ST], onesm[0:64, 0:64], z("YT")[:, h, :], True, True, [bCST, zb("YT")], [bps])
            k.stt(z("YC")[:, h2 * 2:h2 * 2 + 2, :].rearrange("p a b -> p (a b)"), ps[0:64, 0:2 * ST], -1.0 / HS,
                  z("YT")[:, h2 * 2:h2 * 2 + 2, :].rearrange("p a b -> p (a b)"), ALU.mult, ALU.add, [bps, zb("YT")], [zb("YC")])
        k.tt(flat(z("T1")), flat(z("YC")), flat(z("YC")), ALU.mult, [zb("YC")], [zb("T1")])
        for h2 in range(NH // 2):
            ps, bps = newps()
            for hh in range(2):
                h = h2 * 2 + hh
                k.mm(ps[0:64, hh * ST:(hh + 1) * ST], onesm[0:64, 0:64], z("T1")[:, h, :], True, True, [bCST, zb("T1")], [bps])
            k.actf(z("T2")[:, h2 * 2:h2 * 2 + 2, :].rearrange("p a b -> p (a b)"), ps[0:64, 0:2 * ST], AF.Sqrt, [bps], [zb("T2")],
                   bias=GN_EPS, scale=1.0 / HS)
        k.recip(flat(z("T2")), flat(z("T2")), [zb("T2")], [zb("T2")])
        k.tt(flat(z("YC")), flat(z("YC")), flat(z("T2")), ALU.mult, [zb("YC"), zb("T2")], [zb("YC")])
        for h in range(NH):
            k.ts(z("YC")[:, h, :], z("YC")[:, h, :], VEC[:, V_LW + h:V_LW + h + 1], VEC[:, V_LB + h:V_LB + h + 1], ALU.mult, ALU.add,
                 [zb("YC"), bVEC], [zb("YC")])
        k.tt(flat(z("YC")), flat(z("YC")), flat(z("BV")), ALU.add, [zb("YC"), zb("BV")], [zb("YC")])
        k.tt(flat(z("YT")), flat(z("YC")), P[:, 12:16, :].rearrange("p a b -> p (a b)"), ALU.mult, [zb("YC"), bP], [zb("YT")])
        if ypair is None:
            k.dma(k.sp, yTd[:, :, t0:t0 + ST], z("YT")[:, :, :], R=[zb("YT")], key=zb("YT"))
        else:
            for h in range(NH):
                k.dma(k.sp, ypair[64 * (h % 2):64 * (h % 2) + 64, 2 * hg + h // 2, t0:t0 + ST], z("YT")[:, h, :],
                      R=[zb("YT")], key=zb("YT"))

    def run(gen, n):
        for _ in range(n):
            if next(gen, "done") == "done":
                return

    for _ in inproj(0):
        pass
    for s in range(nst):
        pg = prep(s)
        if s + 1 < nst:
            for _ in inproj(s + 1):
                run(pg, 8)
        for _ in pg:
            pass
        scan_post(s)
    k.end_stage()
def rwkv_main_inputs(x_b, w_in, mu, w0, wdu, a0, wiu, k_k, k_a, r_k, ln_w, ln_b, gpre, hg):
    xT = None if x_b is None else feat_major(x_b)
    ch = np.arange(hg * NH * HS, (hg + 1) * NH * HS)
    cols = np.concatenate([ch, D + ch, 2 * D + ch, 3 * D + ch, 4 * D + np.arange(128)])
    Wc = w_in[:, cols]
    W = np.ascontiguousarray(Wc.reshape(8, 128, NB * 64).transpose(1, 0, 2))
    vec = np.zeros((64, NVEC), np.float32)
    vec[:, V_MU:V_MU + NB] = mu[cols].reshape(NB, 64).T
    for nm, arr in ((V_W0, w0), (V_A0, a0), (V_KK, k_k), (V_KA, k_a), (V_RK, r_k.reshape(-1)), (V_LW, ln_w), (V_LB, ln_b)):
        vec[:, nm:nm + NH] = arr[ch].reshape(NH, 64).T
    gp = np.ascontiguousarray(gpre.reshape(8, 128).T)
    cst = np.zeros((128, 4, 128), np.float32)
    cst[:, 0, :] = np.eye(128)
    cst[:, 1, :] = 1.0
    cst[:, 2, :] = np.triu(np.ones((128, 128)))
    cst[:, 3, :] = np.triu(np.ones((128, 128)), 1)
    return {"xT": xT, "W": W, "vec": vec, "vecp": np.ascontiguousarray(mu[cols].reshape(NB // 2, 128).T),
            "gpre": gp, "wdu": np.ascontiguousarray(wdu[:, ch]),
            "wiu": np.ascontiguousarray(wiu[:, ch]), "cst": cst}


GDK, GDV = 128, 256
GW = 2 * GDK + 2 * GDV + 16


def build_gla_main(T):
    k = KB()
    nc = k.nc
    xT = nc.dram_tensor("xT", [128, 8, T], F32, kind="ExternalInput").ap()
    Wd = nc.dram_tensor("W", [128, 8, GW], F32, kind="ExternalInput").ap()
    vecd = nc.dram_tensor("vec", [128, 3], F32, kind="ExternalInput").ap()
    gpd = nc.dram_tensor("gpre", [128, 8], F32, kind="ExternalInput").ap()
    wupd = nc.dram_tensor("wup", [16, GDK], F32, kind="ExternalInput").ap()
    cstd = nc.dram_tensor("cst", [128, 4, 128], F32, kind="ExternalInput").ap()
    yTd = nc.dram_tensor("yT", [128, 2, T], F32, kind="ExternalOutput").ap()
    emit_gla_main(k, T, xT, Wd, vecd, gpd, wupd, cstd, yTd)
    k.finish()
    return k


def emit_gla_main(k, T, xT, Wd, vecd, gpd, wupd, cstd, yTd):
    nc = k.nc
    nst = T // ST
    nck = ST // C
    k.begin_stage()
    XT, bXT = k.sb("XT", [128, 8, ST])
    SQ, bSQ = k.sb("SQ", [128, 8, ST])
    HT, bHT = k.sb("HT", [128, 8, ST])
    RS, bRS = k.sb("RS", [128, ST])
    W, bW = k.sb("Wsb", [128, 8, GW])
    VEC, bVEC = k.sb("VEC", [128, 3])
    GP, bGP = k.sb("GP", [128, 8])
    WUP, bWUP = k.sb("WUP", [16, GDK])
    CST, bCST = k.sb("CST", [128, 4, 128])
    ONES, bONES = k.sb("ONESr", [128, C])
    P, bP = k.sb("P", [128, 6, ST])
    AD, bAD = k.sb("AD", [16, ST])
    Z = {}
    for n in ["LA", "CS", "E1", "E2", "E3", "RT", "KT", "KE", "RSTD"]:
        Z[n] = k.sb(n, [128, ST])
    YT, bYT = k.sb("YT", [128, 2, ST])
    Y2, bY2 = k.sb("Y2", [128, 2, ST])
    SCE, bSCE = k.sb("SCE", [128, nck])
    WC, bWC = k.sb("WC", [128, nck])
    H, bH = k.sb("H", [128, GDV])
    VT, bVT = k.sb("VT", [128, GDV])
    KET, bKET = k.sb("KET", [128, GDK])
    ARK, bARK = k.sb("ARK", [128, 128])
    newps = k.newps

    ident = CST[:, 0, :]
    onesm = CST[:, 1, :]
    for kc in range(8):
        k.dma(k.sp, W[:, kc, :], Wd[:, kc, :], W=[bW], key=bW)
    k.dma(k.sp, VEC[:, :], vecd[:, :], W=[bVEC], key=bVEC)
    k.dma(k.sp, GP[:, :], gpd[:, :], W=[bGP], key=bGP)
    k.dma(k.sp, WUP[:, :], wupd[:, :], W=[bWUP], key=bWUP)
    k.dma(k.sp, CST[:, :, :], cstd[:, :, :], W=[bCST], key=bCST)
    k.op(k.dve, lambda: nc.vector.memset(ONES[:, :], 1.0), [], [bONES])
    k.op(k.dve, lambda: nc.vector.memset(H[:, :], 0.0), [], [bH])

    def z(n):
        return Z[n][0]

    def zb(n):
        return Z[n][1]

    def flat(t):
        return t[:, :, :].rearrange("p a b -> p (a b)")

    for s in range(nst):
        t0 = s * ST
        k.dma(k.sp, XT[:, :, :], xT[:, :, t0:t0 + ST], W=[bXT], key=bXT)
        k.tt(flat(SQ), flat(XT), flat(XT), ALU.mult, [bXT], [bSQ])
        ps, bps = newps()
        for kc in range(8):
            k.mm(ps[:, 0:ST], onesm, SQ[:, kc, :], kc == 0, kc == 7, [bSQ, bCST], [bps])
        k.actf(RS[:, :], ps[:, 0:ST], AF.Sqrt, [bps], [bRS], bias=RMS_EPS, scale=1.0 / D)
        k.recip(RS[:, :], RS[:, :], [bRS], [bRS])
        for kc in range(8):
            k.stt(HT[:, kc, :], XT[:, kc, :], GP[:, kc:kc + 1], RS[:, :], ALU.mult, ALU.mult, [bXT, bGP, bRS], [bHT])
        for j in range(6):
            ps, bps = newps()
            for kc in range(8):
                k.mm(ps[:, 0:ST], W[:, kc, j * 128:(j + 1) * 128], HT[:, kc, :], kc == 0, kc == 7, [bW, bHT], [bps])
            k.cp(P[:, j, :], ps[:, 0:ST], [bps], [bP], eng=(k.act if j % 2 else k.dve))
        ps, bps = newps()
        for kc in range(8):
            k.mm(ps[0:16, 0:ST], W[:, kc, 768:784], HT[:, kc, :], kc == 0, kc == 7, [bW, bHT], [bps])
        k.cp(AD[:, :], ps[0:16, 0:ST], [bps], [bAD])
        ps, bps = newps()
        k.mm(ps[:, 0:ST], WUP[:, :], AD[:, :], True, True, [bWUP, bAD], [bps])
        k.actf(z("LA")[:, :], ps[:, 0:ST], AF.Sigmoid, [bps, bVEC], [zb("LA")], bias=VEC[:, 0:1])
        k.actf(z("LA")[:, :], z("LA")[:, :], AF.Ln, [zb("LA")], [zb("LA")])
        for ck in range(nck):
            sl = slice(ck * C, (ck + 1) * C)
            k.op(k.dve, lambda sl=sl: nc.vector.tensor_tensor_scan(z("CS")[:, sl], ONES[:, :], z("LA")[:, sl], 0.0, ALU.mult, ALU.add),
                 [bONES, zb("LA")], [zb("CS")])
        k.actf(z("E1")[:, :], z("CS")[:, :], AF.Exp, [zb("CS")], [zb("E1")], scale=1.0 / 16)
        k.actf(z("E2")[:, :], z("CS")[:, :], AF.Exp, [zb("CS")], [zb("E2")], scale=-1.0 / 16)
        for ck in range(nck):
            e = (ck + 1) * C - 1
            k.ts(SCE[:, ck:ck + 1], z("CS")[:, e:e + 1], 1.0 / 16, None, ALU.mult, None, [zb("CS")], [bSCE])
        k.actf(WC[:, :], SCE[:, :], AF.Exp, [bSCE], [bWC])
        for ck in range(nck):
            sl = slice(ck * C, (ck + 1) * C)
            k.actf(z("E3")[:, sl], z("CS")[:, sl], AF.Exp, [zb("CS"), bSCE], [zb("E3")], bias=SCE[:, ck:ck + 1], scale=-1.0 / 16)
        k.stt(z("RT")[:, :], P[:, 0, :], float(GDK ** -0.5), z("E1")[:, :], ALU.mult, ALU.mult, [bP, zb("E1")], [zb("RT")])
        k.tt(z("KT")[:, :], P[:, 1, :], z("E2")[:, :], ALU.mult, [bP, zb("E2")], [zb("KT")])
        k.tt(z("KE")[:, :], P[:, 1, :], z("E3")[:, :], ALU.mult, [bP, zb("E3")], [zb("KE")])
        k.actf(P[:, 4:6, :].rearrange("p a b -> p (a b)"), P[:, 4:6, :].rearrange("p a b -> p (a b)"), AF.Silu, [bP], [bP])
        for ck in range(nck):
            sl = slice(ck * C, (ck + 1) * C)
            ps, bps = newps()
            for vb in range(2):
                k.tr(ps[:, vb * 128:(vb + 1) * 128], P[:, 2 + vb, sl], ident, [bP, bCST], [bps])
            k.cp(VT[:, :], ps[:, 0:GDV], [bps], [bVT], eng=k.act)
            ps, bps = newps()
            k.tr(ps[:, 0:128], z("KE")[:, sl], ident, [zb("KE"), bCST], [bps])
            k.cp(KET[:, :], ps[:, 0:128], [bps], [bKET])
            ps, bps = newps()
            k.mm(ps[:, 0:128], z("KT")[:, sl], z("RT")[:, sl], True, True, [zb("KT"), zb("RT")], [bps])
            k.tt(ARK[:, :], ps[:, 0:128], CST[:, 2, :], ALU.mult, [bps, bCST], [bARK])
            psy, bpsy = newps()
            for vb in range(2):
                k.mm(psy[:, vb * 128:(vb + 1) * 128], H[:, vb * 128:(vb + 1) * 128], z("RT")[:, sl], True, False, [bH, zb("RT")], [bpsy])
                k.mm(psy[:, vb * 128:(vb + 1) * 128], VT[:, vb * 128:(vb + 1) * 128], ARK[:, :], False, True, [bVT, bARK], [bpsy])
            for vb in range(2):
                k.cp(YT[:, vb, sl], psy[:, vb * 128:(vb + 1) * 128], [bpsy], [bYT], eng=k.act)
            pss, bpss = newps()
            k.mm(pss[:, 0:GDV], KET[:, :], VT[:, :], True, True, [bKET, bVT], [bpss])
            k.stt(H[:, :], H[:, :], WC[:, ck:ck + 1], pss[:, 0:GDV], ALU.mult, ALU.add, [bH, bWC, bpss], [bH])
        k.tt(flat(Y2), flat(YT), flat(YT), ALU.mult, [bYT], [bY2])
        ps, bps = newps()
        for vb in range(2):
            k.mm(ps[:, 0:ST], onesm, Y2[:, vb, :], vb == 0, vb == 1, [bCST, bY2], [bps])
        k.actf(z("RSTD")[:, :], ps[:, 0:ST], AF.Sqrt, [bps], [zb("RSTD")], bias=RMS_EPS, scale=1.0 / GDV)
        k.recip(z("RSTD")[:, :], z("RSTD")[:, :], [zb("RSTD")], [zb("RSTD")])
        for vb in range(2):
            k.stt(Y2[:, vb, :], YT[:, vb, :], VEC[:, 1 + vb:2 + vb], z("RSTD")[:, :], ALU.mult, ALU.mult, [bYT, bVEC, zb("RSTD")], [bY2])
        k.tt(flat(YT), flat(Y2), P[:, 4:6, :].rearrange("p a b -> p (a b)"), ALU.mult, [bY2, bP], [bYT])
        k.dma(k.sp, yTd[:, :, t0:t0 + ST], YT[:, :, :], R=[bYT], key=bYT)
    k.end_stage()


def make_cst():
    cst = np.zeros((128, 4, 128), np.float32)
    cst[:, 0, :] = np.eye(128)
    cst[:, 1, :] = 1.0
    cst[:, 2, :] = np.triu(np.ones((128, 128)))
    cst[:, 3, :] = np.triu(np.ones((128, 128)), 1)
    return cst


def feat_major(x_b):
    T = x_b.shape[0]
    return np.ascontiguousarray(x_b.T.reshape(8, 128, T).transpose(1, 0, 2))


def gla_main_inputs(xT, w_in, wup, b_alpha, g_head, gpre, h):
    cols = np.concatenate([h * GDK + np.arange(GDK), 512 + h * GDK + np.arange(GDK), 1024 + h * GDV + np.arange(GDV),
                           2048 + h * GDV + np.arange(GDV), 3072 + np.arange(16)])
    W = np.ascontiguousarray(w_in[:, cols].reshape(8, 128, GW).transpose(1, 0, 2))
    vec = np.zeros((128, 3), np.float32)
    vec[:, 0] = b_alpha[h * GDK:(h + 1) * GDK]
    vec[:, 1] = g_head[0:128]
    vec[:, 2] = g_head[128:256]
    return {"xT": xT, "W": W, "vec": vec, "gpre": np.ascontiguousarray(gpre.reshape(8, 128).T),
            "wup": np.ascontiguousarray(wup[:, h * GDK:(h + 1) * GDK]), "cst": make_cst()}


def build_out(Tc, KP, NKC):
    k = KB()
    nc = k.nc
    yTd = nc.dram_tensor("yT", [KP, NKC, Tc], F32, kind="ExternalInput").ap()
    Wd = nc.dram_tensor("W", [KP, NKC, D], F32, kind="ExternalInput").ap()
    xd = nc.dram_tensor("x", [Tc, D], F32, kind="ExternalInput").ap()
    gd = nc.dram_tensor("gpost", [128, D], F32, kind="ExternalInput").ap()
    od = nc.dram_tensor("out", [Tc, D], F32, kind="ExternalOutput").ap()
    W, bW = k.sb("Wsb", [KP, NKC, D])
    G, bG = k.sb("G", [128, D])
    Y, bY = k.sb("Y", [KP, NKC, 128])
    X, bX = k.sb("X", [128, D])
    YO, bYO = k.sb("YO", [128, D])
    SQ, bSQ = k.sb("SQ", [128, D])
    SS, bSS = k.sb("SS", [128, 1])
    O, bO = k.sb("O", [128, D])
    newps = k.newps

    for kc in range(NKC):
        k.dma(k.sp, W[:, kc, :], Wd[:, kc, :], W=[bW], key=bW)
    k.dma(k.sp, G[:, :], gd[:, :], W=[bG], key=bG)
    for t in range(Tc // 128):
        sl = slice(t * 128, (t + 1) * 128)
        k.dma(k.sp, Y[:, :, :], yTd[:, :, sl], W=[bY], key=bY)
        k.dma(k.sp, X[:, :], xd[sl, :], W=[bX], key=bX)
        for hf in range(2):
            ps, bps = newps()
            for kc in range(NKC):
                k.mm(ps[:, :], Y[:, kc, :], W[:, kc, hf * 512:(hf + 1) * 512], kc == 0, kc == NKC - 1, [bY, bW], [bps])
            k.cp(YO[:, hf * 512:(hf + 1) * 512], ps[:, :], [bps], [bYO], eng=(k.act if hf else k.dve))
        k.tt(SQ[:, :], YO[:, :], YO[:, :], ALU.mult, [bYO], [bSQ])
        k.op(k.dve, lambda: nc.vector.reduce_sum(SS[:, :], SQ[:, :], axis=mybir.AxisListType.X), [bSQ], [bSS])
        k.actf(SS[:, :], SS[:, :], AF.Sqrt, [bSS], [bSS], bias=RMS_EPS, scale=1.0 / D)
        k.recip(SS[:, :], SS[:, :], [bSS], [bSS])
        k.stt(O[:, :], YO[:, :], SS[:, 0:1], G[:, :], ALU.mult, ALU.mult, [bYO, bSS, bG], [bO])
        k.tt(O[:, :], O[:, :], X[:, :], ALU.add, [bO, bX], [bO])
        k.dma(k.sp, od[sl, :], O[:, :], R=[bO], key=bO)
    k.finish([bO])
    return k


def run_out(yT_b, w_out, x, gpost, KP, NKC):
    B, T = x.shape[0], x.shape[1]
    Tc = T // 4
    kb = build_out(Tc, KP, NKC)
    W = np.ascontiguousarray(w_out.reshape(NKC, KP, D).transpose(1, 0, 2))
    g = np.ascontiguousarray(np.broadcast_to(gpost[None, :], (128, D)))
    maps = []
    for c in range(8):
        b, sg = c // 4, c % 4
        maps.append({"yT": np.ascontiguousarray(yT_b[b][:, :, sg * Tc:(sg + 1) * Tc]), "W": W,
                     "x": np.ascontiguousarray(x[b, sg * Tc:(sg + 1) * Tc]), "gpost": g})
    res = run_bass_kernel_spmd(kb.nc, maps, core_ids=list(range(8)))
    out = np.empty((B, T, D), np.float32)
    for c in range(8):
        b, sg = c // 4, c % 4
        out[b, sg * Tc:(sg + 1) * Tc] = res.results[c]["out"]
    return out


def gla_layer(x, p):
    B, T = x.shape[0], x.shape[1]
    kb = build_gla_main(T)
    xTs = [feat_major(x[b]) for b in range(B)]
    maps = []
    for c in range(8):
        b, h = c // 4, c % 4
        maps.append(gla_main_inputs(xTs[b], p["gla_w_in"][0], p["gla_w_alpha_up"][0], p["gla_b_alpha"][0],
                                    p["gla_head_norm"][0], p["gla_pre_norm"][0], h))
    res = run_bass_kernel_spmd(kb.nc, maps, core_ids=list(range(8)))
    yT = [np.empty((128, 8, T), np.float32) for _ in range(B)]
    for c in range(8):
        b, h = c // 4, c % 4
        yT[b][:, 2 * h:2 * h + 2, :] = res.results[c]["yT"]
    return run_out(yT, p["gla_w_out"][0], x, p["gla_post_norm"][0], 128, 8)


def rwkv_layer(x, p):
    B, T = x.shape[0], x.shape[1]
    kb = build_rwkv_main(T)
    maps = []
    for c in range(8):
        b, hg = c // 4, c % 4
        maps.append(rwkv_main_inputs(x[b], p["rwkv_w_in"][0], p["rwkv_mu"][0], p["rwkv_w0"][0], p["rwkv_w_decay_up"][0],
                                     p["rwkv_a0"][0], p["rwkv_w_iclr_up"][0], p["rwkv_k_k"][0], p["rwkv_k_a"][0],
                                     p["rwkv_r_k"][0], p["rwkv_ln_w"][0], p["rwkv_ln_b"][0], p["rwkv_pre_norm"][0], hg))
    res = run_bass_kernel_spmd(kb.nc, maps, core_ids=list(range(8)))
    yT = [np.empty((64, 16, T), np.float32) for _ in range(B)]
    for c in range(8):
        b, hg = c // 4, c % 4
        yT[b][:, 4 * hg:4 * hg + 4, :] = res.results[c]["yT"]
    return run_out(yT, p["rwkv_w_out"][0], x, p["rwkv_post_norm"][0], 64, 16)


def emit_outproj_fm(k, T, KP, NKC, Yd, Wd, gpd, Xd, Od, cstd):
    nc = k.nc
    k.begin_stage()
    W, bW = k.sb("Wo", [KP, NKC, D])
    Ys = [k.sb("Yo%d" % i, [KP, NKC, ST]) for i in range(2)]
    XTs = [k.sb("XTo%d" % i, [128, 8, ST]) for i in range(2)]
    YO, bYO = k.sb("YOo", [128, 8, ST])
    SQ, bSQ = k.sb("SQo", [128, 8, ST])
    RS, bRS = k.sb("RSo", [128, ST])
    Os = [k.sb("Oo%d" % i, [128, 8, ST]) for i in range(2)]
    GP, bGP = k.sb("GPo", [128, 8])
    CST, bCST = k.sb("CSTo", [128, 4, 128])
    onesm = CST[:, 1, :]
    for kc in range(NKC):
        k.dma(k.sp, W[:, kc, :], Wd[:, kc, :], W=[bW], key=bW)
    k.dma(k.sp, GP[:, :], gpd[:, :], W=[bGP], key=bGP)
    k.dma(k.sp, CST[:, :, :], cstd[:, :, :], W=[bCST], key=bCST)

    def flat(t):
        return t[:, :, :].rearrange("p a b -> p (a b)")

    for s in range(T // ST):
        t0 = s * ST
        Y, bY = Ys[s % 2]
        XT, bXT = XTs[s % 2]
        O, bO = Os[s % 2]
        k.dma(k.sp, Y[:, :, :], Yd[:, :, t0:t0 + ST], W=[bY], key=bY)
        k.dma(k.sp, XT[:, :, :], Xd[:, :, t0:t0 + ST], W=[bXT], key=bXT)
        for fb in range(8):
            ps, bps = k.newps()
            for kc in range(NKC):
                k.mm(ps[:, 0:ST], W[:, kc, fb * 128:(fb + 1) * 128], Y[:, kc, :], kc == 0, kc == NKC - 1, [bW, bY], [bps])
            k.cp(YO[:, fb, :], ps[:, 0:ST], [bps], [bYO], eng=(k.act if fb % 2 else k.dve))
        k.tt(flat(SQ), flat(YO), flat(YO), ALU.mult, [bYO], [bSQ])
        ps, bps = k.newps()
        for fb in range(8):
            k.mm(ps[:, 0:ST], onesm, SQ[:, fb, :], fb == 0, fb == 7, [bSQ, bCST], [bps])
        k.actf(RS[:, :], ps[:, 0:ST], AF.Sqrt, [bps], [bRS], bias=RMS_EPS, scale=1.0 / D)
        k.recip(RS[:, :], RS[:, :], [bRS], [bRS])
        for fb in range(8):
            k.stt(O[:, fb, :], YO[:, fb, :], GP[:, fb:fb + 1], RS[:, :], ALU.mult, ALU.mult, [bYO, bGP, bRS], [bO])
        k.tt(flat(O), flat(O), flat(XT), ALU.add, [bO, bXT], [bO])
        k.dma(k.sp, Od[:, :, t0:t0 + ST], O[:, :, :], R=[bO], key=bO)
    k.end_stage()


def build_fused(T):
    k = KB()
    nc = k.nc

    def din(name, shape):
        return nc.dram_tensor(name, list(shape), F32, kind="ExternalInput").ap()

    xT = din("xT", [128, 8, T])
    cst = din("cst", [128, 4, 128])
    gW = din("gW", [4, 128, 8, GW])
    gvec = din("gvec", [4, 128, 3])
    gwup = din("gwup", [4, 16, GDK])
    ggpre = din("ggpre", [128, 8])
    gWo = din("gWo", [128, 8, D])
    ggpost = din("ggpost", [128, 8])
    rW = din("rW", [4, 128, 8, NB * 64])
    rvec = din("rvec", [4, 64, NVEC])
    rvecp = din("rvecp", [4, 128, NB // 2])
    rwdu = din("rwdu", [4, 64, NH * 64])
    rwiu = din("rwiu", [4, 64, NH * 64])
    rgpre = din("rgpre", [128, 8])
    rWo = din("rWo", [128, 8, D])
    rgpost = din("rgpost", [128, 8])
    outT = nc.dram_tensor("outT", [128, 8, T], F32, kind="ExternalOutput").ap()
    Y1 = nc.dram_tensor("Y1s", [128, 8, T], F32).ap()
    X1 = nc.dram_tensor("X1s", [128, 8, T], F32).ap()
    Y2 = nc.dram_tensor("Y2s", [128, 8, T], F32).ap()
    for h in range(4):
        emit_gla_main(k, T, xT, gW[h], gvec[h], ggpre, gwup[h], cst, Y1[:, 2 * h:2 * h + 2, :])
    emit_outproj_fm(k, T, 128, 8, Y1, gWo, ggpost, xT, X1, cst)
    for hg in range(4):
        emit_rwkv_main(k, T, X1, rW[hg], rvec[hg], rgpre, rwdu[hg], rwiu[hg], cst, None, ypair=Y2, hg=hg, vecpd=rvecp[hg])
    emit_outproj_fm(k, T, 128, 8, Y2, rWo, rgpost, X1, outT, cst)
    k.finish()
    return k


def fused_inputs(x_b, p):
    g = [gla_main_inputs(None, p["gla_w_in"][0], p["gla_w_alpha_up"][0], p["gla_b_alpha"][0], p["gla_head_norm"][0],
                         p["gla_pre_norm"][0], h) for h in range(4)]
    r = [rwkv_main_inputs(None, p["rwkv_w_in"][0], p["rwkv_mu"][0], p["rwkv_w0"][0], p["rwkv_w_decay_up"][0],
                          p["rwkv_a0"][0], p["rwkv_w_iclr_up"][0], p["rwkv_k_k"][0], p["rwkv_k_a"][0],
                          p["rwkv_r_k"][0], p["rwkv_ln_w"][0], p["rwkv_ln_b"][0], p["rwkv_pre_norm"][0], hg) for hg in range(4)]
    fm = lambda v: np.ascontiguousarray(v.reshape(8, 128).T)
    return {
        "xT": feat_major(x_b), "cst": make_cst(),
        "gW": np.stack([a["W"] for a in g]), "gvec": np.stack([a["vec"] for a in g]),
        "gwup": np.stack([a["wup"] for a in g]), "ggpre": g[0]["gpre"],
        "gWo": np.ascontiguousarray(p["gla_w_out"][0].reshape(8, 128, D).transpose(1, 0, 2)), "ggpost": fm(p["gla_post_norm"][0]),
        "rW": np.stack([a["W"] for a in r]), "rvec": np.stack([a["vec"] for a in r]), "rvecp": np.stack([a["vecp"] for a in r]),
        "rwdu": np.stack([a["wdu"] for a in r]), "rwiu": np.stack([a["wiu"] for a in r]), "rgpre": r[0]["gpre"],
        "rWo": np.ascontiguousarray(p["rwkv_w_out"][0].reshape(8, 128, D).transpose(1, 0, 2)), "rgpost": fm(p["rwkv_post_norm"][0]),
    }


def kernel_unfused(**inputs):
    p = {k_: np.asarray(v, dtype=np.float32) for k_, v in inputs.items()}
    x = p["x"]
    x = gla_layer(x, p)
    x = rwkv_layer(x, p)
    return x


def kernel(**inputs):
    p = {k_: np.asarray(v, dtype=np.float32) for k_, v in inputs.items()}
    x = p["x"]
    B, T = x.shape[0], x.shape[1]
    kb = build_fused(T)
    per_b = [fused_inputs(x[b], p) for b in range(B)]
    maps = [per_b[c % B] for c in range(8)]
    res = run_bass_kernel_spmd(kb.nc, maps, core_ids=list(range(8)))
    out = np.empty((B, T, D), np.float32)
    for b in range(B):
        oT = res.results[b]["outT"]
        out[b] = oT.transpose(2, 1, 0).reshape(T, D)
    return out
```

```python
import math
import numpy as np
import concourse.bass as bass
import concourse.mybir as mybir
from concourse.bass_utils import run_bass_kernel_spmd

F32 = mybir.dt.float32
AF = mybir.ActivationFunctionType
ALU = mybir.AluOpType

D = 1024
C = 128
ST = 256
RMS_EPS = 1e-6
GN_EPS = 64e-5
S_DEC = -math.exp(-0.5)


class Buf:
    def __init__(self, name):
        self.name = name
        self.w = None
        self.r = []
        self.dsem = None
        self.dcnt = 0


class Eng:
    def __init__(self, k, e, name):
        self.k, self.e, self.name = k, e, name
        self.sem = k.new_sem(name)
        self.cnt = 0
        self.seen = {}

    def wait(self, ev):
        if ev is None:
            return
        sem, val = ev
        if self.seen.get(id(sem), 0) >= val:
            return
        if sem is self.sem and self.name == "pe":
            return
        self.e.wait_ge(sem, val)
        self.seen[id(sem)] = val


class KB:
    def __init__(self):
        self.nc = bass.Bass("TRN2", target_bir_lowering=False)
        self.stack = []
        nc = self.nc
        self.pe = Eng(self, nc.tensor, "pe")
        self.act = Eng(self, nc.scalar, "act")
        self.dve = Eng(self, nc.vector, "dve")
        self.pool = Eng(self, nc.gpsimd, "pool")
        self.sp = Eng(self, nc.sync, "sp")
        self.nbuf = 0
        self.ninst = 0
        self.guards = []
        self.marks = []
        self.dma_bufs = []
        self.psb = None
        self.psi = 0

    def new_sem(self, name):
        cm = self.nc.semaphore(name)
        s = cm.__enter__()
        self.stack.append(cm)
        return s

    def close(self):
        for cm in reversed(self.stack):
            cm.__exit__(None, None, None)
        self.stack = []

    def sb(self, name, shape, dtype=F32):
        self.nbuf += 1
        g = self.nc.sbuf_tensor("%s_%d" % (name, self.nbuf), list(shape), dtype)
        t = g.__enter__()
        self.guards.append(g)
        return t, Buf(name)

    def begin_stage(self):
        self.marks.append(len(self.guards))

    def end_stage(self):
        self.barrier()
        m = self.marks.pop()
        while len(self.guards) > m:
            self.guards.pop().__exit__(None, None, None)

    def barrier(self):
        engs = [self.pe, self.act, self.dve, self.sp, self.pool]
        for e in engs:
            for f in engs:
                if f is not e and f.cnt > 0:
                    e.wait((f.sem, f.cnt))
            for b in self.dma_bufs:
                e.wait((b.dsem, b.dcnt))

    def psum_banks(self):
        if self.psb is None:
            self.psb = [self.ps("ps%d" % i, [128, 512]) for i in range(8)]
        return self.psb

    def newps(self):
        r = self.psum_banks()[self.psi % 8]
        self.psi += 1
        return r

    def ps(self, name, shape):
        t = self.nc.alloc_psum_tensor(name, list(shape), F32)
        return t, Buf(name)

    def _deps(self, eng, R, W):
        for b in R:
            eng.wait(b.w)
        for b in W:
            eng.wait(b.w)
            for ev in b.r:
                eng.wait(ev)

    def op(self, eng, fn, R=(), W=()):
        self._deps(eng, R, W)
        ins = fn()
        eng.cnt += 1
        ins.then_inc(eng.sem, 1)
        ev = (eng.sem, eng.cnt)
        eng.seen[id(eng.sem)] = max(eng.seen.get(id(eng.sem), 0), 0)
        for b in R:
            b.r.append(ev)
        for b in W:
            b.w = ev
            b.r = []
        self.ninst += 1
        return ins

    def dma(self, eng, out, in_, R=(), W=(), key=None):
        self._deps(eng, R, W)
        if key.dsem is None:
            self.nbuf += 1
            key.dsem = self.new_sem("d_%s_%d" % (key.name, self.nbuf))
            self.dma_bufs.append(key)
        ins = eng.e.dma_start(out=out, in_=in_)
        key.dcnt += 16
        ins.then_inc(key.dsem, 16)
        ev = (key.dsem, key.dcnt)
        for b in R:
            b.r.append(ev)
        for b in W:
            b.w = ev
            b.r = []
        self.ninst += 1
        return ev

    def mm(self, out, lhsT, rhs, start, stop, R, W):
        return self.op(self.pe, lambda: self.nc.tensor.matmul(out, lhsT=lhsT, rhs=rhs, start=start, stop=stop), R, W)

    def tr(self, out, in_, ident, R, W):
        return self.op(self.pe, lambda: self.nc.tensor.transpose(out, in_, ident), R, W)

    def actf(self, out, in_, func, R, W, bias=0.0, scale=1.0):
        return self.op(self.act, lambda: self.nc.scalar.activation(out=out, in_=in_, func=func, bias=bias, scale=scale), R, W)

    def tt(self, out, a, b, op, R, W, eng=None):
        eng = eng or self.dve
        return self.op(eng, lambda: eng.e.tensor_tensor(out, a, b, op), R, W)

    def ts(self, out, a, s1, s2, op0, op1, R, W, eng=None):
        eng = eng or self.dve
        if s2 is None:
            return self.op(eng, lambda: eng.e.tensor_single_scalar(out, a, s1, op0), R, W)
        return self.op(eng, lambda: eng.e.tensor_scalar(out, a, s1, s2, op0, op1), R, W)

    def stt(self, out, a, s, b, op0, op1, R, W, eng=None):
        eng = eng or self.dve
        return self.op(eng, lambda: eng.e.scalar_tensor_tensor(out, a, s, b, op0, op1), R, W)

    def cp(self, out, in_, R, W, eng=None):
        eng = eng or self.dve
        if eng is self.act:
            return self.op(eng, lambda: self.nc.scalar.copy(out, in_), R, W)
        return self.op(eng, lambda: eng.e.tensor_copy(out, in_), R, W)

    def recip(self, out, in_, R, W):
        return self.op(self.dve, lambda: self.nc.vector.reciprocal(out, in_), R, W)

    def finish(self, out_bufs=()):
        self.barrier()
        for b in self.dma_bufs:
            self.sp.e.wait_ge(b.dsem, b.dcnt)
        while self.guards:
            self.guards.pop().__exit__(None, None, None)
        self.close()


NH = 4
HS = 64
NB = 18
V_MU, V_W0, V_A0, V_KK, V_KA, V_RK, V_LW, V_LB = 0, 18, 22, 26, 30, 34, 38, 42
NVEC = 46


def build_rwkv_main(T):
    k = KB()
    nc = k.nc
    xT = nc.dram_tensor("xT", [128, 8, T], F32, kind="ExternalInput").ap()
    Wd = nc.dram_tensor("W", [128, 8, NB * 64], F32, kind="ExternalInput").ap()
    vecd = nc.dram_tensor("vec", [64, NVEC], F32, kind="ExternalInput").ap()
    gpd = nc.dram_tensor("gpre", [128, 8], F32, kind="ExternalInput").ap()
    wdud = nc.dram_tensor("wdu", [64, NH * 64], F32, kind="ExternalInput").ap()
    wiud = nc.dram_tensor("wiu", [64, NH * 64], F32, kind="ExternalInput").ap()
    cstd = nc.dram_tensor("cst", [128, 4, 128], F32, kind="ExternalInput").ap()
    yTd = nc.dram_tensor("yT", [64, NH, T], F32, kind="ExternalOutput").ap()
    vecpd = nc.dram_tensor("vecp", [128, NB // 2], F32, kind="ExternalInput").ap()
    emit_rwkv_main(k, T, xT, Wd, vecd, gpd, wdud, wiud, cstd, yTd, vecpd=vecpd)
    k.finish()
    return k


def emit_rwkv_main(k, T, xT, Wd, vecd, gpd, wdud, wiud, cstd, yTd, ypair=None, hg=0, vecpd=None):
    nc = k.nc
    nst = T // ST
    nck = ST // C
    k.begin_stage()
    XT, bXT = k.sb("XT", [128, 8, ST])
    SQ, bSQ = k.sb("SQHT", [128, 8, ST])
    HT, bHT = SQ, bSQ
    RS, bRS = k.sb("RS", [128, ST])
    W, bW = k.sb("Wsb", [128, 8, NB * 64])
    VEC, bVEC = k.sb("VEC", [64, NVEC])
    OMKA, bOMKA = k.sb("OMKA", [64, NH])
    GP, bGP = k.sb("GP", [128, 8])
    WDU, bWDU = k.sb("WDU", [64, NH * 64])
    WIU, bWIU = k.sb("WIU", [64, NH * 64])
    CST, bCST = k.sb("CST", [128, 4, 128])
    MSKI, bMSKI = k.sb("MSKI", [128, NH, 128])
    MSKS, bMSKS = k.sb("MSKS", [128, NH, 128])
    MSKL, bMSKL = k.sb("MSKL", [128, NH, 128])
    ONES, bONES = k.sb("ONESr", [64, C])
    PP = [k.sb("P0", [64, NB, ST]), k.sb("P1", [64, NB, ST])]
    NPB = NB // 2
    TMP, bTMP = k.sb("TMP", [128, ST + 1])
    CAR, bCAR = k.sb("CAR", [128, NPB])
    PPR, bPPR = k.sb("PPR", [128, NPB, ST])
    VECP, bVECP = k.sb("VECP", [128, NPB])
    OMMP, bOMMP = k.sb("OMMP", [128, NPB])
    names = ["SG", "A", "CS", "E2", "E3", "KAP", "T1", "T2", "BM", "K2", "BV",
             "RT", "KPT", "KT", "BT", "KE", "BE", "YT"]
    Z = {}
    for n in names:
        Z[n] = k.sb(n, [64, NH, ST])
    Z["YC"] = Z["BM"]
    SCE, bSCE = k.sb("SCE", [64, NH, nck])
    WC, bWC = k.sb("WC", [64, NH, nck])
    H, bH = k.sb("H", [64, NH, HS])
    VT, bVT = k.sb("VT", [128, NH, HS])
    KET, bKET = k.sb("KET", [128, NH, HS])
    BET, bBET = k.sb("BET", [128, NH, HS])
    RSB, bRSB = k.sb("RSB", [128, NH, HS])
    USB, bUSB = k.sb("USB", [128, NH, HS])
    mats = {}
    HG = NH // 2
    for n in ["QN0", "QT0", "QN1", "QT1", "TT0", "TT1", "AKK", "ARK", "ARB"]:
        t_, _b = k.sb(n, [128, NH, 128])
        mats[n] = (t_, [Buf(n + "a"), Buf(n + "b")])
    bHh = [Buf("Ha"), Buf("Hb")]
    bRSBh = [Buf("RSBa"), Buf("RSBb")]
    bUSBh = [Buf("USBa"), Buf("USBb")]
    newps = k.newps

    ident = CST[:, 0, :]
    onesm = CST[:, 1, :]

    for kc in range(8):
        k.dma(k.sp, W[:, kc, :], Wd[:, kc, :], W=[bW], key=bW)
    k.dma(k.sp, VEC[:, :], vecd[:, :], W=[bVEC], key=bVEC)
    k.dma(k.sp, GP[:, :], gpd[:, :], W=[bGP], key=bGP)
    k.dma(k.sp, WDU[:, :], wdud[:, :], W=[bWDU], key=bWDU)
    k.dma(k.sp, WIU[:, :], wiud[:, :], W=[bWIU], key=bWIU)
    k.dma(k.sp, CST[:, :, :], cstd[:, :, :], W=[bCST], key=bCST)
    k.dma(k.sp, VECP[:, :], vecpd[:, :], W=[bVECP], key=bVECP)
    k.ts(OMMP[:, :], VECP[:, :], -1.0, 1.0, ALU.mult, ALU.add, [bVECP], [bOMMP])
    k.ts(OMKA[:, :], VEC[:, V_KA:V_KA + NH], -1.0, 1.0, ALU.mult, ALU.add, [bVEC], [bOMKA])
    for h in range(NH):
        k.cp(MSKI[:, h, :], CST[:, 2, :], [bCST], [bMSKI])
        k.cp(MSKS[:, h, :], CST[:, 3, :], [bCST], [bMSKS])
    pt, bpt = newps()
    k.tr(pt[:, 0:128], CST[:, 3, :], ident, [bCST], [bpt])
    for h in range(NH):
        k.cp(MSKL[:, h, :], pt[:, 0:128], [bpt], [bMSKL])
    k.op(k.dve, lambda: nc.vector.memset(ONES[:, :], 1.0), [], [bONES])
    k.op(k.dve, lambda: nc.vector.memset(CAR[:, :], 0.0), [], [bCAR])
    k.op(k.dve, lambda: nc.vector.memset(H[:, :, :], 0.0), [], bHh)

    def z(n):
        return Z[n][0]

    def zb(n):
        return Z[n][1]

    def flat(t):
        return t[:, :, :].rearrange("p a b -> p (a b)")


    def inproj(s):
        P, bP = PP[s % 2]
        t0 = s * ST
        k.dma(k.sp, XT[:, :, :], xT[:, :, t0:t0 + ST], W=[bXT], key=bXT)
        k.tt(flat(SQ), flat(XT), flat(XT), ALU.mult, [bXT], [bSQ])
        ps, bps = newps()
        for kc in range(8):
            k.mm(ps[:, 0:ST], onesm, SQ[:, kc, :], kc == 0, kc == 7, [bSQ, bCST], [bps])
        k.actf(RS[:, :], ps[:, 0:ST], AF.Sqrt, [bps], [bRS], bias=RMS_EPS, scale=1.0 / D)
        k.recip(RS[:, :], RS[:, :], [bRS], [bRS])
        for kc in range(8):
            k.stt(HT[:, kc, :], XT[:, kc, :], GP[:, kc:kc + 1], RS[:, :], ALU.mult, ALU.mult, [bXT, bGP, bRS], [bHT])
        for pb in range(NPB):
            ps, bps = newps()
            for kc in range(8):
                k.mm(ps[:, 0:ST], W[:, kc, pb * 128:(pb + 1) * 128], HT[:, kc, :], kc == 0, kc == 7, [bW, bHT], [bps])
            k.actf(TMP[:, 1:ST + 1], ps[:, 0:ST], AF.Copy, [bps, bVECP], [bTMP], scale=VECP[:, pb:pb + 1])
            k.cp(TMP[:, 0:1], CAR[:, pb:pb + 1], [bCAR], [bTMP], eng=k.act)
            k.stt(PPR[:, pb, :], ps[:, 0:ST], OMMP[:, pb:pb + 1], TMP[:, 0:ST], ALU.mult, ALU.add, [bps, bOMMP, bTMP], [bPPR])
            k.cp(CAR[:, pb:pb + 1], TMP[:, ST:ST + 1], [bTMP], [bCAR], eng=k.act)
            yield
        Pv = P[:, 0:16, :].rearrange("p (a two) b -> p a two b", two=2)
        k.dma(k.pool, Pv[:, :, 0, :], PPR[0:64, 0:8, :], R=[bPPR], W=[bP], key=bP)
        k.dma(k.pool, Pv[:, :, 1, :], PPR[64:128, 0:8, :], R=[bPPR], W=[bP], key=bP)
        k.dma(k.pool, P[:, 16, :], PPR[0:64, 8, :], R=[bPPR], W=[bP], key=bP)
        k.dma(k.pool, P[:, 17, :], PPR[64:128, 8, :], R=[bPPR], W=[bP], key=bP)
        yield

    def prep(s):
        P, bP = PP[s % 2]
        k.actf(P[:, 16, :], P[:, 16, :], AF.Tanh, [bP], [bP])
        yield
        for (lw, vb, dst, col) in ((WDU, bWDU, "SG", 16), (WIU, bWIU, "A", 17)):
            for h2 in range(NH // 2):
                ps, bps = newps()
                for hh in range(2):
                    h = h2 * 2 + hh
                    k.mm(ps[0:64, hh * ST:(hh + 1) * ST], lw[:, h * 64:(h + 1) * 64], P[:, col, :], True, True, [vb, bP], [bps])
                    yield
                for hh in range(2):
                    h = h2 * 2 + hh
                    vcol = (V_W0 if dst == "SG" else V_A0) + h
                    k.actf(z(dst)[:, h, :], ps[0:64, hh * ST:(hh + 1) * ST], AF.Sigmoid, [bps, bVEC], [zb(dst)],
                           bias=VEC[:, vcol:vcol + 1])
                    yield
        for h in range(NH):
            for ck in range(nck):
                sl = slice(ck * C, (ck + 1) * C)
                k.op(k.dve, lambda h=h, sl=sl: nc.vector.tensor_tensor_scan(z("CS")[:, h, sl], ONES[:, :], z("SG")[:, h, sl], 0.0, ALU.mult, ALU.add),
                     [bONES, zb("SG")], [zb("CS")])
                yield
        k.tt(flat(z("SG")), flat(z("CS")), flat(z("SG")), ALU.subtract, [zb("CS"), zb("SG")], [zb("SG")])
        yield
        k.actf(flat(z("RT")), flat(z("CS")), AF.Exp, [zb("CS")], [zb("RT")], scale=S_DEC)
        yield
        k.actf(flat(z("KPT")), flat(z("SG")), AF.Exp, [zb("SG")], [zb("KPT")], scale=S_DEC)
        yield
        k.actf(flat(z("E2")), flat(z("CS")), AF.Exp, [zb("CS")], [zb("E2")], scale=-S_DEC)
        yield
        for h in range(NH):
            for ck in range(nck):
                e = (ck + 1) * C - 1
                k.ts(SCE[:, h, ck:ck + 1], z("CS")[:, h, e:e + 1], S_DEC, None, ALU.mult, None, [zb("CS")], [bSCE])
                yield
        k.actf(SCE[:, :, :].rearrange("p a b -> p (a b)") if False else WC[:, :, :].rearrange("p a b -> p (a b)"),
               SCE[:, :, :].rearrange("p a b -> p (a b)"), AF.Exp, [bSCE], [bWC])
        yield
        for h in range(NH):
            for ck in range(nck):
                sl = slice(ck * C, (ck + 1) * C)
                k.actf(z("E3")[:, h, sl], z("CS")[:, h, sl], AF.Exp, [zb("CS"), bSCE], [zb("E3")],
                       bias=SCE[:, h, ck:ck + 1], scale=-S_DEC)
                yield
        for h in range(NH):
            k.ts(z("KAP")[:, h, :], P[:, 4 + h, :], VEC[:, V_KK + h:V_KK + h + 1], None, ALU.mult, None, [bP, bVEC], [zb("KAP")])
            yield
        k.tt(flat(z("T1")), flat(z("KAP")), flat(z("KAP")), ALU.mult, [zb("KAP")], [zb("T1")])
        yield
        for h2 in range(NH // 2):
            ps, bps = newps()
            for hh in range(2):
                h = h2 * 2 + hh
                k.mm(ps[0:64, hh * ST:(hh + 1) * ST], onesm[0:64, 0:64], z("T1")[:, h, :], True, True, [bCST, zb("T1")], [bps])
                yield
            k.actf(z("T2")[:, h2 * 2:h2 * 2 + 2, :].rearrange("p a b -> p (a b)"), ps[0:64, 0:2 * ST], AF.Sqrt, [bps], [zb("T2")])
            yield
        k.ts(flat(z("T2")), flat(z("T2")), 1e-12, None, ALU.max, None, [zb("T2")], [zb("T2")])
        yield
        k.recip(flat(z("T2")), flat(z("T2")), [zb("T2")], [zb("T2")])
        yield
        k.tt(flat(z("KAP")), flat(z("KAP")), flat(z("T2")), ALU.mult, [zb("KAP"), zb("T2")], [zb("KAP")])
        yield
        k.tt(flat(z("BM")), flat(z("A")), flat(z("KAP")), ALU.mult, [zb("A"), zb("KAP")], [zb("BM")])
        yield
        for h in range(NH):
            k.ts(z("T1")[:, h, :], z("A")[:, h, :], VEC[:, V_KA + h:V_KA + h + 1], OMKA[:, h:h + 1], ALU.mult, ALU.add,
                 [zb("A"), bVEC, bOMKA], [zb("T1")])
            yield
        k.tt(flat(z("K2")), P[:, 4:8, :].rearrange("p a b -> p (a b)"), flat(z("T1")), ALU.mult, [bP, zb("T1")], [zb("K2")])
        yield
        for h in range(NH):
            k.stt(z("T2")[:, h, :], P[:, h, :], VEC[:, V_RK + h:V_RK + h + 1], z("K2")[:, h, :], ALU.mult, ALU.mult,
                  [bP, bVEC, zb("K2")], [zb("T2")])
            yield
        for h2 in range(NH // 2):
            ps, bps = newps()
            for hh in range(2):
                h = h2 * 2 + hh
                k.mm(ps[0:64, hh * ST:(hh + 1) * ST], onesm[0:64, 0:64], z("T2")[:, h, :], True, True, [bCST, zb("T2")], [bps])
                yield
            k.tt(z("BV")[:, h2 * 2:h2 * 2 + 2, :].rearrange("p a b -> p (a b)"), ps[0:64, 0:2 * ST],
                 P[:, 8 + h2 * 2:10 + h2 * 2, :].rearrange("p a b -> p (a b)"), ALU.mult, [bps, bP], [zb("BV")])
            yield
        k.actf(P[:, 12:16, :].rearrange("p a b -> p (a b)"), P[:, 12:16, :].rearrange("p a b -> p (a b)"), AF.Silu, [bP], [bP])
        yield
        k.tt(flat(z("RT")), P[:, 0:4, :].rearrange("p a b -> p (a b)"), flat(z("RT")), ALU.mult, [bP, zb("RT")], [zb("RT")])
        yield
        k.tt(flat(z("KPT")), flat(z("KAP")), flat(z("KPT")), ALU.mult, [zb("KAP"), zb("KPT")], [zb("KPT")])
        yield
        k.tt(flat(z("KT")), flat(z("K2")), flat(z("E2")), ALU.mult, [zb("K2"), zb("E2")], [zb("KT")])
        yield
        k.tt(flat(z("BT")), flat(z("BM")), flat(z("E2")), ALU.mult, [zb("BM"), zb("E2")], [zb("BT")])
        yield
        k.tt(flat(z("KE")), flat(z("K2")), flat(z("E3")), ALU.mult, [zb("K2"), zb("E3")], [zb("KE")])
        yield
        k.tt(flat(z("BE")), flat(z("BM")), flat(z("E3")), ALU.mult, [zb("BM"), zb("E3")], [zb("BE")])
        yield


    def scan_post(s):
        P, bP = PP[s % 2]
        t0 = s * ST
        for ck in range(nck):
            sl = slice(ck * C, (ck + 1) * C)
            for (src, sb_, dst, db, neg) in ((P, bP, VT, bVT, False), (z("KE"), zb("KE"), KET, bKET, False),
                                             (z("BE"), zb("BE"), BET, bBET, True)):
                ps, bps = newps()
                for h in range(NH):
                    inp = src[:, 8 + h, sl] if src is P else src[:, h, sl]
                    k.tr(ps[:, h * HS:(h + 1) * HS], inp, ident[0:64, 0:64], [sb_, bCST], [bps])
                dflat = dst[:, :, :].rearrange("p a b -> p (a b)")
                if neg:
                    k.ts(dflat, ps[:, 0:NH * HS], -1.0, None, ALU.mult, None, [bps], [db])
                else:
                    k.cp(dflat, ps[:, 0:NH * HS], [bps], [db], eng=k.act)

            def amat(lh, lb, rh, rb, dst, mask, mb, neg):
                d, dbs = mats[dst]
                for g in range(2):
                    ps, bps = newps()
                    for hh in range(HG):
                        h = g * HG + hh
                        k.mm(ps[:, hh * 128:(hh + 1) * 128], lh[:, h, sl], rh[:, h, sl], True, True, [lb, rb], [bps])
                    dfl = d[:, g * HG:(g + 1) * HG, :].rearrange("p a b -> p (a b)")
                    mfl = mask[:, 0:HG, :].rearrange("p a b -> p (a b)")
                    if neg:
                        k.stt(dfl, ps[:, 0:HG * 128], -1.0, mfl, ALU.mult, ALU.mult, [bps, mb], [dbs[g]])
                    else:
                        k.tt(dfl, ps[:, 0:HG * 128], mfl, ALU.mult, [bps, mb], [dbs[g]])

            def hv(name, g):
                return mats[name][0][:, g * HG:(g + 1) * HG, :].rearrange("p a b -> p (a b)")

            amat(z("BT"), zb("BT"), z("KPT"), zb("KPT"), "QT0", MSKS, bMSKS, True)
            amat(z("KPT"), zb("KPT"), z("BT"), zb("BT"), "QN0", MSKL, bMSKL, True)
            amat(z("KT"), zb("KT"), z("KPT"), zb("KPT"), "AKK", MSKS, bMSKS, False)
            amat(z("KT"), zb("KT"), z("RT"), zb("RT"), "ARK", MSKI, bMSKI, False)
            amat(z("BT"), zb("BT"), z("RT"), zb("RT"), "ARB", MSKI, bMSKI, True)
            tt_cur = "TT0"
            for h in range(NH):
                k.tt(mats["TT0"][0][:, h, :], mats["QT0"][0][:, h, :], ident, ALU.add,
                     [mats["QT0"][1][h // HG], bCST], [mats["TT0"][1][h // HG]])
            qn, qt = "QN0", "QT0"
            nlev = 6
            for lv in range(nlev):
                qn2 = "QN1" if qn == "QN0" else "QN0"
                qt2 = "QT1" if qt == "QT0" else "QT0"
                tt2 = "TT1" if tt_cur == "TT0" else "TT0"
                last = lv == nlev - 1
                pq = [None, None]
                for g in range(2):
                    ps, bps = newps()
                    for hh in range(HG):
                        h = g * HG + hh
                        k.mm(ps[:, hh * 128:(hh + 1) * 128], mats[qt][0][:, h, :], mats[qn][0][:, h, :], True, True,
                             [mats[qt][1][g], mats[qn][1][g]], [bps])
                    ps2 = bps2 = None
                    if not last:
                        ps2, bps2 = newps()
                        for hh in range(HG):
                            h = g * HG + hh
                            k.mm(ps2[:, hh * 128:(hh + 1) * 128], mats[qn][0][:, h, :], mats[qt][0][:, h, :], True, True,
                                 [mats[qt][1][g], mats[qn][1][g]], [bps2])
                    pq[g] = (ps, bps, ps2, bps2)
                for g in range(2):
                    ps, bps, ps2, bps2 = pq[g]
                    k.cp(hv(qn2, g), ps[:, 0:HG * 128], [bps], [mats[qn2][1][g]], eng=k.act)
                    if not last:
                        k.cp(hv(qt2, g), ps2[:, 0:HG * 128], [bps2], [mats[qt2][1][g]])
                p3 = [None, None]
                for g in range(2):
                    ps3, bps3 = newps()
                    for hh in range(HG):
                        h = g * HG + hh
                        k.mm(ps3[:, hh * 128:(hh + 1) * 128], mats[qn2][0][:, h, :], mats[tt_cur][0][:, h, :], True, True,
                             [mats[qn2][1][g], mats[tt_cur][1][g]], [bps3])
                    p3[g] = (ps3, bps3)
                for g in range(2):
                    ps3, bps3 = p3[g]
                    k.tt(hv(tt2, g), ps3[:, 0:HG * 128], hv(tt_cur, g), ALU.add, [bps3, mats[tt_cur][1][g]], [mats[tt2][1][g]])
                qn, qt, tt_cur = qn2, qt2, tt2
            TT, bTT = mats[tt_cur]
            AKK, bAKK = mats["AKK"]
            ARK, bARK = mats["ARK"]
            ARB, bARB = mats["ARB"]
            GW_ = HG * HS
            pr = [None, None]
            for g in range(2):
                ps, bps = newps()
                for hh in range(HG):
                    h = g * HG + hh
                    k.mm(ps[:, hh * HS:(hh + 1) * HS], z("KPT")[:, h, sl], H[:, h, :], True, False, [zb("KPT"), bHh[g]], [bps])
                    k.mm(ps[:, hh * HS:(hh + 1) * HS], AKK[:, h, :], VT[:, h, :], False, True, [bAKK[g], bVT], [bps])
                pr[g] = (ps, bps)
            for g in range(2):
                ps, bps = pr[g]
                k.cp(RSB[:, g * HG:(g + 1) * HG, :].rearrange("p a b -> p (a b)"), ps[:, 0:GW_], [bps], [bRSBh[g]], eng=k.act)
            for g in range(2):
                ps, bps = newps()
                for hh in range(HG):
                    h = g * HG + hh
                    k.mm(ps[:, hh * HS:(hh + 1) * HS], TT[:, h, :], RSB[:, h, :], True, True, [bTT[g], bRSBh[g]], [bps])
                pr[g] = (ps, bps)
            for g in range(2):
                ps, bps = pr[g]
                k.cp(USB[:, g * HG:(g + 1) * HG, :].rearrange("p a b -> p (a b)"), ps[:, 0:GW_], [bps], [bUSBh[g]],
                     eng=(k.act if g == 0 else k.dve))
            for g in range(2):
                psy, bpsy = newps()
                for hh in range(HG):
                    h = g * HG + hh
                    o = psy[0:64, hh * 128:(hh + 1) * 128]
                    k.mm(o, H[:, h, :], z("RT")[:, h, sl], True, False, [bHh[g], zb("RT")], [bpsy])
                    k.mm(o, VT[:, h, :], ARK[:, h, :], False, False, [bVT, bARK[g]], [bpsy])
                    k.mm(o, USB[:, h, :], ARB[:, h, :], False, True, [bUSBh[g], bARB[g]], [bpsy])
                pr[g] = (psy, bpsy)
            for g in range(2):
                psy, bpsy = pr[g]
                for hh in range(HG):
                    h = g * HG + hh
                    k.cp(z("YT")[:, h, sl], psy[0:64, hh * 128:(hh + 1) * 128], [bpsy], [zb("YT")], eng=k.act)
            for g in range(2):
                pss, bpss = newps()
                for hh in range(HG):
                    h = g * HG + hh
                    k.mm(pss[0:64, hh * HS:(hh + 1) * HS], KET[:, h, :], VT[:, h, :], True, False, [bKET, bVT], [bpss])
                    k.mm(pss[0:64, hh * HS:(hh + 1) * HS], BET[:, h, :], USB[:, h, :], False, True, [bBET, bUSBh[g]], [bpss])
                pr[g] = (pss, bpss)
            for g in range(2):
                pss, bpss = pr[g]
                for hh in range(HG):
                    h = g * HG + hh
                    k.stt(H[:, h, :], H[:, h, :], WC[:, h, ck:ck + 1], pss[0:64, hh * HS:(hh + 1) * HS], ALU.mult, ALU.add,
                          [bHh[g], bWC, bpss], [bHh[g]])

        for h2 in range(NH // 2):
            ps, bps = newps()
            for hh in range(2):
                h = h2 * 2 + hh
                k.mm(ps[0:64, hh * ST:(hh + 1) * ST], onesm[0:64, 0:64], z("YT")[:, h, :], True, True, [bCST, zb("YT")], [bps])
            k.stt(z("YC")[:, h2 * 2:h2 * 2 + 2, :].rearrange("p a b -> p (a b)"), ps[0:64, 0:2 * ST], -1.0 / HS,
                  z("YT")[:, h2 * 2:h2 * 2 + 2, :].rearrange("p a b -> p (a b)"), ALU.mult, ALU.add, [bps, zb("YT")], [zb("YC")])
        k.tt(flat(z("T1")), flat(z("YC")), flat(z("YC")), ALU.mult, [zb("YC")], [zb("T1")])
        for h2 in range(NH // 2):
            ps, bps = newps()
            for hh in range(2):
                h = h2 * 2 + hh
                k.mm(ps[0:64, hh * ST:(hh + 1) * ST], onesm[0:64, 0:64], z("T1")[:, h, :], True, True, [bCST, zb("T1")], [bps])
            k.actf(z("T2")[:, h2 * 2:h2 * 2 + 2, :].rearrange("p a b -> p (a b)"), ps[0:64, 0:2 * ST], AF.Sqrt, [bps], [zb("T2")],
                   bias=GN_EPS, scale=1.0 / HS)
        k.recip(flat(z("T2")), flat(z("T2")), [zb("T2")], [zb("T2")])
        k.tt(flat(z("YC")), flat(z("YC")), flat(z("T2")), ALU.mult, [zb("YC"), zb("T2")], [zb("YC")])
        for h in range(NH):
            k.ts(z("YC")[:, h, :], z("YC")[:, h, :], VEC[:, V_LW + h:V_LW + h + 1], VEC[:, V_LB + h:V_LB + h + 1], ALU.mult, ALU.add,
                 [zb("YC"), bVEC], [zb("YC")])
        k.tt(flat(z("YC")), flat(z("YC")), flat(z("BV")), ALU.add, [zb("YC"), zb("BV")], [zb("YC")])
        k.tt(flat(z("YT")), flat(z("YC")), P[:, 12:16, :].rearrange("p a b -> p (a b)"), ALU.mult, [zb("YC"), bP], [zb("YT")])
        if ypair is None:
            k.dma(k.sp, yTd[:, :, t0:t0 + ST], z("YT")[:, :, :], R=[zb("YT")], key=zb("YT"))
        else:
            for h in range(NH):
                k.dma(k.sp, ypair[64 * (h % 2):64 * (h % 2) + 64, 2 * hg + h // 2, t0:t0 + ST], z("YT")[:, h, :],
                      R=[zb("YT")], key=zb("YT"))

    def run(gen, n):
        for _ in range(n):
            if next(gen, "done") == "done":
                return

    for _ in inproj(0):
        pass
    for s in range(nst):
        pg = prep(s)
        if s + 1 < nst:
            for _ in inproj(s + 1):
                run(pg, 8)
        for _ in pg:
            pass
        scan_post(s)
    k.end_stage()
def rwkv_main_inputs(x_b, w_in, mu, w0, wdu, a0, wiu, k_k, k_a, r_k, ln_w, ln_b, gpre, hg):
    xT = None if x_b is None else feat_major(x_b)
    ch = np.arange(hg * NH * HS, (hg + 1) * NH * HS)
    cols = np.concatenate([ch, D + ch, 2 * D + ch, 3 * D + ch, 4 * D + np.arange(128)])
    Wc = w_in[:, cols]
    W = np.ascontiguousarray(Wc.reshape(8, 128, NB * 64).transpose(1, 0, 2))
    vec = np.zeros((64, NVEC), np.float32)
    vec[:, V_MU:V_MU + NB] = mu[cols].reshape(NB, 64).T
    for nm, arr in ((V_W0, w0), (V_A0, a0), (V_KK, k_k), (V_KA, k_a), (V_RK, r_k.reshape(-1)), (V_LW, ln_w), (V_LB, ln_b)):
        vec[:, nm:nm + NH] = arr[ch].reshape(NH, 64).T
    gp = np.ascontiguousarray(gpre.reshape(8, 128).T)
    cst = np.zeros((128, 4, 128), np.float32)
    cst[:, 0, :] = np.eye(128)
    cst[:, 1, :] = 1.0
    cst[:, 2, :] = np.triu(np.ones((128, 128)))
    cst[:, 3, :] = np.triu(np.ones((128, 128)), 1)
    return {"xT": xT, "W": W, "vec": vec, "vecp": np.ascontiguousarray(mu[cols].reshape(NB // 2, 128).T),
            "gpre": gp, "wdu": np.ascontiguousarray(wdu[:, ch]),
            "wiu": np.ascontiguousarray(wiu[:, ch]), "cst": cst}


GDK, GDV = 128, 256
GW = 2 * GDK + 2 * GDV + 16


def build_gla_main(T):
    k = KB()
    nc = k.nc
    xT = nc.dram_tensor("xT", [128, 8, T], F32, kind="ExternalInput").ap()
    Wd = nc.dram_tensor("W", [128, 8, GW], F32, kind="ExternalInput").ap()
    vecd = nc.dram_tensor("vec", [128, 3], F32, kind="ExternalInput").ap()
    gpd = nc.dram_tensor("gpre", [128, 8], F32, kind="ExternalInput").ap()
    wupd = nc.dram_tensor("wup", [16, GDK], F32, kind="ExternalInput").ap()
    cstd = nc.dram_tensor("cst", [128, 4, 128], F32, kind="ExternalInput").ap()
    yTd = nc.dram_tensor("yT", [128, 2, T], F32, kind="ExternalOutput").ap()
    emit_gla_main(k, T, xT, Wd, vecd, gpd, wupd, cstd, yTd)
    k.finish()
    return k


def emit_gla_main(k, T, xT, Wd, vecd, gpd, wupd, cstd, yTd):
    nc = k.nc
    nst = T // ST
    nck = ST // C
    k.begin_stage()
    XT, bXT = k.sb("XT", [128, 8, ST])
    SQ, bSQ = k.sb("SQ", [128, 8, ST])
    HT, bHT = k.sb("HT", [128, 8, ST])
    RS, bRS = k.sb("RS", [128, ST])
    W, bW = k.sb("Wsb", [128, 8, GW])
    VEC, bVEC = k.sb("VEC", [128, 3])
    GP, bGP = k.sb("GP", [128, 8])
    WUP, bWUP = k.sb("WUP", [16, GDK])
    CST, bCST = k.sb("CST", [128, 4, 128])
    ONES, bONES = k.sb("ONESr", [128, C])
    PP = [k.sb("P0", [128, 6, ST]), k.sb("P1", [128, 6, ST])]
    ADs = [k.sb("AD0", [16, ST]), k.sb("AD1", [16, ST])]
    Z = {}
    for n in ["LA", "CS", "E1", "E2", "E3", "RT", "KT", "KE", "RSTD"]:
        Z[n] = k.sb(n, [128, ST])
    YT, bYT = k.sb("YT", [128, 2, ST])
    Y2, bY2 = k.sb("Y2", [128, 2, ST])
    SCE, bSCE = k.sb("SCE", [128, nck])
    WC, bWC = k.sb("WC", [128, nck])
    H, bH = k.sb("H", [128, GDV])
    VT, bVT = k.sb("VT", [128, GDV])
    KET, bKET = k.sb("KET", [128, GDK])
    ARK, bARK = k.sb("ARK", [128, 128])
    newps = k.newps

    ident = CST[:, 0, :]
    onesm = CST[:, 1, :]
    for kc in range(8):
        k.dma(k.sp, W[:, kc, :], Wd[:, kc, :], W=[bW], key=bW)
    k.dma(k.sp, VEC[:, :], vecd[:, :], W=[bVEC], key=bVEC)
    k.dma(k.sp, GP[:, :], gpd[:, :], W=[bGP], key=bGP)
    k.dma(k.sp, WUP[:, :], wupd[:, :], W=[bWUP], key=bWUP)
    k.dma(k.sp, CST[:, :, :], cstd[:, :, :], W=[bCST], key=bCST)
    k.op(k.dve, lambda: nc.vector.memset(ONES[:, :], 1.0), [], [bONES])
    k.op(k.dve, lambda: nc.vector.memset(H[:, :], 0.0), [], [bH])

    def z(n):
        return Z[n][0]

    def zb(n):
        return Z[n][1]

    def flat(t):
        return t[:, :, :].rearrange("p a b -> p (a b)")


    def inproj(s):
        P, bP = PP[s % 2]
        AD, bAD = ADs[s % 2]
        t0 = s * ST
        k.dma(k.sp, XT[:, :, :], xT[:, :, t0:t0 + ST], W=[bXT], key=bXT)
        k.tt(flat(SQ), flat(XT), flat(XT), ALU.mult, [bXT], [bSQ])
        ps, bps = newps()
        for kc in range(8):
            k.mm(ps[:, 0:ST], onesm, SQ[:, kc, :], kc == 0, kc == 7, [bSQ, bCST], [bps])
        k.actf(RS[:, :], ps[:, 0:ST], AF.Sqrt, [bps], [bRS], bias=RMS_EPS, scale=1.0 / D)
        k.recip(RS[:, :], RS[:, :], [bRS], [bRS])
        for kc in range(8):
            k.stt(HT[:, kc, :], XT[:, kc, :], GP[:, kc:kc + 1], RS[:, :], ALU.mult, ALU.mult, [bXT, bGP, bRS], [bHT])
        for j in range(6):
            ps, bps = newps()
            for kc in range(8):
                k.mm(ps[:, 0:ST], W[:, kc, j * 128:(j + 1) * 128], HT[:, kc, :], kc == 0, kc == 7, [bW, bHT], [bps])
            k.cp(P[:, j, :], ps[:, 0:ST], [bps], [bP], eng=(k.act if j % 2 else k.dve))
            yield
        ps, bps = newps()
        for kc in range(8):
            k.mm(ps[0:16, 0:ST], W[:, kc, 768:784], HT[:, kc, :], kc == 0, kc == 7, [bW, bHT], [bps])
        k.cp(AD[:, :], ps[0:16, 0:ST], [bps], [bAD])
        yield

    def rest(s):
        P, bP = PP[s % 2]
        AD, bAD = ADs[s % 2]
        t0 = s * ST
        ps, bps = newps()
        k.mm(ps[:, 0:ST], WUP[:, :], AD[:, :], True, True, [bWUP, bAD], [bps])
        yield
        k.actf(z("LA")[:, :], ps[:, 0:ST], AF.Sigmoid, [bps, bVEC], [zb("LA")], bias=VEC[:, 0:1])
        yield
        k.actf(z("LA")[:, :], z("LA")[:, :], AF.Ln, [zb("LA")], [zb("LA")])
        yield
        for ck in range(nck):
            sl = slice(ck * C, (ck + 1) * C)
            k.op(k.dve, lambda sl=sl: nc.vector.tensor_tensor_scan(z("CS")[:, sl], ONES[:, :], z("LA")[:, sl], 0.0, ALU.mult, ALU.add),
                 [bONES, zb("LA")], [zb("CS")])
            yield
        k.actf(z("E1")[:, :], z("CS")[:, :], AF.Exp, [zb("CS")], [zb("E1")], scale=1.0 / 16)
        yield
        k.actf(z("E2")[:, :], z("CS")[:, :], AF.Exp, [zb("CS")], [zb("E2")], scale=-1.0 / 16)
        yield
        for ck in range(nck):
            e = (ck + 1) * C - 1
            k.ts(SCE[:, ck:ck + 1], z("CS")[:, e:e + 1], 1.0 / 16, None, ALU.mult, None, [zb("CS")], [bSCE])
            yield
        k.actf(WC[:, :], SCE[:, :], AF.Exp, [bSCE], [bWC])
        yield
        for ck in range(nck):
            sl = slice(ck * C, (ck + 1) * C)
            k.actf(z("E3")[:, sl], z("CS")[:, sl], AF.Exp, [zb("CS"), bSCE], [zb("E3")], bias=SCE[:, ck:ck + 1], scale=-1.0 / 16)
            yield
        k.stt(z("RT")[:, :], P[:, 0, :], float(GDK ** -0.5), z("E1")[:, :], ALU.mult, ALU.mult, [bP, zb("E1")], [zb("RT")])
        yield
        k.tt(z("KT")[:, :], P[:, 1, :], z("E2")[:, :], ALU.mult, [bP, zb("E2")], [zb("KT")])
        yield
        k.tt(z("KE")[:, :], P[:, 1, :], z("E3")[:, :], ALU.mult, [bP, zb("E3")], [zb("KE")])
        yield
        k.actf(P[:, 4:6, :].rearrange("p a b -> p (a b)"), P[:, 4:6, :].rearrange("p a b -> p (a b)"), AF.Silu, [bP], [bP])
        yield
        for ck in range(nck):
            sl = slice(ck * C, (ck + 1) * C)
            ps, bps = newps()
            for vb in range(2):
                k.tr(ps[:, vb * 128:(vb + 1) * 128], P[:, 2 + vb, sl], ident, [bP, bCST], [bps])
                yield
            k.cp(VT[:, :], ps[:, 0:GDV], [bps], [bVT], eng=k.act)
            yield
            ps, bps = newps()
            k.tr(ps[:, 0:128], z("KE")[:, sl], ident, [zb("KE"), bCST], [bps])
            yield
            k.cp(KET[:, :], ps[:, 0:128], [bps], [bKET])
            yield
            ps, bps = newps()
            k.mm(ps[:, 0:128], z("KT")[:, sl], z("RT")[:, sl], True, True, [zb("KT"), zb("RT")], [bps])
            yield
            k.tt(ARK[:, :], ps[:, 0:128], CST[:, 2, :], ALU.mult, [bps, bCST], [bARK])
            yield
            psy, bpsy = newps()
            for vb in range(2):
                k.mm(psy[:, vb * 128:(vb + 1) * 128], H[:, vb * 128:(vb + 1) * 128], z("RT")[:, sl], True, False, [bH, zb("RT")], [bpsy])
                yield
                k.mm(psy[:, vb * 128:(vb + 1) * 128], VT[:, vb * 128:(vb + 1) * 128], ARK[:, :], False, True, [bVT, bARK], [bpsy])
                yield
            for vb in range(2):
                k.cp(YT[:, vb, sl], psy[:, vb * 128:(vb + 1) * 128], [bpsy], [bYT], eng=k.act)
                yield
            pss, bpss = newps()
            k.mm(pss[:, 0:GDV], KET[:, :], VT[:, :], True, True, [bKET, bVT], [bpss])
            yield
            k.stt(H[:, :], H[:, :], WC[:, ck:ck + 1], pss[:, 0:GDV], ALU.mult, ALU.add, [bH, bWC, bpss], [bH])
            yield
        k.tt(flat(Y2), flat(YT), flat(YT), ALU.mult, [bYT], [bY2])
        yield
        ps, bps = newps()
        for vb in range(2):
            k.mm(ps[:, 0:ST], onesm, Y2[:, vb, :], vb == 0, vb == 1, [bCST, bY2], [bps])
            yield
        k.actf(z("RSTD")[:, :], ps[:, 0:ST], AF.Sqrt, [bps], [zb("RSTD")], bias=RMS_EPS, scale=1.0 / GDV)
        yield
        k.recip(z("RSTD")[:, :], z("RSTD")[:, :], [zb("RSTD")], [zb("RSTD")])
        yield
        for vb in range(2):
            k.stt(Y2[:, vb, :], YT[:, vb, :], VEC[:, 1 + vb:2 + vb], z("RSTD")[:, :], ALU.mult, ALU.mult, [bYT, bVEC, zb("RSTD")], [bY2])
            yield
        k.tt(flat(YT), flat(Y2), P[:, 4:6, :].rearrange("p a b -> p (a b)"), ALU.mult, [bY2, bP], [bYT])
        yield
        k.dma(k.sp, yTd[:, :, t0:t0 + ST], YT[:, :, :], R=[bYT], key=bYT)
        yield

    def run(gen, n):
        for _ in range(n):
            if next(gen, "done") == "done":
                return

    for _ in inproj(0):
        pass
    for s in range(nst):
        rg = rest(s)
        if s + 1 < nst:
            for _ in inproj(s + 1):
                run(rg, 10)
        for _ in rg:
            pass
    k.end_stage()
def make_cst():
    cst = np.zeros((128, 4, 128), np.float32)
    cst[:, 0, :] = np.eye(128)
    cst[:, 1, :] = 1.0
    cst[:, 2, :] = np.triu(np.ones((128, 128)))
    cst[:, 3, :] = np.triu(np.ones((128, 128)), 1)
    return cst


def feat_major(x_b):
    T = x_b.shape[0]
    return np.ascontiguousarray(x_b.T.reshape(8, 128, T).transpose(1, 0, 2))


def gla_main_inputs(xT, w_in, wup, b_alpha, g_head, gpre, h):
    cols = np.concatenate([h * GDK + np.arange(GDK), 512 + h * GDK + np.arange(GDK), 1024 + h * GDV + np.arange(GDV),
                           2048 + h * GDV + np.arange(GDV), 3072 + np.arange(16)])
    W = np.ascontiguousarray(w_in[:, cols].reshape(8, 128, GW).transpose(1, 0, 2))
    vec = np.zeros((128, 3), np.float32)
    vec[:, 0] = b_alpha[h * GDK:(h + 1) * GDK]
    vec[:, 1] = g_head[0:128]
    vec[:, 2] = g_head[128:256]
    return {"xT": xT, "W": W, "vec": vec, "gpre": np.ascontiguousarray(gpre.reshape(8, 128).T),
            "wup": np.ascontiguousarray(wup[:, h * GDK:(h + 1) * GDK]), "cst": make_cst()}


def build_out(Tc, KP, NKC):
    k = KB()
    nc = k.nc
    yTd = nc.dram_tensor("yT", [KP, NKC, Tc], F32, kind="ExternalInput").ap()
    Wd = nc.dram_tensor("W", [KP, NKC, D], F32, kind="ExternalInput").ap()
    xd = nc.dram_tensor("x", [Tc, D], F32, kind="ExternalInput").ap()
    gd = nc.dram_tensor("gpost", [128, D], F32, kind="ExternalInput").ap()
    od = nc.dram_tensor("out", [Tc, D], F32, kind="ExternalOutput").ap()
    W, bW = k.sb("Wsb", [KP, NKC, D])
    G, bG = k.sb("G", [128, D])
    Y, bY = k.sb("Y", [KP, NKC, 128])
    X, bX = k.sb("X", [128, D])
    YO, bYO = k.sb("YO", [128, D])
    SQ, bSQ = k.sb("SQ", [128, D])
    SS, bSS = k.sb("SS", [128, 1])
    O, bO = k.sb("O", [128, D])
    newps = k.newps

    for kc in range(NKC):
        k.dma(k.sp, W[:, kc, :], Wd[:, kc, :], W=[bW], key=bW)
    k.dma(k.sp, G[:, :], gd[:, :], W=[bG], key=bG)
    for t in range(Tc // 128):
        sl = slice(t * 128, (t + 1) * 128)
        k.dma(k.sp, Y[:, :, :], yTd[:, :, sl], W=[bY], key=bY)
        k.dma(k.sp, X[:, :], xd[sl, :], W=[bX], key=bX)
        for hf in range(2):
            ps, bps = newps()
            for kc in range(NKC):
                k.mm(ps[:, :], Y[:, kc, :], W[:, kc, hf * 512:(hf + 1) * 512], kc == 0, kc == NKC - 1, [bY, bW], [bps])
            k.cp(YO[:, hf * 512:(hf + 1) * 512], ps[:, :], [bps], [bYO], eng=(k.act if hf else k.dve))
        k.tt(SQ[:, :], YO[:, :], YO[:, :], ALU.mult, [bYO], [bSQ])
        k.op(k.dve, lambda: nc.vector.reduce_sum(SS[:, :], SQ[:, :], axis=mybir.AxisListType.X), [bSQ], [bSS])
        k.actf(SS[:, :], SS[:, :], AF.Sqrt, [bSS], [bSS], bias=RMS_EPS, scale=1.0 / D)
        k.recip(SS[:, :], SS[:, :], [bSS], [bSS])
        k.stt(O[:, :], YO[:, :], SS[:, 0:1], G[:, :], ALU.mult, ALU.mult, [bYO, bSS, bG], [bO])
        k.tt(O[:, :], O[:, :], X[:, :], ALU.add, [bO, bX], [bO])
        k.dma(k.sp, od[sl, :], O[:, :], R=[bO], key=bO)
    k.finish([bO])
    return k


def run_out(yT_b, w_out, x, gpost, KP, NKC):
    B, T = x.shape[0], x.shape[1]
    Tc = T // 4
    kb = build_out(Tc, KP, NKC)
    W = np.ascontiguousarray(w_out.reshape(NKC, KP, D).transpose(1, 0, 2))
    g = np.ascontiguousarray(np.broadcast_to(gpost[None, :], (128, D)))
    maps = []
    for c in range(8):
        b, sg = c // 4, c % 4
        maps.append({"yT": np.ascontiguousarray(yT_b[b][:, :, sg * Tc:(sg + 1) * Tc]), "W": W,
                     "x": np.ascontiguousarray(x[b, sg * Tc:(sg + 1) * Tc]), "gpost": g})
    res = run_bass_kernel_spmd(kb.nc, maps, core_ids=list(range(8)))
    out = np.empty((B, T, D), np.float32)
    for c in range(8):
        b, sg = c // 4, c % 4
        out[b, sg * Tc:(sg + 1) * Tc] = res.results[c]["out"]
    return out


def gla_layer(x, p):
    B, T = x.shape[0], x.shape[1]
    kb = build_gla_main(T)
    xTs = [feat_major(x[b]) for b in range(B)]
    maps = []
    for c in range(8):
        b, h = c // 4, c % 4
        maps.append(gla_main_inputs(xTs[b], p["gla_w_in"][0], p["gla_w_alpha_up"][0], p["gla_b_alpha"][0],
                                    p["gla_head_norm"][0], p["gla_pre_norm"][0], h))
    res = run_bass_kernel_spmd(kb.nc, maps, core_ids=list(range(8)))
    yT = [np.empty((128, 8, T), np.float32) for _ in range(B)]
    for c in range(8):
        b, h = c // 4, c % 4
        yT[b][:, 2 * h:2 * h + 2, :] = res.results[c]["yT"]
    return run_out(yT, p["gla_w_out"][0], x, p["gla_post_norm"][0], 128, 8)


def rwkv_layer(x, p):
    B, T = x.shape[0], x.shape[1]
    kb = build_rwkv_main(T)
    maps = []
    for c in range(8):
        b, hg = c // 4, c % 4
        maps.append(rwkv_main_inputs(x[b], p["rwkv_w_in"][0], p["rwkv_mu"][0], p["rwkv_w0"][0], p["rwkv_w_decay_up"][0],
                                     p["rwkv_a0"][0], p["rwkv_w_iclr_up"][0], p["rwkv_k_k"][0], p["rwkv_k_a"][0],
                                     p["rwkv_r_k"][0], p["rwkv_ln_w"][0], p["rwkv_ln_b"][0], p["rwkv_pre_norm"][0], hg))
    res = run_bass_kernel_spmd(kb.nc, maps, core_ids=list(range(8)))
    yT = [np.empty((64, 16, T), np.float32) for _ in range(B)]
    for c in range(8):
        b, hg = c // 4, c % 4
        yT[b][:, 4 * hg:4 * hg + 4, :] = res.results[c]["yT"]
    return run_out(yT, p["rwkv_w_out"][0], x, p["rwkv_post_norm"][0], 64, 16)


def emit_outproj_fm(k, T, KP, NKC, Yd, Wd, gpd, Xd, Od, cstd):
    nc = k.nc
    k.begin_stage()
    W, bW = k.sb("Wo", [KP, NKC, D])
    Ys = [k.sb("Yo%d" % i, [KP, NKC, ST]) for i in range(2)]
    XTs = [k.sb("XTo%d" % i, [128, 8, ST]) for i in range(2)]
    YO, bYO = k.sb("YOo", [128, 8, ST])
    SQ, bSQ = k.sb("SQo", [128, 8, ST])
    RS, bRS = k.sb("RSo", [128, ST])
    Os = [k.sb("Oo%d" % i, [128, 8, ST]) for i in range(2)]
    GP, bGP = k.sb("GPo", [128, 8])
    CST, bCST = k.sb("CSTo", [128, 4, 128])
    onesm = CST[:, 1, :]
    for kc in range(NKC):
        k.dma(k.sp, W[:, kc, :], Wd[:, kc, :], W=[bW], key=bW)
    k.dma(k.sp, GP[:, :], gpd[:, :], W=[bGP], key=bGP)
    k.dma(k.sp, CST[:, :, :], cstd[:, :, :], W=[bCST], key=bCST)

    def flat(t):
        return t[:, :, :].rearrange("p a b -> p (a b)")

    for s in range(T // ST):
        t0 = s * ST
        Y, bY = Ys[s % 2]
        XT, bXT = XTs[s % 2]
        O, bO = Os[s % 2]
        k.dma(k.sp, Y[:, :, :], Yd[:, :, t0:t0 + ST], W=[bY], key=bY)
        k.dma(k.sp, XT[:, :, :], Xd[:, :, t0:t0 + ST], W=[bXT], key=bXT)
        for fb in range(8):
            ps, bps = k.newps()
            for kc in range(NKC):
                k.mm(ps[:, 0:ST], W[:, kc, fb * 128:(fb + 1) * 128], Y[:, kc, :], kc == 0, kc == NKC - 1, [bW, bY], [bps])
            k.cp(YO[:, fb, :], ps[:, 0:ST], [bps], [bYO], eng=(k.act if fb % 2 else k.dve))
        k.tt(flat(SQ), flat(YO), flat(YO), ALU.mult, [bYO], [bSQ])
        ps, bps = k.newps()
        for fb in range(8):
            k.mm(ps[:, 0:ST], onesm, SQ[:, fb, :], fb == 0, fb == 7, [bSQ, bCST], [bps])
        k.actf(RS[:, :], ps[:, 0:ST], AF.Sqrt, [bps], [bRS], bias=RMS_EPS, scale=1.0 / D)
        k.recip(RS[:, :], RS[:, :], [bRS], [bRS])
        for fb in range(8):
            k.stt(O[:, fb, :], YO[:, fb, :], GP[:, fb:fb + 1], RS[:, :], ALU.mult, ALU.mult, [bYO, bGP, bRS], [bO])
        k.tt(flat(O), flat(O), flat(XT), ALU.add, [bO, bXT], [bO])
        k.dma(k.sp, Od[:, :, t0:t0 + ST], O[:, :, :], R=[bO], key=bO)
    k.end_stage()


def build_fused(T):
    k = KB()
    nc = k.nc

    def din(name, shape):
        return nc.dram_tensor(name, list(shape), F32, kind="ExternalInput").ap()

    xT = din("xT", [128, 8, T])
    cst = din("cst", [128, 4, 128])
    gW = din("gW", [4, 128, 8, GW])
    gvec = din("gvec", [4, 128, 3])
    gwup = din("gwup", [4, 16, GDK])
    ggpre = din("ggpre", [128, 8])
    gWo = din("gWo", [128, 8, D])
    ggpost = din("ggpost", [128, 8])
    rW = din("rW", [4, 128, 8, NB * 64])
    rvec = din("rvec", [4, 64, NVEC])
    rvecp = din("rvecp", [4, 128, NB // 2])
    rwdu = din("rwdu", [4, 64, NH * 64])
    rwiu = din("rwiu", [4, 64, NH * 64])
    rgpre = din("rgpre", [128, 8])
    rWo = din("rWo", [128, 8, D])
    rgpost = din("rgpost", [128, 8])
    outT = nc.dram_tensor("outT", [128, 8, T], F32, kind="ExternalOutput").ap()
    Y1 = nc.dram_tensor("Y1s", [128, 8, T], F32).ap()
    X1 = nc.dram_tensor("X1s", [128, 8, T], F32).ap()
    Y2 = nc.dram_tensor("Y2s", [128, 8, T], F32).ap()
    for h in range(4):
        emit_gla_main(k, T, xT, gW[h], gvec[h], ggpre, gwup[h], cst, Y1[:, 2 * h:2 * h + 2, :])
    emit_outproj_fm(k, T, 128, 8, Y1, gWo, ggpost, xT, X1, cst)
    for hg in range(4):
        emit_rwkv_main(k, T, X1, rW[hg], rvec[hg], rgpre, rwdu[hg], rwiu[hg], cst, None, ypair=Y2, hg=hg, vecpd=rvecp[hg])
    emit_outproj_fm(k, T, 128, 8, Y2, rWo, rgpost, X1, outT, cst)
    k.finish()
    return k


def fused_inputs(x_b, p):
    g = [gla_main_inputs(None, p["gla_w_in"][0], p["gla_w_alpha_up"][0], p["gla_b_alpha"][0], p["gla_head_norm"][0],
                         p["gla_pre_norm"][0], h) for h in range(4)]
    r = [rwkv_main_inputs(None, p["rwkv_w_in"][0], p["rwkv_mu"][0], p["rwkv_w0"][0], p["rwkv_w_decay_up"][0],
                          p["rwkv_a0"][0], p["rwkv_w_iclr_up"][0], p["rwkv_k_k"][0], p["rwkv_k_a"][0],
                          p["rwkv_r_k"][0], p["rwkv_ln_w"][0], p["rwkv_ln_b"][0], p["rwkv_pre_norm"][0], hg) for hg in range(4)]
    fm = lambda v: np.ascontiguousarray(v.reshape(8, 128).T)
    return {
        "xT": feat_major(x_b), "cst": make_cst(),
        "gW": np.stack([a["W"] for a in g]), "gvec": np.stack([a["vec"] for a in g]),
        "gwup": np.stack([a["wup"] for a in g]), "ggpre": g[0]["gpre"],
        "gWo": np.ascontiguousarray(p["gla_w_out"][0].reshape(8, 128, D).transpose(1, 0, 2)), "ggpost": fm(p["gla_post_norm"][0]),
        "rW": np.stack([a["W"] for a in r]), "rvec": np.stack([a["vec"] for a in r]), "rvecp": np.stack([a["vecp"] for a in r]),
        "rwdu": np.stack([a["wdu"] for a in r]), "rwiu": np.stack([a["wiu"] for a in r]), "rgpre": r[0]["gpre"],
        "rWo": np.ascontiguousarray(p["rwkv_w_out"][0].reshape(8, 128, D).transpose(1, 0, 2)), "rgpost": fm(p["rwkv_post_norm"][0]),
    }


def kernel_unfused(**inputs):
    p = {k_: np.asarray(v, dtype=np.float32) for k_, v in inputs.items()}
    x = p["x"]
    x = gla_layer(x, p)
    x = rwkv_layer(x, p)
    return x


def kernel(**inputs):
    p = {k_: np.asarray(v, dtype=np.float32) for k_, v in inputs.items()}
    x = p["x"]
    B, T = x.shape[0], x.shape[1]
    kb = build_fused(T)
    per_b = [fused_inputs(x[b], p) for b in range(B)]
    maps = [per_b[c % B] for c in range(8)]
    res = run_bass_kernel_spmd(kb.nc, maps, core_ids=list(range(8)))
    out = np.empty((B, T, D), np.float32)
    for b in range(B):
        oT = res.results[b]["outT"]
        out[b] = oT.transpose(2, 1, 0).reshape(T, D)
    return out
```
